# Optimizing a Trainium2 kernel written in Bass

```python
import math
import jax
import jax.numpy as jnp
from jax import lax
import numpy as np

D_MODEL = 2048
BATCH = 1
SEQ = 8192
DEPTH = 2
DEC_BATCH = 4
DEC_SEQ = 2048
PAST_LEN = 128

D_MIX = D_MODEL
N_GROUPS = 4
GROUP_W = D_MIX // N_GROUPS
GDN_HEADS = 4
GDN_DK = 128
GDN_DV = GROUP_W // GDN_HEADS
GDN_KW = GDN_HEADS * GDN_DK
GDN_VW = GDN_HEADS * GDN_DV
LRU_W = GROUP_W
LRU_BLOCKS = 4
LRU_BW = LRU_W // LRU_BLOCKS
LRU_C = 8.0
SSD_DI = GROUP_W
SSD_HEADDIM = 64
SSD_HEADS = SSD_DI // SSD_HEADDIM
SSD_GROUPS = 2
SSD_STATE = 128
SSD_XBC = SSD_DI + 2 * SSD_GROUPS * SSD_STATE
HGRN_HEADS = 4
HGRN_DK = 128
HGRN_DV = GROUP_W // HGRN_HEADS
HGRN_KW = HGRN_HEADS * HGRN_DK
HGRN_VW = HGRN_HEADS * HGRN_DV
CONV_W = 4
FFN_CONV_W = 3
D_FF = 5632
CHUNK = 64
HGRN_CHUNK = 16
EPS = 1e-6
IN_SIZES = (2 * GDN_KW + GDN_VW, GDN_VW, 2 * GDN_HEADS, 2 * GDN_HEADS,
            LRU_W, LRU_W,
            SSD_DI, SSD_XBC, 2 * SSD_HEADS,
            HGRN_KW, 2 * HGRN_KW, HGRN_VW, HGRN_VW)
D_IN = sum(IN_SIZES)

kernel_name = 'hybrid_bidir_head_group_encoder'

F32 = jnp.float32


def _split(u, sizes):
    idx, acc = [], 0
    for s in sizes[:-1]:
        acc += s
        idx.append(acc)
    return jnp.split(u, idx, axis=-1)


def _rmsnorm(x, w):
    xf = x.astype(F32)
    y = xf * lax.rsqrt(jnp.mean(xf * xf, axis=-1, keepdims=True) + EPS)
    return (y * w.astype(F32)).astype(x.dtype)


def _l2norm(a):
    return a * lax.rsqrt(jnp.sum(a * a, axis=-1, keepdims=True) + EPS)


def _rev(a):
    return jnp.flip(a, axis=1)


def _dwconv(x, w, b=None):
    k = w.shape[0]
    y = lax.conv_general_dilated(
        x, w[:, None, :].astype(x.dtype), window_strides=(1,),
        padding=((k // 2, k - 1 - k // 2),),
        dimension_numbers=('NWC', 'WIO', 'NWC'),
        feature_group_count=x.shape[-1])
    if b is not None:
        y = y + b.astype(y.dtype)
    return y


def _gdn_chunked(q, k, v, g, beta):
    bsz, t, h, dk = q.shape
    dv = v.shape[-1]
    n = t // CHUNK

    def blocks(a):
        return jnp.moveaxis(a.reshape(bsz, n, CHUNK, h, -1), 3, 1)

    q, k, v = blocks(q), blocks(k), blocks(v)
    g = blocks(g[..., None])[..., 0]
    beta = blocks(beta[..., None])
    gc = jnp.cumsum(g, axis=-1)
    tri = jnp.tril(jnp.ones((CHUNK, CHUNK), dtype=bool))
    decay = jnp.exp(jnp.where(tri, gc[..., :, None] - gc[..., None, :], -jnp.inf))
    kb = k * beta
    a_low = jnp.tril(jnp.einsum('bhncd,bhnsd->bhncs', kb, k) * decay, -1)
    w = lax.linalg.triangular_solve(a_low, kb * jnp.exp(gc)[..., None],
                                    left_side=True, lower=True, unit_diagonal=True)
    u = lax.linalg.triangular_solve(a_low, v * beta,
                                    left_side=True, lower=True, unit_diagonal=True)
    qk = jnp.einsum('bhncd,bhnsd->bhncs', q, k) * decay
    q_dec = q * jnp.exp(gc)[..., None]
    k_dec = k * jnp.exp(gc[..., -1:] - gc)[..., None]
    g_last = jnp.exp(gc[..., -1])

    def step(state, inp):
        w_c, u_c, qk_c, qd_c, kd_c, gl_c = inp
        v_new = u_c - w_c @ state
        out = qd_c @ state + qk_c @ v_new
        state = state * gl_c[..., None, None] + jnp.swapaxes(kd_c, -1, -2) @ v_new
        return state, out

    xs = tuple(jnp.moveaxis(a, 2, 0) for a in (w, u, qk, q_dec, k_dec, g_last))
    s0 = jnp.zeros((bsz, h, dk, dv), F32)
    _, o = lax.scan(step, s0, xs)
    return jnp.transpose(o, (1, 0, 3, 2, 4)).reshape(bsz, t, h, dv)


def _gdn_mixer(qkv, z, beta_logit, alpha_logit, conv_w, a_log, dt_bias, norm_w):
    bsz, t, _ = qkv.shape
    qkv = jax.nn.silu(_dwconv(qkv, conv_w)).astype(F32)
    q, k, v = _split(qkv, (GDN_KW, GDN_KW, GDN_VW))
    q = _l2norm(q.reshape(bsz, t, GDN_HEADS, GDN_DK)) * (GDN_DK ** -0.5)
    k = _l2norm(k.reshape(bsz, t, GDN_HEADS, GDN_DK))
    v = v.reshape(bsz, t, GDN_HEADS, GDN_DV)
    beta = jax.nn.sigmoid(beta_logit.astype(F32)).reshape(bsz, t, 2, GDN_HEADS)
    g = -jnp.exp(a_log.astype(F32)) * jax.nn.softplus(
        alpha_logit.astype(F32).reshape(bsz, t, 2, GDN_HEADS) + dt_bias.astype(F32))
    o = (_gdn_chunked(q, k, v, g[:, :, 0], beta[:, :, 0])
         + _rev(_gdn_chunked(_rev(q), _rev(k), _rev(v), _rev(g[:, :, 1]), _rev(beta[:, :, 1]))))
    o = _rmsnorm(o, norm_w) * jax.nn.silu(z.astype(F32).reshape(bsz, t, GDN_HEADS, GDN_DV))
    return o.reshape(bsz, t, GDN_VW).astype(z.dtype)


def _lru_scan(log_a, gx):
    a = jnp.exp(log_a)
    mult = jnp.sqrt(-jnp.expm1(2.0 * log_a))
    mult = mult.at[:, 0].set(1.0)

    def combine(left, right):
        return left[0] * right[0], right[0] * left[1] + right[1]

    _, h = lax.associative_scan(combine, (a, mult * gx), axis=1)
    return h


def _rglru_mixer(xb, gate, conv_w, conv_b, wa, ba, wi, bi, lam):
    bsz, t, _ = xb.shape
    xc = _dwconv(xb, conv_w, conv_b).astype(F32)
    xblk = xc.reshape(bsz, t, LRU_BLOCKS, LRU_BW)
    r = jax.nn.sigmoid(jnp.einsum('btni,rnij->btrnj', xblk, wa.astype(F32)).reshape(bsz, t, 2, LRU_W)
                       + ba.astype(F32))
    ig = jax.nn.sigmoid(jnp.einsum('btni,rnij->btrnj', xblk, wi.astype(F32)).reshape(bsz, t, 2, LRU_W)
                        + bi.astype(F32))
    log_a = -LRU_C * r * jax.nn.softplus(-lam.astype(F32))
    gx = ig * xc[:, :, None]
    h = (_lru_scan(log_a[:, :, 0], gx[:, :, 0])
         + _rev(_lru_scan(_rev(log_a[:, :, 1]), _rev(gx[:, :, 1]))))
    return (h * jax.nn.gelu(gate.astype(F32))).astype(xb.dtype)


def _ssd_chunked(x, dt, a, bm, cm):
    bsz, t, h, p = x.shape
    n = t // CHUNK
    xdt = (x * dt[..., None]).reshape(bsz, n, CHUNK, h, p)
    bc = bm.reshape(bsz, n, CHUNK, h, -1)
    cc = cm.reshape(bsz, n, CHUNK, h, -1)
    acs = jnp.cumsum(jnp.moveaxis((dt * a).reshape(bsz, n, CHUNK, h), 3, 1), axis=-1)
    tri = jnp.tril(jnp.ones((CHUNK, CHUNK), dtype=bool))
    lmat = jnp.exp(jnp.where(tri, acs[..., :, None] - acs[..., None, :], -jnp.inf))
    scores = jnp.einsum('bnlhs,bnjhs->bhnlj', cc, bc) * lmat
    y_diag = jnp.einsum('bhnlj,bnjhp->bnlhp', scores, xdt)
    states = jnp.einsum('bnjhs,bhnj,bnjhp->bnhps', bc, jnp.exp(acs[..., -1:] - acs), xdt)

    def step(state, inp):
        st, dec = inp
        return state * dec[..., None, None] + st, state

    s0 = jnp.zeros((bsz, h, p, bc.shape[-1]), F32)
    _, s_in = lax.scan(step, s0, (jnp.moveaxis(states, 1, 0),
                                  jnp.moveaxis(jnp.exp(acs[..., -1]), 2, 0)))
    y_off = jnp.einsum('bnlhs,nbhps,bhnl->bnlhp', cc, s_in, jnp.exp(acs))
    return (y_diag + y_off).reshape(bsz, t, h, p)


def _ssd_mixer(z, xbc, dt_raw, conv_w, conv_b, a_log, dt_bias, d_skip, norm_w):
    bsz, t, _ = z.shape
    xbc = jax.nn.silu(_dwconv(xbc, conv_w, conv_b)).astype(F32)
    xs, bm, cm = _split(xbc, (SSD_DI, SSD_GROUPS * SSD_STATE, SSD_GROUPS * SSD_STATE))
    xs = xs.reshape(bsz, t, SSD_HEADS, SSD_HEADDIM)
    rep = SSD_HEADS // SSD_GROUPS
    bm = jnp.repeat(bm.reshape(bsz, t, SSD_GROUPS, SSD_STATE), rep, axis=2)
    cm = jnp.repeat(cm.reshape(bsz, t, SSD_GROUPS, SSD_STATE), rep, axis=2)
    dt = jax.nn.softplus(dt_raw.astype(F32).reshape(bsz, t, 2, SSD_HEADS) + dt_bias.astype(F32))
    a = -jnp.exp(a_log.astype(F32))
    y = (_ssd_chunked(xs, dt[:, :, 0], a[0], bm, cm)
         + _rev(_ssd_chunked(_rev(xs), _rev(dt[:, :, 1]), a[1], _rev(bm), _rev(cm))))
    y = y + d_skip.astype(F32)[:, None] * xs
    y = y.reshape(bsz, t, SSD_DI) * jax.nn.silu(z.astype(F32))
    y = _rmsnorm(y.reshape(bsz, t, SSD_GROUPS, SSD_DI // SSD_GROUPS),
                 norm_w.reshape(SSD_GROUPS, SSD_DI // SSD_GROUPS))
    return y.reshape(bsz, t, SSD_DI).astype(z.dtype)


def _hgrn_chunked(q, k, v, logf):
    bsz, t, h, dk = q.shape
    dv = v.shape[-1]
    n = t // HGRN_CHUNK
    q = q.reshape(bsz, n, HGRN_CHUNK, h, dk)
    k = k.reshape(bsz, n, HGRN_CHUNK, h, dk)
    v = v.reshape(bsz, n, HGRN_CHUNK, h, dv)
    b = jnp.cumsum(logf.reshape(bsz, n, HGRN_CHUNK, h, dk), axis=2)
    tri = jnp.tril(jnp.ones((HGRN_CHUNK, HGRN_CHUNK), dtype=bool))[:, :, None, None]
    dec = jnp.exp(jnp.where(tri, b[:, :, :, None] - b[:, :, None, :], -jnp.inf))
    scores = jnp.einsum('bnthk,bnshk,bntshk->bnhts', q, k, dec)
    y_intra = jnp.einsum('bnhts,bnshv->bnthv', scores, v)
    q_dec = q * jnp.exp(b)
    k_dec = k * jnp.exp(b[:, :, -1:] - b)
    g_last = jnp.exp(b[:, :, -1])

    def step(state, inp):
        qd, kd, gl, vc = inp
        out = jnp.einsum('bthk,bhkv->bthv', qd, state)
        state = state * gl[..., None] + jnp.einsum('bthk,bthv->bhkv', kd, vc)
        return state, out

    xs = tuple(jnp.moveaxis(a, 1, 0) for a in (q_dec, k_dec, g_last, v))
    s0 = jnp.zeros((bsz, h, dk, dv), F32)
    _, y_inter = lax.scan(step, s0, xs)
    return (y_intra + jnp.moveaxis(y_inter, 0, 1)).reshape(bsz, t, h, dv)


def _hgrn_mixer(q, f_logit, i, g, lb_param, layer, norm_w):
    bsz, t, _ = q.shape
    p = jax.nn.softmax(lb_param.astype(F32), axis=1)
    lb = (jnp.cumsum(p, axis=1) - p[:, :1])[:, layer]
    f = lb + (1.0 - lb) * jax.nn.sigmoid(f_logit.astype(F32).reshape(bsz, t, 2, HGRN_KW))
    logf = jnp.log(f).reshape(bsz, t, 2, HGRN_HEADS, HGRN_DK)
    kk = (1.0 - f).reshape(bsz, t, 2, HGRN_HEADS, HGRN_DK)
    qq = jax.nn.silu(q.astype(F32)).reshape(bsz, t, HGRN_HEADS, HGRN_DK)
    vv = i.astype(F32).reshape(bsz, t, HGRN_HEADS, HGRN_DV)
    o = (_hgrn_chunked(qq, kk[:, :, 0], vv, logf[:, :, 0])
         + _rev(_hgrn_chunked(_rev(qq), _rev(kk[:, :, 1]), _rev(vv), _rev(logf[:, :, 1]))))
    o = _rmsnorm(o, norm_w) * jax.nn.silu(g.astype(F32).reshape(bsz, t, HGRN_HEADS, HGRN_DV))
    return o.reshape(bsz, t, HGRN_VW).astype(q.dtype)


def setup_inputs(seed: int = 0) -> dict:
    key = jax.random.key(seed)
    ks = iter(jax.random.split(key, 40))

    def normal(shape, scale):
        return scale * jax.random.normal(next(ks), shape, F32)

    def gain(shape):
        return 1.0 + normal(shape, 0.02)

    def uniform(shape, lo, hi):
        return jax.random.uniform(next(ks), shape, F32, lo, hi)

    def dt_bias(shape):
        dt = jnp.exp(uniform(shape, math.log(1e-3), math.log(1e-1)))
        return dt + jnp.log(-jnp.expm1(-dt))

    s = jnp.power(uniform((DEPTH, 2, LRU_W), 0.9, 0.999), 1.0 / LRU_C)
    lam = jnp.log(s) - jnp.log1p(-s)
    return {
        'x_prompt': normal((BATCH, SEQ, D_MODEL), 1.0),
        'x_sample': normal((DEC_BATCH, DEC_SEQ, D_MODEL), 1.0),
        'ln1': gain((DEPTH, D_MODEL)),
        'w_in': normal((DEPTH, D_MODEL, D_IN), D_MODEL ** -0.5),
        'gdn_conv_w': normal((DEPTH, CONV_W, 2 * GDN_KW + GDN_VW), CONV_W ** -0.5),
        'gdn_a_log': jnp.log(uniform((DEPTH, 2, GDN_HEADS), 1.0, 16.0)),
        'gdn_dt_bias': dt_bias((DEPTH, 2, GDN_HEADS)),
        'gdn_norm_w': gain((DEPTH, GDN_DV)),
        'lru_conv_w': normal((DEPTH, CONV_W, LRU_W), CONV_W ** -0.5),
        'lru_conv_b': normal((DEPTH, LRU_W), 0.01),
        'lru_wa': normal((DEPTH, 2, LRU_BLOCKS, LRU_BW, LRU_BW), LRU_BW ** -0.5),
        'lru_ba': normal((DEPTH, 2, LRU_W), 0.01),
        'lru_wi': normal((DEPTH, 2, LRU_BLOCKS, LRU_BW, LRU_BW), LRU_BW ** -0.5),
        'lru_bi': normal((DEPTH, 2, LRU_W), 0.01),
        'lru_lambda': lam,
        'ssd_conv_w': normal((DEPTH, CONV_W, SSD_XBC), CONV_W ** -0.5),
        'ssd_conv_b': normal((DEPTH, SSD_XBC), 0.01),
        'ssd_a_log': jnp.log(uniform((DEPTH, 2, SSD_HEADS), 1.0, 16.0)),
        'ssd_dt_bias': dt_bias((DEPTH, 2, SSD_HEADS)),
        'ssd_d': 1.0 + normal((DEPTH, SSD_HEADS), 0.1),
        'ssd_norm_w': gain((DEPTH, SSD_DI)),
        'hgrn_lb': normal((2, DEPTH, HGRN_KW), 0.5),
        'hgrn_norm_w': gain((DEPTH, HGRN_DV)),
        'group_norm_w': gain((DEPTH, N_GROUPS, GROUP_W)),
        'w_out': normal((DEPTH, D_MIX, D_MODEL), D_MIX ** -0.5),
        'ln2': gain((DEPTH, D_MODEL)),
        'w_up': normal((DEPTH, D_MODEL, 2 * D_FF), D_MODEL ** -0.5),
        'ffn_conv_w': normal((DEPTH, FFN_CONV_W, 2 * D_FF), FFN_CONV_W ** -0.5),
        'ffn_conv_b': normal((DEPTH, 2 * D_FF), 0.01),
        'w_down': normal((DEPTH, D_FF, D_MODEL), D_FF ** -0.5),
        'final_norm': gain((D_MODEL,)),
    }


def reference(x_prompt, x_sample, ln1, w_in, gdn_conv_w, gdn_a_log, gdn_dt_bias, gdn_norm_w,
              lru_conv_w, lru_conv_b, lru_wa, lru_ba, lru_wi, lru_bi, lru_lambda,
              ssd_conv_w, ssd_conv_b, ssd_a_log, ssd_dt_bias, ssd_d, ssd_norm_w,
              hgrn_lb, hgrn_norm_w, group_norm_w, w_out, ln2, w_up, ffn_conv_w, ffn_conv_b,
              w_down, final_norm):
    def run(x):
        for l in range(DEPTH):
            h = _rmsnorm(x, ln1[l])
            (gdn_qkv, gdn_z, gdn_beta, gdn_alpha, lru_x, lru_gate, ssd_z, ssd_xbc, ssd_dt,
             hg_q, hg_f, hg_i, hg_g) = _split(h @ w_in[l], IN_SIZES)
            ya = _gdn_mixer(gdn_qkv, gdn_z, gdn_beta, gdn_alpha, gdn_conv_w[l], gdn_a_log[l],
                            gdn_dt_bias[l], gdn_norm_w[l])
            yb = _rglru_mixer(lru_x, lru_gate, lru_conv_w[l], lru_conv_b[l], lru_wa[l], lru_ba[l],
                              lru_wi[l], lru_bi[l], lru_lambda[l])
            yc = _ssd_mixer(ssd_z, ssd_xbc, ssd_dt, ssd_conv_w[l], ssd_conv_b[l], ssd_a_log[l],
                            ssd_dt_bias[l], ssd_d[l], ssd_norm_w[l])
            yd = _hgrn_mixer(hg_q, hg_f, hg_i, hg_g, hgrn_lb, l, hgrn_norm_w[l])
            gn = group_norm_w[l]
            mix = jnp.concatenate([_rmsnorm(ya, gn[0]), _rmsnorm(yb, gn[1]),
                                   _rmsnorm(yc, gn[2]), _rmsnorm(yd, gn[3])], axis=-1)
            x = x + mix @ w_out[l]
            h = _rmsnorm(x, ln2[l])
            up = _dwconv(h @ w_up[l], ffn_conv_w[l], ffn_conv_b[l])
            gate, val = jnp.split(up, 2, axis=-1)
            x = x + (jax.nn.silu(gate) * val) @ w_down[l]
        return _rmsnorm(x, final_norm)

    y_prompt = run(x_prompt)
    y_sample = run(x_sample)
    return (y_prompt, y_sample)
```

```python
import numpy as np
import ml_dtypes
from contextlib import ExitStack
import concourse.bass as bass
import concourse.mybir as mybir
from concourse.bass_utils import run_bass_kernel_spmd

F32 = mybir.dt.float32
BF16 = mybir.dt.bfloat16
AF = mybir.ActivationFunctionType
ALU = mybir.AluOpType
AX = mybir.AxisListType

ENGS = ('sync', 'scalar', 'vector', 'gpsimd', 'tensor')

D = 2048
KC = 16
DIN = 7200
DFF = 5632
NFC = 44
FFQ = 4
EPS = 1e-6
DEPTH = 2


class Tk:
    __slots__ = ('w', 'r', 'sem', 'cnt', 'name')

    def __init__(self, name=''):
        self.w = None
        self.r = {}
        self.sem = None
        self.cnt = 0
        self.name = name


class Prog:
    def __init__(self, nc, es, same_engine_sync=True):
        self.nc = nc
        self.es = es
        self.q = {e: [] for e in ENGS}
        self.sem = {e: es.enter_context(nc.semaphore('s_' + e)) for e in ENGS}
        self.cnt = {e: 0 for e in ENGS}
        self.waited = {e: {} for e in ENGS}
        self.semobj = {}
        self.same_engine_sync = same_engine_sync
        self.dma_tks = []
        self.pool = []
        self.nsem = len(ENGS)

    def _deps(self, e, reads, writes):
        deps = {}

        def add(s, v):
            k = id(s)
            self.semobj[k] = s
            if deps.get(k, 0) < v:
                deps[k] = v
        for t in reads:
            if t.w is not None:
                add(*t.w)
        for t in writes:
            if t.w is not None:
                add(*t.w)
            for k, (s_, v_) in t.r.items():
                add(s_, v_)
        waits = []
        own = id(self.sem[e])
        for k, v in deps.items():
            if k == own and (e == 'tensor' or not self.same_engine_sync):
                continue
            if self.waited[e].get(k, 0) < v:
                self.waited[e][k] = v
                waits.append((self.semobj[k], v))
        return waits

    def _mark(self, d, reads, writes):
        for t in writes:
            t.w = d
            t.r = {}
        for t in reads:
            k = id(d[0])
            if k not in t.r or t.r[k][1] < d[1]:
                t.r[k] = d

    def op(self, e, fn, reads=(), writes=()):
        waits = self._deps(e, reads, writes)
        self.cnt[e] += 1
        self.q[e].append((waits, fn, (self.sem[e], 1)))
        self._mark((self.sem[e], self.cnt[e]), reads, writes)

    def dma(self, e, out_ap, in_ap, reads=(), writes=(), dtk=None, **kw):
        waits = self._deps(e, reads, writes)
        tk = dtk
        if tk.sem is None:
            if self.pool:
                tk.sem, tk.cnt = self.pool.pop()
            else:
                tk.sem = self.es.enter_context(self.nc.semaphore())
                tk.cnt = 0
                self.nsem += 1
            self.dma_tks.append(tk)
        tk.cnt += 16
        self.q[e].append((waits, lambda eng: eng.dma_start(out=out_ap, in_=in_ap, **kw), (tk.sem, 16)))
        self._mark((tk.sem, tk.cnt), reads, writes)

    def barrier(self, final=False):
        for e in ENGS:
            waits = []
            for e2 in ENGS:
                if e2 != e and self.cnt[e2] > 0 and self.waited[e].get(id(self.sem[e2]), 0) < self.cnt[e2]:
                    self.waited[e][id(self.sem[e2])] = self.cnt[e2]
                    waits.append((self.sem[e2], self.cnt[e2]))
            for tk in self.dma_tks:
                if self.waited[e].get(id(tk.sem), 0) < tk.cnt:
                    self.waited[e][id(tk.sem)] = tk.cnt
                    waits.append((tk.sem, tk.cnt))
            if waits:
                self.q[e].append((waits, None, None))
        for tk in self.dma_tks:
            self.pool.append((tk.sem, tk.cnt))
            tk.sem = None
        self.dma_tks = []

    def replay(self):
        nc = self.nc
        with nc.Block() as block:
            def mk(e):
                def body(eng):
                    for waits, fn, inc in self.q[e]:
                        for s, v in waits:
                            eng.wait_ge(s, v)
                        if fn is not None:
                            fn(eng).then_inc(*inc)
                return body
            block.sync(mk('sync'))
            block.scalar(mk('scalar'))
            block.vector(mk('vector'))
            block.gpsimd(mk('gpsimd'))
            block.tensor(mk('tensor'))


class PL:
    def __init__(self):
        self.off = {}
        self.n = 0

    def add(self, name, ncols):
        self.off[name] = self.n
        self.n += ncols


def param_layout():
    pl = PL()
    for l in range(DEPTH):
        p = 'L%d_' % l
        pl.add(p + 'ln1', 16)
        pl.add(p + 'ln2', 16)
        pl.add(p + 'gdn_conv', 48)
        pl.add(p + 'gdn_alog', 1)
        pl.add(p + 'gdn_dtb', 1)
        pl.add(p + 'gdn_norm', 1)
        pl.add(p + 'lru_conv', 16)
        pl.add(p + 'lru_convb', 4)
        pl.add(p + 'lru_ba', 8)
        pl.add(p + 'lru_bi', 8)
        pl.add(p + 'lru_lam', 8)
        pl.add(p + 'ssd_conv', 32)
        pl.add(p + 'ssd_convb', 8)
        pl.add(p + 'ssd_alog', 1)
        pl.add(p + 'ssd_dtb', 1)
        pl.add(p + 'ssd_d', 4)
        pl.add(p + 'ssd_dtb_bc', 16)
        pl.add(p + 'ssd_alog_bc', 16)
        pl.add(p + 'gdn_dtb_bc', 8)
        pl.add(p + 'gdn_alog_bc', 8)
        pl.add(p + 'ssd_norm', 4)
        pl.add(p + 'hg_lb0', 8)
        pl.add(p + 'hg_lb1', 8)
        pl.add(p + 'hg_norm', 1)
        pl.add(p + 'gn', 16)
        pl.add(p + 'ffn_conv', 264)
        pl.add(p + 'ffn_convb', 88)
    pl.add('final', 16)
    return pl


def _cols(v):
    v = np.asarray(v, np.float32).reshape(-1, 128)
    return np.ascontiguousarray(v.T)


def _rows(v):
    v = np.asarray(v, np.float32).reshape(-1)
    o = np.zeros((128, 1), np.float32)
    o[:v.size, 0] = v
    return o


def pack_params(inp):
    pl = param_layout()
    P = np.zeros((128, pl.n), np.float32)

    def put(name, arr):
        o = pl.off[name]
        P[:, o:o + arr.shape[1]] = arr
    for l in range(DEPTH):
        p = 'L%d_' % l
        put(p + 'ln1', _cols(inp['ln1'][l]))
        put(p + 'ln2', _cols(inp['ln2'][l]))
        put(p + 'gdn_conv', np.concatenate([_cols(inp['gdn_conv_w'][l, t]) for t in range(4)], 1))
        put(p + 'gdn_alog', _rows(inp['gdn_a_log'][l]))
        put(p + 'gdn_dtb', _rows(inp['gdn_dt_bias'][l]))
        put(p + 'gdn_norm', _cols(inp['gdn_norm_w'][l]))
        put(p + 'lru_conv', np.concatenate([_cols(inp['lru_conv_w'][l, t]) for t in range(4)], 1))
        put(p + 'lru_convb', _cols(inp['lru_conv_b'][l]))
        put(p + 'lru_ba', _cols(inp['lru_ba'][l]))
        put(p + 'lru_bi', _cols(inp['lru_bi'][l]))
        put(p + 'lru_lam', _cols(inp['lru_lambda'][l]))
        put(p + 'ssd_conv', np.concatenate([_cols(inp['ssd_conv_w'][l, t]) for t in range(4)], 1))
        put(p + 'ssd_convb', _cols(inp['ssd_conv_b'][l]))
        put(p + 'ssd_alog', _rows(inp['ssd_a_log'][l]))
        put(p + 'ssd_dtb', _rows(inp['ssd_dt_bias'][l]))
        put(p + 'ssd_d', _cols(np.repeat(np.asarray(inp['ssd_d'][l]), 64)))
        put(p + 'ssd_norm', _cols(inp['ssd_norm_w'][l]))
        put(p + 'ssd_dtb_bc', np.broadcast_to(np.asarray(inp['ssd_dt_bias'][l], np.float32).reshape(1, 16), (128, 16)))
        put(p + 'ssd_alog_bc', np.broadcast_to(np.asarray(inp['ssd_a_log'][l], np.float32).reshape(1, 16), (128, 16)))
        put(p + 'gdn_dtb_bc', np.broadcast_to(np.asarray(inp['gdn_dt_bias'][l], np.float32).reshape(1, 8), (128, 8)))
        put(p + 'gdn_alog_bc', np.broadcast_to(np.asarray(inp['gdn_a_log'][l], np.float32).reshape(1, 8), (128, 8)))
        put(p + 'hg_lb0', _cols(inp['hgrn_lb'][:, 0]))
        put(p + 'hg_lb1', _cols(inp['hgrn_lb'][:, 1]))
        put(p + 'hg_norm', _cols(inp['hgrn_norm_w'][l]))
        put(p + 'gn', _cols(inp['group_norm_w'][l]))
        put(p + 'ffn_conv', np.concatenate([_cols(inp['ffn_conv_w'][l, t]) for t in range(3)], 1))
        put(p + 'ffn_convb', _cols(inp['ffn_conv_b'][l]))
    put('final', _cols(inp['final_norm']))
    return P, pl


class Builder:
    def __init__(self, T, n_layers=DEPTH, mixers='abcd', same_engine_sync=True, dbg=None, nseg=1):
        self.NSEG = nseg
        self.TT = T * nseg
        self.seg = 0
        self.T = T
        self.NB = T // 512
        self.NT = T // 128
        self.n_layers = n_layers
        self.mixers = mixers
        self.dbg = dbg
        self.nc = bass.Bass("TRN2", target_bir_lowering=False)
        self.es = ExitStack()
        self.P = Prog(self.nc, self.es, same_engine_sync)
        self.pl = param_layout()

    def sb(self, name, shape, dt=F32):
        return self.es.enter_context(self.nc.sbuf_tensor(name, shape, dt))

    def dram_in(self, name, shape, dt=F32):
        return self.nc.dram_tensor(name, shape, dt, kind="ExternalInput").ap()

    def dram_out(self, name, shape, dt=F32):
        return self.nc.dram_tensor(name, shape, dt, kind="ExternalOutput").ap()

    def dram_scr(self, name, shape, dt=F32):
        return self.nc.dram_tensor(name, shape, dt).ap()

    def pc(self, name, j=0, n=1, rows=128):
        o = self.pl.off[name] + j
        return self.params[0:rows, o:o + n]

    def rx_reset(self):
        self.P.barrier()
        self.rx_off = 0

    def rx(self, shape, dt=F32):
        n = int(np.prod(shape))
        units = n if dt == F32 else (n + 1) // 2
        a = self.arena[:, self.rx_off:self.rx_off + units]
        self.rx_off += units
        assert self.rx_off <= self.RXN, ("arena overflow", self.rx_off, self.RXN)
        if dt != F32:
            a = a.bitcast(dt)
        if len(shape) == 1:
            a = a.rearrange("p (a b) -> p a b", a=1)
        elif len(shape) == 2:
            a = a.rearrange("p (a b) -> p a b", b=shape[1])
        elif len(shape) == 3:
            a = a.rearrange("p (a b c) -> p a b c", b=shape[1], c=shape[2])
        return a

    def setup(self):
        nc, T = self.nc, self.T
        L = DEPTH
        TT = self.TT
        self.xT = self.dram_in("xT", [D, TT])
        self.flags_d = self.dram_in("flags", [128, 4])
        self.w_in = self.dram_in("w_in", [L, D, DIN])
        self.w_out = self.dram_in("w_out", [L, D, D])
        self.w_up = self.dram_in("w_up", [L, D, 2 * DFF])
        self.w_down = self.dram_in("w_down", [L, DFF, D])
        self.lru_w = self.dram_in("lru_w", [L, 128, 16 * 128])
        self.params_d = self.dram_in("params", [128, self.pl.n])
        self.consts_d = self.dram_in("consts", [128, 768])
        self.yT = self.dram_out("yT", [D, TT])
        self.xs = self.dram_scr("xs", [D, TT])
        self.mixD = self.dram_scr("mixD", [D, TT], BF16)
        self.yF = self.dram_scr("yF", [D, TT])
        self.xs_tk = [[Tk() for _ in range(self.NB * self.NSEG)] for _ in range(KC)]
        self.x0_tk = Tk()
        self.mix_tk = [[Tk() for _ in range(self.NSEG)] for _ in range(KC)]
        self.yF_tk = [[Tk() for _ in range(self.NSEG)] for _ in range(KC)]
        self.flags = self.sb("flags_sb", [128, 4])
        self.hh2 = self.sb("hh2", [128, self.NSEG, KC, 2], BF16)
        self.hh2_tk = Tk()
        self.st_gdn = self.sb("st_gdn", [128, 4, 128]); self.st_gdn_tk = [Tk() for _ in range(4)]
        self.st_hg = self.sb("st_hg", [128, 4, 128]); self.st_hg_tk = [Tk() for _ in range(4)]
        self.st_ssd = self.sb("st_ssd", [128, 8, 64]); self.st_ssd_tk = [Tk() for _ in range(8)]
        self.st_lru = self.sb("st_lru", [128, 4]); self.st_lru_tk = Tk()
        self.params = self.sb("params_sb", [128, self.pl.n])
        self.consts = self.sb("consts_sb", [128, 768])
        self.cb = self.sb("consts_bf", [128, 768], BF16)
        self.hT = self.sb("hT", [128, KC, T], BF16)
        self.hT_tk = [Tk() for _ in range(self.NB)]
        self.hh = self.sb("hhalo", [128, KC, 4], BF16)
        self.hh_tk = Tk()
        self.wst = [self.sb("wst%d" % i, [128, KC, 128]) for i in range(2)]
        self.wst_tk = [Tk() for _ in range(2)]
        self.wbf = [self.sb("wbf%d" % i, [128, KC, 128], BF16) for i in range(3)]
        self.wbf_tk = [Tk() for _ in range(3)]
        self.wi = 0
        self.RXN = 24 * 1024
        self.arena = self.sb("arena", [128, self.RXN])
        self.rx_off = 0
        self.ps = self.es.enter_context(nc.psum_tensor("ps", [128, 8, 512], F32))
        self.ps_tk = [Tk() for _ in range(8)]
        P = self.P
        self.c_tk = Tk()
        P.dma('sync', self.params[:], self.params_d[:, :], writes=[self.c_tk], dtk=self.c_tk)
        P.dma('sync', self.consts[:], self.consts_d[:, :], writes=[self.c_tk], dtk=self.c_tk)
        P.dma('sync', self.flags[:], self.flags_d[:, :], writes=[self.c_tk], dtk=self.c_tk)
        P.op('vector', lambda e: e.tensor_copy(out=self.cb[:], in_=self.consts[:]), reads=[self.c_tk], writes=[self.c_tk])
        P.op('vector', lambda e: e.memset(self.hh[:], 0.0), writes=[self.hh_tk])
        self.ident_f = self.consts[:, 0:128]
        self.ident_b = self.cb[:, 0:128]
        self.triu_b = self.cb[:, 128:256]
        self.ones_b = self.cb[:, 384:512]
        self.striu_b = self.cb[:, 256:384]
        self.blk32_f = self.consts[:, 512:640]
        self.J_f = self.consts[:, 640:768]
        self.triu_f = self.consts[:, 128:256]
        self.striu_f = self.consts[:, 256:384]
        self.ones_f = self.consts[:, 384:512]
        self.carry = self.flags[:, 0:1]
        self.ncarry = self.flags[:, 1:2]

    def gb(self, tb):
        return self.seg * self.NB + tb

    def tsl(self, a, b):
        return slice(self.seg * self.T + a, self.seg * self.T + b)

    def ln_halo(self, src, src_tks, wname, toks, dst, dst_tk):
        P = self.P
        nh = len(toks)
        xh = self.rx([KC, nh]); xh_tk = Tk()
        sqh = self.rx([KC, nh], BF16); sqh_tk = Tk()
        rs = self.rx([nh])[:, 0, :]; rs_tk = Tk()
        e_memset(self, xh, 0.0, [xh_tk])
        for j, t in enumerate(toks):
            if t is None:
                continue
            P.dma('gpsimd', xh[:, :, j:j + 1], src[:, t:t + 1].rearrange("(kc p) o -> p kc o", p=128), reads=src_tks, writes=[xh_tk], dtk=xh_tk,
                  allow_slow_non_contiguous=True)
        e_act(self, sqh, xh, AF.Square, [xh_tk], [sqh_tk])
        for kc in range(KC):
            e_mm(self, self.ps[:, 5, 0:nh], self.ones_b, sqh[:, kc, :], kc == 0, kc == KC - 1, [sqh_tk, self.c_tk], [self.ps_tk[5]])
        e_act(self, rs, self.ps[:, 5, 0:nh], AF.Sqrt, [self.ps_tk[5]], [rs_tk], scale=1.0 / D, bias=EPS)
        P.op('vector', lambda e: e.reciprocal(out=rs, in_=rs), reads=[rs_tk], writes=[rs_tk])
        e_ts(self, rs, rs, self.carry, None, ALU.mult, None, [rs_tk, self.c_tk], [rs_tk])
        o = self.pl.off[wname]
        wv = self.params[:, o:o + KC].rearrange("p (k o) -> p k o", o=1).to_broadcast([128, KC, nh])
        e_tt(self, xh, xh, wv, ALU.mult, [xh_tk, self.c_tk], [xh_tk])
        e_tt(self, dst, xh, rs.rearrange("p (o n) -> p o n", o=1).to_broadcast([128, KC, nh]), ALU.mult, [xh_tk, rs_tk], [dst_tk])

    def load_w(self, src_ap, nk, ncol):
        P = self.P
        i = self.wi
        self.wi += 1
        st, st_tk = self.wst[i % 2], self.wst_tk[i % 2]
        wb, wb_tk = self.wbf[i % 3], self.wbf_tk[i % 3]
        src3 = src_ap if len(src_ap.shape) == 3 else src_ap.rearrange("(kc p) n -> p kc n", p=128)
        P.dma('sync', st[:, 0:nk, 0:ncol], src3, writes=[st_tk], dtk=st_tk)
        P.op('gpsimd', lambda e: e.tensor_copy(out=wb[:, 0:nk, 0:ncol], in_=st[:, 0:nk, 0:ncol]), reads=[st_tk], writes=[wb_tk])
        return wb, wb_tk

    def w_begin(self, specs):
        self.wspecs = specs
        self.wpos = 0
        self.wloaded = {}

    def w_get(self):
        i = self.wpos
        if i not in self.wloaded:
            self.wloaded[i] = self.load_w(*self.wspecs[i])
        self.wpos += 1
        return self.wloaded.pop(i)

    def w_prefetch(self):
        i = self.wpos
        if i < len(self.wspecs) and i not in self.wloaded:
            self.wloaded[i] = self.load_w(*self.wspecs[i])

    def proj(self, nk, ncol, rhs_fn, rhs_tks_fn, evac_fn, banks=(0, 1, 2, 3), halo=None):
        P = self.P
        wb, wb_tk = self.w_get()
        if halo is not None:
            hr, hr_tk, hev = halo
            nh = hr.shape[2]
            for kc in range(nk):
                e_mm(self, self.ps[0:ncol, 5, 0:nh], wb[:, kc, 0:ncol], hr[:, kc, :], kc == 0, kc == nk - 1, [wb_tk, hr_tk], [self.ps_tk[5]])
            hev(self.ps[0:ncol, 5, 0:nh], self.ps_tk[5])
        for kc in range(nk):
            for tb in range(self.NB):
                b = banks[tb]
                P.op('tensor', lambda e, kc=kc, tb=tb, b=b: e.matmul(self.ps[0:ncol, b, :], lhsT=wb[:, kc, 0:ncol], rhs=rhs_fn(kc, tb),
                                                                  start=(kc == 0), stop=(kc == nk - 1)),
                     reads=[wb_tk] + rhs_tks_fn(kc, tb), writes=[self.ps_tk[b]])
        self.w_prefetch()
        for tb in range(self.NB):
            b = banks[tb]
            evac_fn(tb, self.ps[0:ncol, b, :], self.ps_tk[b])

    def ln_phase(self, src, src_tks_fn, wname, final=False, reset=True):
        P, T = self.P, self.T
        if reset:
            self.rx_reset()
        xts = [self.rx([KC, 512]) for _ in range(2)]
        xt_tk = [Tk() for _ in range(2)]
        sqs = [self.rx([512], BF16) for _ in range(2)]
        sq_tk = [Tk() for _ in range(2)]
        rstd = self.rx([512])
        rstd_tk = Tk()
        for tb in range(self.NB):
            xt, xtk = xts[tb % 2], xt_tk[tb % 2]
            P.dma('sync', xt, src[:, self.tsl(tb * 512, (tb + 1) * 512)].rearrange("(kc p) t -> p kc t", p=128),
                  reads=src_tks_fn(self.gb(tb)), writes=[xtk], dtk=xtk)
            bank = 4 + tb % 2
            for kc in range(KC):
                sq, stk = sqs[kc % 2], sq_tk[kc % 2]
                P.op('scalar', lambda e, sq=sq, xt=xt, kc=kc: e.activation(out=sq[:, 0, :], in_=xt[:, kc, :], func=AF.Square),
                     reads=[xtk], writes=[stk])
                P.op('tensor', lambda e, sq=sq, kc=kc, bank=bank: e.matmul(self.ps[:, bank, :], lhsT=self.ones_b, rhs=sq[:, 0, :],
                                                                          start=(kc == 0), stop=(kc == KC - 1)),
                     reads=[stk, self.c_tk], writes=[self.ps_tk[bank]])
            P.op('scalar', lambda e, bank=bank: e.activation(out=rstd[:, 0, :], in_=self.ps[:, bank, :], func=AF.Sqrt, scale=1.0 / D, bias=EPS),
                 reads=[self.ps_tk[bank]], writes=[rstd_tk])
            P.op('vector', lambda e: e.reciprocal(out=rstd[:, 0, :], in_=rstd[:, 0, :]), reads=[rstd_tk], writes=[rstd_tk])
            for kc in range(KC):
                eng = 'vector' if kc % 2 == 0 else 'gpsimd'
                if final:
                    dst = xt[:, kc, :]
                    wr = [xtk]
                else:
                    dst = self.hT[:, kc, tb * 512:(tb + 1) * 512]
                    wr = [self.hT_tk[tb]]
                if eng == 'vector':
                    P.op('vector', lambda e, dst=dst, xt=xt, kc=kc: e.scalar_tensor_tensor(out=dst, in0=xt[:, kc, :], scalar=self.pc(wname, kc), in1=rstd[:, 0, :],
                                                                                       op0=ALU.mult, op1=ALU.mult),
                         reads=[xtk, rstd_tk, self.c_tk], writes=wr)
                else:
                    P.op('gpsimd', lambda e, xt=xt, kc=kc: e.tensor_scalar(out=xt[:, kc, :], in0=xt[:, kc, :], scalar1=self.pc(wname, kc), scalar2=None, op0=ALU.mult),
                         reads=[self.c_tk], writes=[xtk])
                    P.op('gpsimd', lambda e, dst=dst, xt=xt, kc=kc: e.tensor_tensor(out=dst, in0=xt[:, kc, :], in1=rstd[:, 0, :], op=ALU.mult),
                         reads=[xtk, rstd_tk], writes=wr)
            if final:
                P.dma('sync', self.yT[:, self.tsl(tb * 512, (tb + 1) * 512)].rearrange("(kc p) t -> p kc t", p=128), xt, reads=[xtk], dtk=xtk)

    def resid_specs(self, w_fn, nk):
        return [(w_fn(dc), nk, 128) for dc in range(KC)]

    def resid_proj(self, nk, act, act_tks_fn, src, src_tks_fn):
        P = self.P
        for dc in range(KC):
            def evac(tb, ps_ap, ps_tk, dc=dc):
                xi, xi_tk = self.xin[self.xin_i % 3], self.xin_tk[self.xin_i % 3]
                self.xin_i += 1
                gsl = self.tsl(tb * 512, (tb + 1) * 512)
                P.dma('gpsimd', xi[:, 0, :], src[dc * 128:(dc + 1) * 128, gsl], reads=src_tks_fn(dc, self.gb(tb)), writes=[xi_tk], dtk=xi_tk)
                P.op('vector', lambda e: e.tensor_tensor(out=xi[:, 0, :], in0=ps_ap, in1=xi[:, 0, :], op=ALU.add), reads=[ps_tk, xi_tk], writes=[xi_tk])
                P.dma('gpsimd', self.xs[dc * 128:(dc + 1) * 128, gsl], xi[:, 0, :], reads=[xi_tk], writes=[self.xs_tk[dc][self.gb(tb)]], dtk=xi_tk)
            self.proj(nk, 128, lambda kc, tb: act[:, kc, tb * 512:(tb + 1) * 512], act_tks_fn, evac)

    def alloc_xin(self):
        self.xin = [self.rx([512]) for _ in range(3)]
        self.xin_tk = [Tk() for _ in range(3)]
        self.xin_i = 0

    def ffn_phase(self, l):
        P, T = self.P, self.T
        self.rx_reset()
        NQ = NFC // FFQ
        act = self.rx([NQ, T], BF16)
        act_tk = [Tk() for _ in range(NQ)]
        ug = self.rx([T + 2]); ug_tk = Tk()
        cg = self.rx([T]); cg_tk = Tk()
        uv = self.rx([T + 2]); uv_tk = Tk()
        cv = self.rx([T]); cv_tk = Tk()
        self.alloc_xin()
        ug, cg, uv, cv = ug[:, 0, :], cg[:, 0, :], uv[:, 0, :], cv[:, 0, :]
        pre = 'L%d_' % l
        hT = self.hT

        def up_chunk(col, fcg, u, u_tk, c, c_tk):
            def evac(tb, ps_ap, ps_tk):
                P.op('scalar', lambda e: e.activation(out=u[:, 1 + tb * 512:1 + (tb + 1) * 512], in_=ps_ap, func=AF.Copy), reads=[ps_tk], writes=[u_tk])
                P.op('scalar', lambda e: e.activation(out=c[:, tb * 512:(tb + 1) * 512], in_=ps_ap, func=AF.Identity,
                                                      scale=self.pc(pre + 'ffn_conv', 88 + fcg), bias=self.pc(pre + 'ffn_convb', fcg)),
                     reads=[ps_tk, self.c_tk], writes=[c_tk])
            def hev(ps_ap, ps_tk):
                e_act(self, u[:, 0:1], ps_ap[:, 0:1], AF.Copy, [ps_tk], [u_tk])
                e_act(self, u[:, T + 1:T + 2], ps_ap[:, 1:2], AF.Copy, [ps_tk], [u_tk])
            self.proj(KC, 128, lambda kc, tb: hT[:, kc, tb * 512:(tb + 1) * 512],
                      lambda kc, tb: [self.hT_tk[tb]], evac, halo=(self.hh2[:, self.seg], self.hh2_tk, hev))
            P.op('vector', lambda e: e.scalar_tensor_tensor(out=c, in0=u[:, 0:T], scalar=self.pc(pre + 'ffn_conv', fcg), in1=c, op0=ALU.mult, op1=ALU.add),
                 reads=[u_tk, c_tk, self.c_tk], writes=[c_tk])
            P.op('vector', lambda e: e.scalar_tensor_tensor(out=c, in0=u[:, 2:T + 2], scalar=self.pc(pre + 'ffn_conv', 176 + fcg), in1=c, op0=ALU.mult, op1=ALU.add),
                 reads=[u_tk, c_tk, self.c_tk], writes=[c_tk])

        specs = []
        for q in range(FFQ):
            for j in range(NQ):
                fc = q * NQ + j
                specs.append((self.w_up[l, :, fc * 128:(fc + 1) * 128], KC, 128))
                specs.append((self.w_up[l, :, DFF + fc * 128:DFF + (fc + 1) * 128], KC, 128))
            specs += self.resid_specs(lambda dc, q=q: self.w_down[l, q * NQ * 128:(q + 1) * NQ * 128, dc * 128:(dc + 1) * 128], NQ)
        self.w_begin(specs)
        for q in range(FFQ):
            for j in range(NQ):
                fc = q * NQ + j
                up_chunk(fc * 128, fc, ug, ug_tk, cg, cg_tk)
                P.op('scalar', lambda e: e.activation(out=cg, in_=cg, func=AF.Silu), reads=[cg_tk], writes=[cg_tk])
                up_chunk(DFF + fc * 128, NFC + fc, uv, uv_tk, cv, cv_tk)
                P.op('gpsimd', lambda e, j=j: e.tensor_tensor(out=act[:, j, :], in0=cg, in1=cv, op=ALU.mult), reads=[cg_tk, cv_tk], writes=[act_tk[j]])
            self.resid_proj(NQ, act, lambda kc, tb: [act_tk[kc]], self.xs, lambda dc, tb: [self.xs_tk[dc][tb]])

    def mixer_phase(self, l):
        from_mixers(self, l)

    def outproj_phase(self, l, src, src_tks_fn):
        P, T = self.P, self.T
        self.rx_reset()
        self.alloc_xin()
        for kc in range(KC):
            P.dma('sync', self.hT[:, kc, :], self.mixD[kc * 128:(kc + 1) * 128, self.tsl(0, T)], reads=[self.mix_tk[kc][self.seg]], writes=self.hT_tk, dtk=self.hT_tk[0])
        self.w_begin(self.resid_specs(lambda dc: self.w_out[l, :, dc * 128:(dc + 1) * 128], KC))
        self.resid_proj(KC, self.hT, lambda kc, tb: [self.hT_tk[tb]], src, src_tks_fn)

    def halo_toks(self, left, right):
        t0 = self.seg * self.T
        toks = []
        for k in range(left, 0, -1):
            toks.append(t0 - k if t0 - k >= 0 else None)
        for k in range(right):
            t = t0 + self.T + k
            toks.append(t if t < self.TT else None)
        return toks

    def build(self):
        self.setup()
        NS = self.NSEG
        all_tk = lambda gbk: [self.xs_tk[dc][gbk] for dc in range(KC)]
        for l in range(self.n_layers):
            pre = 'L%d_' % l
            if l == 0:
                src, stf, stf2 = self.xT, (lambda gbk: [self.x0_tk]), (lambda dc, gbk: [self.x0_tk])
                halo_src_tks = [self.x0_tk]
            else:
                src, stf, stf2 = self.xs, all_tk, (lambda dc, gbk: [self.xs_tk[dc][gbk]])
                halo_src_tks = [t for row in self.xs_tk for t in row]
            for d in (0, 1):
                for seg in (range(NS) if d == 0 else range(NS - 1, -1, -1)):
                    self.seg = seg
                    self.ln_phase(src, stf, pre + 'ln1')
                    self.ln_halo(src, halo_src_tks, pre + 'ln1', self.halo_toks(2, 1), self.hh[:, :, 0:3], self.hh_tk)
                    from_mixers(self, l, d)
            for seg in range(NS):
                self.seg = seg
                self.outproj_phase(l, src, stf2)
            self.rx_reset()
            xs_all = [t for row in self.xs_tk for t in row]
            for seg in range(NS):
                self.seg = seg
                self.ln_halo(self.xs, xs_all, pre + 'ln2', self.halo_toks(1, 1), self.hh2[:, seg], self.hh2_tk)
            for seg in range(NS):
                self.seg = seg
                self.ln_phase(self.xs, all_tk, pre + 'ln2')
                self.ffn_phase(l)
        for seg in range(NS):
            self.seg = seg
            self.ln_phase(self.xs, all_tk, 'final', final=True)
        self.P.barrier(final=True)
        self.P.replay()
        return self.nc


def from_mixers(B, l, d):
    P, T = B.P, B.T
    for g, (name, fn) in enumerate((('a', mixer_gdn), ('b', mixer_lru), ('c', mixer_ssd), ('d', mixer_hgrn))):
        B.rx_reset()
        if name in B.mixers:
            fn(B, l, g, d)
        elif d == 1:
            z = B.rx([T], BF16)
            ztk = Tk()
            P.op('vector', lambda e, z=z: e.memset(z[:, 0, :], 0.0), writes=[ztk])
            for c in range(4):
                kc = g * 4 + c
                P.dma('sync', B.mixD[kc * 128:(kc + 1) * 128, B.tsl(0, T)], z[:, 0, :], reads=[ztk], writes=[B.mix_tk[kc][B.seg]], dtk=ztk)


def is_first(B, d):
    return B.seg == 0 if d == 0 else B.seg == B.NSEG - 1


def state_init(B, d, S, S_tk):
    if is_first(B, d):
        e_memset(B, S, 0.0, [S_tk])
    else:
        e_ts(B, S, S, B.carry, None, ALU.mult, None, [S_tk, B.c_tk], [S_tk])


def yf_store(B, kc, y_ap, y_tk):
    B.P.dma('sync', B.yF[kc * 128:(kc + 1) * 128, B.tsl(0, B.T)], y_ap, reads=[y_tk], writes=[B.yF_tk[kc][B.seg]], dtk=y_tk)


def yf_load(B, kc, y_ap, y_tk):
    B.P.dma('sync', y_ap, B.yF[kc * 128:(kc + 1) * 128, B.tsl(0, B.T)], reads=[B.yF_tk[kc][B.seg]], writes=[y_tk], dtk=y_tk)


def inproj(B, ncol, evac, halo=None):
    B.proj(KC, ncol, lambda kc, tb: B.hT[:, kc, tb * 512:(tb + 1) * 512], lambda kc, tb: [B.hT_tk[tb]], evac, halo=halo)


def win_spec(B, l, col0, ncol):
    return (B.w_in[l, :, col0:col0 + ncol], KC, ncol)


def inproj_raw(B, ncol, u, u_tk, off=2):
    P = B.P

    def evac(tb, ps_ap, ps_tk):
        P.op('scalar', lambda e: e.activation(out=u[0:ncol, off + tb * 512:off + (tb + 1) * 512], in_=ps_ap, func=AF.Copy), reads=[ps_tk], writes=[u_tk])

    def hev(ps_ap, ps_tk):
        T = B.T
        e_act(B, u[0:ncol, 0:2], ps_ap[:, 0:2], AF.Copy, [ps_tk], [u_tk])
        e_act(B, u[0:ncol, T + 2:T + 3], ps_ap[:, 2:3], AF.Copy, [ps_tk], [u_tk])
    inproj(B, ncol, evac, halo=(B.hh[:, :, 0:3], B.hh_tk, hev))


def inproj_act(B, ncol, dst, dst_tk, func, **kw):
    P = B.P

    def evac(tb, ps_ap, ps_tk):
        P.op('scalar', lambda e: e.activation(out=dst[0:ncol, tb * 512:(tb + 1) * 512], in_=ps_ap, func=func, **kw), reads=[ps_tk, B.c_tk], writes=[dst_tk])
    inproj(B, ncol, evac)


def conv4(B, u, u_tk, c, c_tk, wname, nch, ch, bias_ap, dst, dst_tk, func):
    P, T = B.P, B.T
    w = lambda k: B.pc(wname, k * nch + ch)
    P.op('vector', lambda e: e.tensor_scalar(out=c, in0=u[:, 0:T], scalar1=w(0), scalar2=(bias_ap if bias_ap is not None else 0.0), op0=ALU.mult, op1=ALU.add),
         reads=[u_tk, B.c_tk], writes=[c_tk])
    for k in (1, 2, 3):
        P.op('vector', lambda e, k=k: e.scalar_tensor_tensor(out=c, in0=u[:, k:k + T], scalar=w(k), in1=c, op0=ALU.mult, op1=ALU.add),
             reads=[u_tk, c_tk, B.c_tk], writes=[c_tk])
    if dst is not None:
        P.op('scalar', lambda e: e.activation(out=dst, in_=c, func=func), reads=[c_tk], writes=[dst_tk])


def alloc_u(B):
    P, T = B.P, B.T
    u = B.rx([T + 3])[:, 0, :]
    u_tk = Tk()
    return u, u_tk


def group_out(B, l, g, y, y_tk):
    P, T = B.P, B.T
    pre = 'L%d_' % l
    sqs = [B.rx([512], BF16) for _ in range(2)]
    sq_tk = [Tk() for _ in range(2)]
    rstd = B.rx([512])
    rstd_tk = Tk()
    ots = [B.rx([512], BF16) for _ in range(2)]
    ot_tk = [Tk() for _ in range(2)]
    oi = 0
    for tb in range(B.NB):
        bank = 4 + tb % 2
        sl = slice(tb * 512, (tb + 1) * 512)
        for c in range(4):
            sq, stk = sqs[c % 2], sq_tk[c % 2]
            P.op('scalar', lambda e, sq=sq, c=c, sl=sl: e.activation(out=sq[:, 0, :], in_=y[:, c, sl], func=AF.Square), reads=[y_tk], writes=[stk])
            P.op('tensor', lambda e, sq=sq, c=c, bank=bank: e.matmul(B.ps[:, bank, :], lhsT=B.ones_b, rhs=sq[:, 0, :], start=(c == 0), stop=(c == 3)),
                 reads=[stk, B.c_tk], writes=[B.ps_tk[bank]])
        P.op('scalar', lambda e, bank=bank: e.activation(out=rstd[:, 0, :], in_=B.ps[:, bank, :], func=AF.Sqrt, scale=1.0 / 512, bias=EPS),
             reads=[B.ps_tk[bank]], writes=[rstd_tk])
        P.op('vector', lambda e: e.reciprocal(out=rstd[:, 0, :], in_=rstd[:, 0, :]), reads=[rstd_tk], writes=[rstd_tk])
        for c in range(4):
            ot, otk = ots[oi % 2], ot_tk[oi % 2]
            oi += 1
            P.op('vector', lambda e, ot=ot, c=c, sl=sl: e.scalar_tensor_tensor(out=ot[:, 0, :], in0=y[:, c, sl], scalar=B.pc(pre + 'gn', g * 4 + c), in1=rstd[:, 0, :],
                                                                            op0=ALU.mult, op1=ALU.mult),
                 reads=[y_tk, rstd_tk, B.c_tk], writes=[otk])
            kc = g * 4 + c
            P.dma('sync', B.mixD[kc * 128:(kc + 1) * 128, B.tsl(tb * 512, (tb + 1) * 512)], ot[:, 0, :], reads=[otk], writes=[B.mix_tk[kc][B.seg]], dtk=otk)


def e_act(B, out, in_, func, reads, writes, eng='scalar', **kw):
    B.P.op(eng, lambda e: e.activation(out=out, in_=in_, func=func, **kw), reads=reads, writes=writes)


def e_mm(B, out, lhsT, rhs, start, stop, reads, writes):
    B.P.op('tensor', lambda e: e.matmul(out, lhsT=lhsT, rhs=rhs, start=start, stop=stop), reads=reads, writes=writes)


def e_tr(B, out, in_, ident, reads, writes):
    B.P.op('tensor', lambda e: e.transpose(out, in_, ident), reads=reads, writes=writes)


def e_tt(B, out, in0, in1, op, reads, writes, eng='vector'):
    B.P.op(eng, lambda e: e.tensor_tensor(out=out, in0=in0, in1=in1, op=op), reads=reads, writes=writes)


def e_ts(B, out, in0, s1, s2, op0, op1, reads, writes, eng='vector'):
    if s2 is None:
        B.P.op(eng, lambda e: e.tensor_scalar(out=out, in0=in0, scalar1=s1, scalar2=None, op0=op0), reads=reads, writes=writes)
    else:
        B.P.op(eng, lambda e: e.tensor_scalar(out=out, in0=in0, scalar1=s1, scalar2=s2, op0=op0, op1=op1), reads=reads, writes=writes)


def e_stt(B, out, in0, scalar, in1, op0, op1, reads, writes):
    B.P.op('vector', lambda e: e.scalar_tensor_tensor(out=out, in0=in0, scalar=scalar, in1=in1, op0=op0, op1=op1), reads=reads, writes=writes)


def e_scan(B, out, d0, d1, reads, writes, initial=0.0):
    B.P.op('vector', lambda e: e.tensor_tensor_scan(out=out, data0=d0, data1=d1, initial=initial, op0=ALU.mult, op1=ALU.add), reads=reads, writes=writes)


def e_memset(B, ap, val, writes, eng='vector'):
    B.P.op(eng, lambda e: e.memset(ap, val), writes=writes)


def e_copy(B, out, in_, reads, writes, eng='vector'):
    B.P.op(eng, lambda e: e.tensor_copy(out=out, in_=in_), reads=reads, writes=writes)


def mixer_lru(B, l, g, d):
    P, T = B.P, B.T
    pre = 'L%d_' % l
    X0, G0 = 2064, 2576
    y = B.rx([4, T]); y_tk = Tk()
    u, u_tk = alloc_u(B)
    xc = B.rx([T])[:, 0, :]; xc_tk = Tk()
    xcb = B.rx([T], BF16)[:, 0, :]; xcb_tk = Tk()
    gg = B.rx([T])[:, 0, :]; gg_tk = Tk()
    bA = B.rx([T])[:, 0, :]; bA_tk = Tk()
    bB = B.rx([T])[:, 0, :]; bB_tk = Tk()
    bC = B.rx([T])[:, 0, :]; bC_tk = Tk()
    lw = B.rx([16, 128], BF16); lw_tk = Tk()
    sp = B.rx([16])[:, 0, :]; sp_tk = Tk()
    hin = B.rx([4])[:, 0, :]; hin_tk = Tk()
    st, st_tk = B.wst[B.wi % 2], B.wst_tk[B.wi % 2]
    B.wi += 1
    P.dma('sync', st[:, :, :], B.lru_w[l].rearrange("p (k n) -> p k n", n=128), writes=[st_tk], dtk=st_tk)
    e_copy(B, lw, st[:, :, :], [st_tk], [lw_tk], eng='gpsimd')
    e_act(B, sp[:, 0:8], B.pc(pre + 'lru_lam', 0, 8), AF.Exp, [B.c_tk], [sp_tk], scale=-1.0)
    e_act(B, sp[:, 0:8], sp[:, 0:8], AF.Ln, [sp_tk], [sp_tk], bias=1.0)
    e_ts(B, sp[:, 8:16], sp[:, 0:8], -16.0, None, ALU.mult, None, [sp_tk], [sp_tk])
    e_ts(B, sp[:, 0:8], sp[:, 0:8], -8.0, None, ALU.mult, None, [sp_tk], [sp_tk])
    first = is_first(B, d)
    if not first:
        e_ts(B, hin, B.st_lru[:, 0:4], B.carry, None, ALU.mult, None, [B.st_lru_tk, B.c_tk], [hin_tk])
    specs = []
    for n in range(4):
        specs.append(win_spec(B, l, X0 + n * 128, 128))
        if d == 1:
            specs.append(win_spec(B, l, G0 + n * 128, 128))
    B.w_begin(specs)
    for n in range(4):
        inproj_raw(B, 128, u, u_tk)
        conv4(B, u, u_tk, xc, xc_tk, pre + 'lru_conv', 4, n, B.pc(pre + 'lru_convb', n), xcb, xcb_tk, AF.Copy)
        if d == 1:
            inproj_act(B, 128, gg, gg_tk, AF.Gelu)
            yf_load(B, g * 4 + n, y[:, n, :], y_tk)
        for which, dst, dst_tk, bname in ((0, bA, bA_tk, 'lru_ba'), (1, bC, bC_tk, 'lru_bi')):
            for tb in range(B.NB):
                bank = 4 + (tb % 2) + 2 * which
                sl = slice(tb * 512, (tb + 1) * 512)
                e_mm(B, B.ps[:, bank, :], lw[:, which * 8 + d * 4 + n, :], xcb[:, sl], True, True, [lw_tk, xcb_tk], [B.ps_tk[bank]])
                e_act(B, dst[:, sl], B.ps[:, bank, :], AF.Sigmoid, [B.ps_tk[bank], B.c_tk], [dst_tk], bias=B.pc(pre + bname, d * 4 + n))
        k = d * 4 + n
        e_act(B, bB, bA, AF.Exp, [bA_tk, sp_tk], [bB_tk], scale=sp[:, k:k + 1])
        e_act(B, bA, bA, AF.Exp, [bA_tk, sp_tk], [bA_tk], scale=sp[:, 8 + k:9 + k])
        e_act(B, bA, bA, AF.Sqrt, [bA_tk], [bA_tk], scale=-1.0, bias=1.0)
        fp = 0 if d == 0 else T - 1
        if first:
            e_memset(B, bA[:, fp:fp + 1], 1.0, [bA_tk])
        else:
            e_ts(B, bA[:, fp:fp + 1], bA[:, fp:fp + 1], B.carry, B.ncarry, ALU.mult, ALU.add, [bA_tk, B.c_tk], [bA_tk])
        e_tt(B, bC, bC, xc, ALU.mult, [bC_tk, xc_tk], [bC_tk])
        e_tt(B, bC, bC, bA, ALU.mult, [bC_tk, bA_tk], [bC_tk])
        init = 0.0 if first else hin[:, n:n + 1]
        rd = [bB_tk, bC_tk] + ([] if first else [hin_tk])
        if d == 0:
            e_scan(B, bA, bB, bC, rd, [bA_tk], initial=init)
            e_copy(B, B.st_lru[:, n:n + 1], bA[:, T - 1:T], [bA_tk], [B.st_lru_tk])
            yf_store(B, g * 4 + n, bA, bA_tk)
        else:
            e_scan(B, bA[:, ::-1], bB[:, ::-1], bC[:, ::-1], rd, [bA_tk], initial=init)
            e_copy(B, B.st_lru[:, n:n + 1], bA[:, 0:1], [bA_tk], [B.st_lru_tk])
            e_tt(B, y[:, n, :], y[:, n, :], bA, ALU.add, [bA_tk, y_tk], [y_tk])
            e_tt(B, y[:, n, :], y[:, n, :], gg, ALU.mult, [gg_tk, y_tk], [y_tk])
    if d == 1:
        group_out(B, l, g, y, y_tk)


def mixer_gdn(B, l, g, d):
    P, T, NB, NT = B.P, B.T, B.NB, B.NT
    pre = 'L%d_' % l
    ybf = B.rx([4, T], BF16); ybf_tk = Tk()
    specs = [win_spec(B, l, 2048, 16)]
    for h in range(4):
        specs += [win_spec(B, l, h * 128, 128), win_spec(B, l, 512 + h * 128, 128), win_spec(B, l, 1024 + h * 128, 128)]
        if d == 1:
            specs += [win_spec(B, l, 1536 + h * 128, 128)]
    B.w_begin(specs)
    nr, W = 16, NT * 16
    def buf():
        return B.rx([NT, nr])
    RAW, ORD, ACS, ACSL = buf(), buf(), buf(), buf()
    abc = B.rx([8])[:, 0, :]
    stk = Tk()
    flat = lambda a: a.rearrange("p a b -> p (a b)")
    wb, wb_tk = B.w_get()
    for i in range(NT):
        for kc in range(KC):
            e_mm(B, B.ps[:, 4, i * nr:(i + 1) * nr], B.hT[:, kc, i * 128:(i + 1) * 128], wb[:, kc, 0:nr], kc == 0, kc == KC - 1,
                 [B.hT_tk[i // 4], wb_tk], [B.ps_tk[4]])
    B.w_prefetch()
    ps3 = B.ps[:, 4, 0:W].rearrange("p (a b) -> p a b", b=nr)
    e_act(B, RAW[:, :, 0:8], ps3[:, :, 0:8], AF.Sigmoid, [B.ps_tk[4]], [stk])
    e_tt(B, RAW[:, :, 8:16], ps3[:, :, 8:16], B.pc(pre + 'gdn_dtb_bc', 0, 8).rearrange("p (a b) -> p a b", a=1).to_broadcast([128, NT, 8]),
         ALU.add, [B.ps_tk[4], B.c_tk], [stk])
    e_act(B, RAW[:, :, 8:16], RAW[:, :, 8:16], AF.Exp, [stk], [stk])
    e_act(B, RAW[:, :, 8:16], RAW[:, :, 8:16], AF.Ln, [stk], [stk], bias=1.0)
    e_act(B, abc, B.pc(pre + 'gdn_alog_bc', 0, 8), AF.Exp, [B.c_tk], [stk])
    e_ts(B, abc, abc, -1.0, None, ALU.mult, None, [stk], [stk])
    e_tt(B, RAW[:, :, 8:16], RAW[:, :, 8:16], abc.rearrange("p (a b) -> p a b", a=1).to_broadcast([128, NT, 8]), ALU.mult, [stk], [stk])
    e_mm(B, B.ps[:, 5, 0:W], B.J_f, flat(RAW), True, True, [stk, B.c_tk], [B.ps_tk[5]])
    pj3 = B.ps[:, 5, 0:W].rearrange("p (a b) -> p a b", b=nr)
    for c0 in (0, 8):
        e_copy(B, ORD[:, :, c0:c0 + 4], RAW[:, :, c0:c0 + 4], [stk], [stk])
        e_copy(B, ORD[:, :, c0 + 4:c0 + 8], pj3[:, ::-1, c0 + 4:c0 + 8], [B.ps_tk[5], stk], [stk])
    e_mm(B, B.ps[:, 4, 0:W], B.triu_f, flat(ORD), True, True, [stk, B.c_tk], [B.ps_tk[4]])
    e_copy(B, flat(ACS), B.ps[:, 4, 0:W], [B.ps_tk[4]], [stk])
    e_mm(B, B.ps[:, 5, 0:W], B.ones_f, flat(ORD), True, True, [stk, B.c_tk], [B.ps_tk[5]])
    e_copy(B, flat(ACSL), B.ps[:, 5, 0:W], [B.ps_tk[5]], [stk])
    BETA = ORD[:, :, 0:8]
    GC = ACS[:, :, 8:16]
    EG = RAW[:, :, 0:8]; EGL = RAW[:, :, 8:16]; GLt = ACSL[:, :, 0:8]
    e_act(B, EG, GC, AF.Exp, [stk], [stk])
    e_tt(B, EGL, ACSL[:, :, 8:16], GC, ALU.subtract, [stk], [stk])
    e_act(B, EGL, EGL, AF.Exp, [stk], [stk])
    e_act(B, GLt, ACSL[:, :, 8:16], AF.Exp, [stk], [stk])
    qf = B.rx([T], BF16)[:, 0, :]; qf_tk = Tk()
    kf = B.rx([T], BF16)[:, 0, :]; kf_tk = Tk()
    vf = B.rx([T], BF16)[:, 0, :]; vf_tk = Tk()
    zs = B.rx([T], BF16)[:, 0, :]; zs_tk = Tk()
    y = B.rx([T])[:, 0, :]; y_tk = Tk()
    u, u_tk = alloc_u(B)
    cc = B.rx([T])[:, 0, :]; cc_tk = Tk()
    OB = B.rx([3, 512], BF16); OB_tk = Tk()
    def t16(n=1):
        a = B.rx([n, 128], BF16)
        return a, Tk()
    tok3, tok3_tk = t16(3)
    sc = B.rx([8])[:, 0, :]; sc_tk = Tk()
    junk = B.rx([128], BF16)[:, 0, :]; junk_tk = Tk()
    tkv, tkv_tk = t16(6)
    vb_, vb_tk = t16(1)
    fm4, fm4_tk = t16(4)
    RBf = B.rx([128])[:, 0, :]
    Dm = B.rx([128])[:, 0, :]; Dm_tk = Tk()
    EE = B.rx([128])[:, 0, :]; EE_tk = Tk()
    Es = B.rx([128])[:, 0, :]; Es_tk = Tk()
    Et = B.rx([128])[:, 0, :]; Et_tk = Tk()
    Nf = B.rx([2, 128]); Nf_tk = Tk()
    Rr = B.rx([128])[:, 0, :]; Rr_tk = Tk()
    Rb, Rb_tk = t16(1)
    qkT, qkT_tk = t16(1)
    wT, wT_tk = t16(1)
    vn, vn_tk = t16(1)
    Sb, Sb_tk = t16(1)
    sq = B.rx([512], BF16)[:, 0, :]; sq_tk = Tk()
    rstd = B.rx([512])[:, 0, :]; rstd_tk = Tk()
    TRk, RBk, Ak, Bk, Ck, Dk = 0, 1, 2, 3, 6, 7
    ptr = psbf(B, TRk)
    for h in range(4):
        for which, dst, dst_tk in ((0, qf, qf_tk), (1, kf, kf_tk), (2, vf, vf_tk)):
            inproj_raw(B, 128, u, u_tk)
            conv4(B, u, u_tk, cc, cc_tk, pre + 'gdn_conv', 12, which * 4 + h, None, dst, dst_tk, AF.Silu)
        if d == 1:
            inproj_act(B, 128, zs, zs_tk, AF.Silu)
            yf_load(B, g * 4 + h, y, y_tk)
        S = B.st_gdn[:, h, :]
        S_tk = B.st_gdn_tk[h]
        for d in (d,):
            r = d * 4 + h
            state_init(B, d, S, S_tk)
            e_act(B, Sb[:, 0, :], S, AF.Copy, [S_tk], [Sb_tk])
            for b in range(NB):
                if d == 0:
                    sl = slice(b * 512, (b + 1) * 512)
                    srcs = [kf[:, sl], qf[:, sl], vf[:, sl]]
                else:
                    sl = slice(T - (b + 1) * 512, T - b * 512)
                    srcs = [kf[:, sl][:, ::-1], qf[:, sl][:, ::-1], vf[:, sl][:, ::-1]]
                for k, (src, t_) in enumerate(zip(srcs, (kf_tk, qf_tk, vf_tk))):
                    e_copy(B, OB[:, k, :], src, [t_], [OB_tk], eng=('vector' if k != 1 else 'gpsimd'))
                for i in range(4):
                    ti = b * 4 + i
                    tsl = slice(i * 128, (i + 1) * 128)
                    col = lambda A_: A_[:, ti, r:r + 1]
                    for k in range(3):
                        e_tr(B, ptr[:, k * 128:(k + 1) * 128], OB[:, k, tsl], B.ident_b, [OB_tk, B.c_tk], [B.ps_tk[TRk]])
                    e_copy(B, tok3.rearrange("p a b -> p (a b)"), ptr[:, 0:384], [B.ps_tk[TRk]], [tok3_tk])
                    for k in range(2):
                        e_act(B, junk, tok3[:, k, :], AF.Square, [tok3_tk], [junk_tk, sc_tk], accum_out=sc[:, k:k + 1])
                    e_act(B, sc[:, 0:2], sc[:, 0:2], AF.Sqrt, [sc_tk], [sc_tk], bias=EPS)
                    B.P.op('vector', lambda e: e.reciprocal(out=sc[:, 0:2], in_=sc[:, 0:2]), reads=[sc_tk], writes=[sc_tk])
                    e_tt(B, sc[:, 2:3], sc[:, 0:1], col(BETA), ALU.mult, [sc_tk, stk], [sc_tk])
                    e_tt(B, sc[:, 3:4], sc[:, 2:3], col(EG), ALU.mult, [sc_tk, stk], [sc_tk])
                    e_tt(B, sc[:, 4:5], sc[:, 0:1], col(EGL), ALU.mult, [sc_tk, stk], [sc_tk])
                    e_ts(B, sc[:, 5:6], sc[:, 1:2], 128.0 ** -0.5, None, ALU.mult, None, [sc_tk], [sc_tk])
                    e_tt(B, sc[:, 6:7], sc[:, 5:6], col(EG), ALU.mult, [sc_tk, stk], [sc_tk])
                    for j, (srcj, scj) in enumerate(((0, 0), (0, 2), (0, 3), (0, 4), (1, 5), (1, 6))):
                        e_act(B, tkv[:, j, :], tok3[:, srcj, :], AF.Copy, [tok3_tk, sc_tk], [tkv_tk], scale=sc[:, scj:scj + 1],
                              eng='scalar')
                    e_act(B, vb_[:, 0, :], tok3[:, 2, :], AF.Copy, [tok3_tk, stk], [vb_tk], scale=col(BETA))
                    for j, srcj in enumerate((0, 1, 4, 5)):
                        e_tr(B, ptr[:, j * 128:(j + 1) * 128], tkv[:, srcj, :], B.ident_b, [tkv_tk, B.c_tk], [B.ps_tk[TRk]])
                    e_copy(B, fm4.rearrange("p a b -> p (a b)"), ptr[:, 0:512], [B.ps_tk[TRk]], [fm4_tk])
                    knT, kbT, qnT, qdT = fm4[:, 0, :], fm4[:, 1, :], fm4[:, 2, :], fm4[:, 3, :]
                    gcs = col(GC)
                    e_mm(B, B.ps[:, RBk, 0:128], gcs.to_broadcast([128, 128]), B.ident_f, True, True, [stk, B.c_tk], [B.ps_tk[RBk]])
                    e_ts(B, Dm, B.ps[:, RBk, 0:128], gcs, 0.0, ALU.subtract, ALU.min, [B.ps_tk[RBk], stk], [Dm_tk])
                    e_act(B, EE, Dm, AF.Exp, [Dm_tk], [EE_tk])
                    e_tt(B, Es, EE, B.striu_f, ALU.mult, [EE_tk, B.c_tk], [Es_tk])
                    e_tt(B, Et, EE, B.triu_f, ALU.mult, [EE_tk, B.c_tk], [Et_tk], eng='gpsimd')
                    e_mm(B, B.ps[:, Ak, 0:128], knT, kbT, True, True, [fm4_tk], [B.ps_tk[Ak]])
                    e_tt(B, Nf[:, 0, :], B.ps[:, Ak, 0:128], Es, ALU.mult, [B.ps_tk[Ak], Es_tk], [Nf_tk])
                    e_mm(B, B.ps[:, Bk, 0:128], knT, qnT, True, True, [fm4_tk], [B.ps_tk[Bk]])
                    e_tt(B, qkT[:, 0, :], B.ps[:, Bk, 0:128], Et, ALU.mult, [B.ps_tk[Bk], Et_tk], [qkT_tk])
                    e_tr(B, B.ps[:, Ck, 0:128], Nf[:, 0, :], B.ident_f, [Nf_tk, B.c_tk], [B.ps_tk[Ck]])
                    e_copy(B, Nf[:, 1, :], B.ps[:, Ck, 0:128], [B.ps_tk[Ck]], [Nf_tk])
                    e_tt(B, Rr, B.ident_f, Nf[:, 0, :], ALU.subtract, [Nf_tk, B.c_tk], [Rr_tk])
                    for lev in range(6):
                        e_mm(B, B.ps[:, Ak, 0:128], Nf[:, 1, :], Nf[:, 0, :], True, True, [Nf_tk], [B.ps_tk[Ak]])
                        e_mm(B, B.ps[:, Bk, 0:128], Nf[:, 0, :], Nf[:, 1, :], True, True, [Nf_tk], [B.ps_tk[Bk]])
                        e_copy(B, Nf[:, 0, :], B.ps[:, Ak, 0:128], [B.ps_tk[Ak]], [Nf_tk])
                        e_act(B, Nf[:, 1, :], B.ps[:, Bk, 0:128], AF.Copy, [B.ps_tk[Bk]], [Nf_tk])
                        e_mm(B, B.ps[:, Ck, 0:128], Nf[:, 1, :], Rr, True, True, [Nf_tk, Rr_tk], [B.ps_tk[Ck]])
                        e_tt(B, Rr, Rr, B.ps[:, Ck, 0:128], ALU.add, [Rr_tk, B.ps_tk[Ck]], [Rr_tk])
                    e_copy(B, Rb[:, 0, :], Rr, [Rr_tk], [Rb_tk], eng='gpsimd')
                    X = Rb[:, 0, :]
                    e_mm(B, B.ps[:, Ak, 0:128], tkv[:, 2, :], X, True, True, [tkv_tk, Rb_tk], [B.ps_tk[Ak]])
                    e_act(B, wT[:, 0, :], B.ps[:, Ak, 0:128], AF.Copy, [B.ps_tk[Ak]], [wT_tk], scale=-1.0)
                    e_mm(B, B.ps[:, Bk, 0:128], X, vb_[:, 0, :], True, False, [Rb_tk, vb_tk], [B.ps_tk[Bk]])
                    e_mm(B, B.ps[:, Bk, 0:128], wT[:, 0, :], Sb[:, 0, :], False, True, [wT_tk, Sb_tk], [B.ps_tk[Bk]])
                    e_copy(B, vn[:, 0, :], B.ps[:, Bk, 0:128], [B.ps_tk[Bk]], [vn_tk])
                    e_mm(B, B.ps[:, Dk, 0:128], Sb[:, 0, :], qdT, True, False, [Sb_tk, fm4_tk], [B.ps_tk[Dk]])
                    e_mm(B, B.ps[:, Dk, 0:128], vn[:, 0, :], qkT[:, 0, :], False, True, [vn_tk, qkT_tk], [B.ps_tk[Dk]])
                    e_mm(B, B.ps[:, Ck, 0:128], tkv[:, 3, :], vn[:, 0, :], True, True, [tkv_tk, vn_tk], [B.ps_tk[Ck]])
                    e_stt(B, S, S, col(GLt), B.ps[:, Ck, 0:128], ALU.mult, ALU.add, [S_tk, stk, B.ps_tk[Ck]], [S_tk])
                    e_act(B, Sb[:, 0, :], S, AF.Copy, [S_tk], [Sb_tk])
                    t0 = ti * 128
                    if d == 0:
                        e_act(B, y[:, t0:t0 + 128], B.ps[:, Dk, 0:128], AF.Copy, [B.ps_tk[Dk]], [y_tk])
                    else:
                        yv = y[:, T - t0 - 128:T - t0][:, ::-1]
                        e_tt(B, yv, yv, B.ps[:, Dk, 0:128], ALU.add, [B.ps_tk[Dk], y_tk], [y_tk])
        if d == 0:
            yf_store(B, g * 4 + h, y, y_tk)
            continue
        for tb in range(NB):
            sl = slice(tb * 512, (tb + 1) * 512)
            bank = 4 + tb % 2
            e_act(B, sq, y[:, sl], AF.Square, [y_tk], [sq_tk])
            e_mm(B, B.ps[:, bank, :], B.ones_b, sq, True, True, [sq_tk, B.c_tk], [B.ps_tk[bank]])
            e_act(B, rstd, B.ps[:, bank, :], AF.Sqrt, [B.ps_tk[bank]], [rstd_tk], scale=1.0 / 128, bias=EPS)
            B.P.op('vector', lambda e: e.reciprocal(out=rstd, in_=rstd), reads=[rstd_tk], writes=[rstd_tk])
            e_stt(B, y[:, sl], y[:, sl], B.pc(pre + 'gdn_norm', 0), rstd, ALU.mult, ALU.mult, [y_tk, rstd_tk, B.c_tk], [y_tk])
            e_tt(B, ybf[:, h, sl], y[:, sl], zs[:, sl], ALU.mult, [y_tk, zs_tk], [ybf_tk])
    if d == 1:
        group_out(B, l, g, ybf, ybf_tk)


def tok_scalars(B, l, pre, col0, nr, dtb_name, alog_name, pfx):
    P, T, NT = B.P, B.T, B.NT
    nh = nr // 2
    W = NT * nr
    def buf():
        return B.rx([NT, nr])
    RAW, DTO, DA, ACS, ACSL = buf(), buf(), buf(), buf(), buf()
    abc = B.rx([nr])[:, 0, :]
    tk = Tk()
    wb, wb_tk = B.w_get()
    bank = 4
    for i in range(NT):
        for kc in range(KC):
            e_mm(B, B.ps[:, bank, i * nr:(i + 1) * nr], B.hT[:, kc, i * 128:(i + 1) * 128], wb[:, kc, 0:nr], kc == 0, kc == KC - 1,
                 [B.hT_tk[i // 4], wb_tk], [B.ps_tk[bank]])
    B.w_prefetch()
    flat = lambda a: a.rearrange("p a b -> p (a b)")
    e_tt(B, RAW, B.ps[:, bank, 0:W].rearrange("p (a b) -> p a b", b=nr), B.pc(pre + dtb_name, 0, nr).rearrange("p (a b) -> p a b", a=1).to_broadcast([128, NT, nr]),
         ALU.add, [B.ps_tk[bank], B.c_tk], [tk])
    e_act(B, flat(RAW), flat(RAW), AF.Exp, [tk], [tk])
    e_act(B, flat(RAW), flat(RAW), AF.Ln, [tk], [tk], bias=1.0)
    e_mm(B, B.ps[:, bank + 1, 0:W], B.J_f, flat(RAW), True, True, [tk, B.c_tk], [B.ps_tk[bank + 1]])
    e_copy(B, DTO[:, :, 0:nh], RAW[:, :, 0:nh], [tk], [tk])
    e_copy(B, DTO[:, :, nh:nr], B.ps[:, bank + 1, 0:W].rearrange("p (a b) -> p a b", b=nr)[:, ::-1, nh:nr], [B.ps_tk[bank + 1], tk], [tk])
    e_act(B, abc, B.pc(pre + alog_name, 0, nr), AF.Exp, [B.c_tk], [tk])
    e_ts(B, abc, abc, -1.0, None, ALU.mult, None, [tk], [tk])
    e_tt(B, DA, DTO, abc.rearrange("p (a b) -> p a b", a=1).to_broadcast([128, NT, nr]), ALU.mult, [tk], [tk])
    e_mm(B, B.ps[:, bank, 0:W], B.triu_f, flat(DA), True, True, [tk, B.c_tk], [B.ps_tk[bank]])
    e_copy(B, flat(ACS), B.ps[:, bank, 0:W], [B.ps_tk[bank]], [tk])
    e_mm(B, B.ps[:, bank + 1, 0:W], B.ones_f, flat(DA), True, True, [tk, B.c_tk], [B.ps_tk[bank + 1]])
    e_copy(B, flat(ACSL), B.ps[:, bank + 1, 0:W], [B.ps_tk[bank + 1]], [tk])
    return dict(DT=DTO, DA=DA, ACS=ACS, ACSL=ACSL, RAW=RAW, tk=tk)


def mixer_ssd(B, l, g, d):
    P, T, NB, NT = B.P, B.T, B.NB, B.NT
    pre = 'L%d_' % l
    Z0, X0, B0, C0, DT0 = 3088, 3600, 4112, 4368, 4624
    ybf = B.rx([4, T], BF16); ybf_tk = Tk()
    specs = [win_spec(B, l, DT0, 16)]
    for grp in range(2):
        specs += [win_spec(B, l, X0 + (2 * grp) * 128, 128), win_spec(B, l, X0 + (2 * grp + 1) * 128, 128),
                  win_spec(B, l, B0 + grp * 128, 128), win_spec(B, l, C0 + grp * 128, 128)]
        if d == 1:
            specs += [win_spec(B, l, Z0 + (2 * grp) * 128, 128), win_spec(B, l, Z0 + (2 * grp + 1) * 128, 128)]
    B.w_begin(specs)
    ts_ = tok_scalars(B, l, pre, DT0, 16, 'ssd_dtb_bc', 'ssd_alog_bc', 'ssd')
    DT, ACS, ACSL, stk = ts_['DT'], ts_['ACS'], ts_['ACSL'], ts_['tk']
    GLt = ts_['DA']
    Wt = ts_['RAW']
    flat = lambda a: a.rearrange("p a b -> p (a b)")
    e_tt(B, Wt, ACSL, ACS, ALU.subtract, [stk], [stk])
    e_act(B, flat(Wt), flat(Wt), AF.Exp, [stk], [stk])
    e_tt(B, Wt, Wt, DT, ALU.mult, [stk], [stk])
    e_act(B, flat(GLt), flat(ACSL), AF.Exp, [stk], [stk])
    y = B.rx([2, T]); y_tk = Tk()
    xsT = B.rx([2, T], BF16); xs_tk = Tk()
    BT = B.rx([T], BF16)[:, 0, :]; BT_tk = Tk()
    CT = B.rx([T], BF16)[:, 0, :]; CT_tk = Tk()
    u, u_tk = alloc_u(B)
    cc = B.rx([T])[:, 0, :]; cc_tk = Tk()
    OB = B.rx([4, 512], BF16); OB_tk = Tk()
    xtok = B.rx([4, 256], BF16); xtok_tk = Tk()
    btok = B.rx([4, 128], BF16); btok_tk = Tk()
    GmT = B.rx([128])[:, 0, :]; GmT_tk = Tk()
    Dm = B.rx([128])[:, 0, :]; Dm_tk = Tk()
    EE = B.rx([128])[:, 0, :]; EE_tk = Tk()
    MT = B.rx([128], BF16)[:, 0, :]; MT_tk = Tk()
    E2 = B.rx([128])[:, 0, :]; E2_tk = Tk()
    Cd = B.rx([128], BF16)[:, 0, :]; Cd_tk = Tk()
    xdt = B.rx([64], BF16)[:, 0, :]; xdt_tk = Tk()
    xw = B.rx([64], BF16)[:, 0, :]; xw_tk = Tk()
    STb = B.rx([4, 64], BF16); STb_tk = [Tk() for _ in range(4)]
    zs = B.rx([T], BF16)[:, 0, :]; zs_tk = Tk()
    sq = B.rx([512], BF16)[:, 0, :]; sq_tk = Tk()
    rstd = B.rx([512])[:, 0, :]; rstd_tk = Tk()
    RBk, GBk, YBk, SUk, TXk, TBk = 0, 1, 2, 3, 6, 7
    for grp in range(2):
        for cl in range(2):
            ch = 2 * grp + cl
            inproj_raw(B, 128, u, u_tk)
            if d == 0:
                conv4(B, u, u_tk, cc, cc_tk, pre + 'ssd_conv', 8, ch, B.pc(pre + 'ssd_convb', ch), y[:, cl, :], y_tk, AF.Silu)
                e_copy(B, xsT[:, cl, :], y[:, cl, :], [y_tk], [xs_tk], eng='gpsimd')
                e_ts(B, y[:, cl, :], y[:, cl, :], B.pc(pre + 'ssd_d', ch), None, ALU.mult, None, [y_tk, xs_tk, B.c_tk], [y_tk])
            else:
                conv4(B, u, u_tk, cc, cc_tk, pre + 'ssd_conv', 8, ch, B.pc(pre + 'ssd_convb', ch), xsT[:, cl, :], xs_tk, AF.Silu)
                yf_load(B, g * 4 + ch, y[:, cl, :], y_tk)
        inproj_raw(B, 128, u, u_tk)
        conv4(B, u, u_tk, cc, cc_tk, pre + 'ssd_conv', 8, 4 + grp, B.pc(pre + 'ssd_convb', 4 + grp), BT, BT_tk, AF.Silu)
        inproj_raw(B, 128, u, u_tk)
        conv4(B, u, u_tk, cc, cc_tk, pre + 'ssd_conv', 8, 6 + grp, B.pc(pre + 'ssd_convb', 6 + grp), CT, CT_tk, AF.Silu)
        ST = B.st_ssd[:, grp * 4:(grp + 1) * 4, :]
        ST_tk = B.st_ssd_tk[grp * 4:(grp + 1) * 4]
        for d in (d,):
            for hh in range(4):
                state_init(B, d, ST[:, hh, :], ST_tk[hh])
                e_act(B, STb[:, hh, :], ST[:, hh, :], AF.Copy, [ST_tk[hh]], [STb_tk[hh]])
            for b in range(NB):
                if d == 0:
                    sl = slice(b * 512, (b + 1) * 512)
                    srcs = [xsT[:, 0, sl], xsT[:, 1, sl], BT[:, sl], CT[:, sl]]
                else:
                    sl = slice(T - (b + 1) * 512, T - b * 512)
                    srcs = [xsT[:, 0, sl][:, ::-1], xsT[:, 1, sl][:, ::-1], BT[:, sl][:, ::-1], CT[:, sl][:, ::-1]]
                for k, (src, stk_) in enumerate(zip(srcs, (xs_tk, xs_tk, BT_tk, CT_tk))):
                    e_copy(B, OB[:, k, :], src, [stk_], [OB_tk], eng=('vector' if k % 2 == 0 else 'gpsimd'))
                px, pb = psbf(B, TXk), psbf(B, TBk)
                for i in range(4):
                    for cl in range(2):
                        e_tr(B, px[:, (i * 2 + cl) * 128:(i * 2 + cl + 1) * 128], OB[:, cl, i * 128:(i + 1) * 128], B.ident_b, [OB_tk, B.c_tk], [B.ps_tk[TXk]])
                    e_tr(B, pb[:, i * 128:(i + 1) * 128], OB[:, 2, i * 128:(i + 1) * 128], B.ident_b, [OB_tk, B.c_tk], [B.ps_tk[TBk]])
                e_copy(B, xtok.rearrange("p a b -> p (a b)"), px[:, 0:1024], [B.ps_tk[TXk]], [xtok_tk])
                e_act(B, btok.rearrange("p a b -> p (a b)"), pb[:, 0:512], AF.Copy, [B.ps_tk[TBk]], [btok_tk])
                for i in range(4):
                    ti = b * 4 + i
                    tsl = slice(i * 128, (i + 1) * 128)
                    e_mm(B, B.ps[:, GBk, 0:128], OB[:, 2, tsl], OB[:, 3, tsl], True, True, [OB_tk], [B.ps_tk[GBk]])
                    e_tt(B, GmT, B.ps[:, GBk, 0:128], B.triu_f, ALU.mult, [B.ps_tk[GBk], B.c_tk], [GmT_tk])
                    for hh in range(4):
                        r = d * 8 + grp * 4 + hh
                        cl = hh // 2
                        po = (hh % 2) * 64
                        acs = ACS[:, ti, r:r + 1]
                        e_mm(B, B.ps[:, RBk, 0:128], acs.to_broadcast([128, 128]), B.ident_f, True, True, [stk, B.c_tk], [B.ps_tk[RBk]])
                        e_ts(B, Dm, B.ps[:, RBk, 0:128], acs, 0.0, ALU.subtract, ALU.min, [B.ps_tk[RBk], stk], [Dm_tk])
                        e_act(B, EE, Dm, AF.Exp, [Dm_tk], [EE_tk])
                        e_tt(B, MT, EE, GmT, ALU.mult, [EE_tk, GmT_tk], [MT_tk])
                        e_act(B, E2, B.ps[:, RBk, 0:128], AF.Exp, [B.ps_tk[RBk]], [E2_tk])
                        e_tt(B, Cd, OB[:, 3, tsl], E2, ALU.mult, [OB_tk, E2_tk], [Cd_tk], eng='gpsimd')
                        xs_h = xtok[:, i, cl * 128 + po:cl * 128 + po + 64]
                        e_act(B, xdt, xs_h, AF.Copy, [xtok_tk, stk], [xdt_tk], scale=DT[:, ti, r:r + 1])
                        e_act(B, xw, xs_h, AF.Copy, [xtok_tk, stk], [xw_tk], scale=Wt[:, ti, r:r + 1])
                        yo = B.ps[po:po + 64, YBk, cl * 128:(cl + 1) * 128]
                        e_mm(B, yo, xdt, MT, True, False, [xdt_tk, MT_tk], [B.ps_tk[YBk]])
                        e_mm(B, yo, STb[:, hh, :], Cd, False, True, [STb_tk[hh], Cd_tk], [B.ps_tk[YBk]])
                        e_mm(B, B.ps[:, SUk, 0:64], btok[:, i, :], xw, True, True, [btok_tk, xw_tk], [B.ps_tk[SUk]])
                        e_stt(B, ST[:, hh, :], ST[:, hh, :], GLt[:, ti, r:r + 1], B.ps[:, SUk, 0:64], ALU.mult, ALU.add, [ST_tk[hh], stk, B.ps_tk[SUk]], [ST_tk[hh]])
                        e_act(B, STb[:, hh, :], ST[:, hh, :], AF.Copy, [ST_tk[hh]], [STb_tk[hh]])
                    for cl in range(2):
                        t0 = ti * 128
                        if d == 0:
                            yv = y[:, cl, t0:t0 + 128]
                        else:
                            yv = y[:, cl, T - t0 - 128:T - t0][:, ::-1]
                        e_tt(B, yv, yv, B.ps[:, YBk, cl * 128:(cl + 1) * 128], ALU.add, [B.ps_tk[YBk], y_tk], [y_tk])
        if d == 0:
            for cl in range(2):
                yf_store(B, g * 4 + 2 * grp + cl, y[:, cl, :], y_tk)
            continue
        for cl in range(2):
            inproj_act(B, 128, zs, zs_tk, AF.Silu)
            e_tt(B, y[:, cl, :], y[:, cl, :], zs, ALU.mult, [y_tk, zs_tk], [y_tk])
        for tb in range(NB):
            sl = slice(tb * 512, (tb + 1) * 512)
            bank = 4 + tb % 2
            for cl in range(2):
                e_act(B, sq, y[:, cl, sl], AF.Square, [y_tk], [sq_tk])
                e_mm(B, B.ps[:, bank, :], B.ones_b, sq, cl == 0, cl == 1, [sq_tk, B.c_tk], [B.ps_tk[bank]])
            e_act(B, rstd, B.ps[:, bank, :], AF.Sqrt, [B.ps_tk[bank]], [rstd_tk], scale=1.0 / 256, bias=EPS)
            B.P.op('vector', lambda e: e.reciprocal(out=rstd, in_=rstd), reads=[rstd_tk], writes=[rstd_tk])
            for cl in range(2):
                e_stt(B, ybf[:, 2 * grp + cl, sl], y[:, cl, sl], B.pc(pre + 'ssd_norm', 2 * grp + cl), rstd, ALU.mult, ALU.mult, [y_tk, rstd_tk, B.c_tk], [ybf_tk])
    if d == 1:
        group_out(B, l, g, ybf, ybf_tk)


def psbf(B, bank):
    return B.ps[:, bank, :].bitcast(BF16)


def mixer_hgrn(B, l, g, d):
    P, T, NB = B.P, B.T, B.NB
    pre = 'L%d_' % l
    Q0, F0c, I0, G0 = 4640, 5152, 6176, 6688
    C = 32
    NCB = 512 // C
    ybf = B.rx([4, T], BF16); ybf_tk = Tk()
    yh = B.rx([T])[:, 0, :]; yh_tk = Tk()
    qb = B.rx([T], BF16)[:, 0, :]; qb_tk = Tk()
    vb = B.rx([T], BF16)[:, 0, :]; vb_tk = Tk()
    qbr = B.rx([T], BF16)[:, 0, :]; qbr_tk = Tk()
    vbr = B.rx([T], BF16)[:, 0, :]; vbr_tk = Tk()
    sg = B.rx([T], BF16)[:, 0, :]; sg_tk = Tk()
    Fs = [B.rx([T])[:, 0, :] for _ in range(2)]; F_tk = [Tk() for _ in range(2)]
    LF = B.rx([512])[:, 0, :]; LF_tk = Tk()
    BC = B.rx([512])[:, 0, :]; BC_tk = Tk()
    KK = B.rx([512])[:, 0, :]; KK_tk = Tk()
    DF = B.rx([512])[:, 0, :]; DF_tk = Tk()
    EE = B.rx([512])[:, 0, :]; EE_tk = Tk()
    QT = B.rx([512], BF16)[:, 0, :]; QT_tk = Tk()
    KT = B.rx([512], BF16)[:, 0, :]; KT_tk = Tk()
    KD = B.rx([512], BF16)[:, 0, :]; KD_tk = Tk()
    QD = B.rx([512])[:, 0, :]; QD_tk = Tk()
    KDT = B.rx([8, 128], BF16); KDT_tk = Tk()
    VT = B.rx([8, 128], BF16); VT_tk = Tk()
    sTm = B.rx([128], BF16)[:, 0, :]; sTm_tk = Tk()
    BS = B.rx([NCB])[:, 0, :]; BS_tk = Tk()
    GL = B.rx([NCB])[:, 0, :]; GL_tk = Tk()
    ones = B.rx([512])[:, 0, :]; ones_tk = Tk()
    lbv = B.rx([16])[:, 0, :]; lb_tk = Tk()
    zb = B.rx([128], BF16)[:, 0, :]; zb_tk = Tk()
    sq = B.rx([512], BF16)[:, 0, :]; sq_tk = Tk()
    rstd = B.rx([512])[:, 0, :]; rstd_tk = Tk()
    e_memset(B, ones, 1.0, [ones_tk])
    e_memset(B, zb, 0.0, [zb_tk])
    if l == 0:
        e_memset(B, lbv[:, 0:8], 0.0, [lb_tk])
    else:
        e_tt(B, lbv[:, 0:8], B.pc(pre + 'hg_lb1', 0, 8), B.pc(pre + 'hg_lb0', 0, 8), ALU.subtract, [B.c_tk], [lb_tk])
        e_act(B, lbv[:, 0:8], lbv[:, 0:8], AF.Sigmoid, [lb_tk], [lb_tk])
    e_ts(B, lbv[:, 8:16], lbv[:, 0:8], -1.0, 1.0, ALU.mult, ALU.add, [lb_tk], [lb_tk])
    SB_, YB_ = 6, 7
    e_mm(B, B.ps[:, SB_, 0:128], zb, zb, True, True, [zb_tk], [B.ps_tk[SB_]])
    specs = []
    for hd in range(4):
        for c0 in ((Q0, I0, F0c) if d == 0 else (Q0, I0, G0, F0c + 512)):
            specs.append(win_spec(B, l, c0 + hd * 128, 128))
    B.w_begin(specs)

    def rev_sl(sl):
        return slice(T - sl.stop, T - sl.start)

    for hd in range(4):
        inproj_act(B, 128, qb, qb_tk, AF.Silu)
        inproj_act(B, 128, vb, vb_tk, AF.Copy)
        if d == 0:
            inproj_act(B, 128, Fs[0], F_tk[0], AF.Sigmoid)
        else:
            inproj_act(B, 128, sg, sg_tk, AF.Silu)
            def evac_r(tb, ps_ap, ps_tk):
                sl = rev_sl(slice(tb * 512, (tb + 1) * 512))
                e_act(B, Fs[1][:, sl][:, ::-1], ps_ap, AF.Sigmoid, [ps_tk], [F_tk[1]])
            inproj(B, 128, evac_r)
            e_copy(B, qbr, qb[:, ::-1], [qb_tk], [qbr_tk])
            e_copy(B, vbr, vb[:, ::-1], [vb_tk], [vbr_tk])
            yf_load(B, g * 4 + hd, yh, yh_tk)
        S = B.st_hg[:, hd, :]
        S_tk = B.st_hg_tk[hd]
        for d in (d,):
            k8 = d * 4 + hd
            F, Ftk = Fs[d], F_tk[d]
            e_ts(B, F, F, lbv[:, 8 + k8:9 + k8], lbv[:, k8:k8 + 1], ALU.mult, ALU.add, [Ftk, lb_tk], [Ftk])
            Q, Qtk = (qb, qb_tk) if d == 0 else (qbr, qbr_tk)
            V, Vtk = (vb, vb_tk) if d == 0 else (vbr, vbr_tk)
            state_init(B, d, S, S_tk)
            for b in range(NB):
                sl = slice(b * 512, (b + 1) * 512)
                e_act(B, LF, F[:, sl], AF.Ln, [Ftk], [LF_tk])
                e_ts(B, KK, F[:, sl], -1.0, 1.0, ALU.mult, ALU.add, [Ftk], [KK_tk])
                e_scan(B, BC, ones, LF, [ones_tk, LF_tk], [BC_tk])
                BC3 = BC.rearrange("p (n c) -> p n c", c=C)
                DF3 = DF.rearrange("p (n c) -> p n c", c=C)
                sh = [128, NCB, C]
                e_memset(B, BS[:, 0:1], 0.0, [BS_tk])
                e_copy(B, BS[:, 1:NCB], BC3[:, 0:NCB - 1, C - 1], [BC_tk], [BS_tk])
                e_tt(B, GL, BC3[:, :, C - 1], BS, ALU.subtract, [BC_tk, BS_tk], [GL_tk])
                e_act(B, GL, GL, AF.Exp, [GL_tk], [GL_tk])
                e_tt(B, DF3, BC3, BC3[:, :, C // 2 - 1:C // 2].to_broadcast(sh), ALU.subtract, [BC_tk], [DF_tk])
                e_act(B, EE, DF, AF.Exp, [DF_tk], [EE_tk])
                e_tt(B, QT, Q[:, sl], EE, ALU.mult, [Qtk, EE_tk], [QT_tk])
                e_act(B, EE, DF, AF.Exp, [DF_tk], [EE_tk], scale=-1.0)
                e_tt(B, KT, KK, EE, ALU.mult, [KK_tk, EE_tk], [KT_tk])
                e_tt(B, DF3, BC3, BS.rearrange("p (n o) -> p n o", o=1).to_broadcast(sh), ALU.subtract, [BC_tk, BS_tk], [DF_tk])
                e_act(B, EE, DF, AF.Exp, [DF_tk], [EE_tk])
                e_tt(B, QD, Q[:, sl], EE, ALU.mult, [Qtk, EE_tk], [QD_tk])
                e_tt(B, DF3, BC3[:, :, C - 1:C].to_broadcast(sh), BC3, ALU.subtract, [BC_tk], [DF_tk])
                e_act(B, EE, DF, AF.Exp, [DF_tk], [EE_tk])
                e_tt(B, KD, KK, EE, ALU.mult, [KK_tk, EE_tk], [KD_tk])
                pk, pv = psbf(B, 2), psbf(B, 3)
                for i in range(8):
                    e_tr(B, pk[0:64, i * 128:(i + 1) * 128], KD[:, i * 64:(i + 1) * 64], B.ident_b, [KD_tk, B.c_tk], [B.ps_tk[2]])
                    e_tr(B, pv[0:64, i * 128:(i + 1) * 128], V[:, sl][:, i * 64:(i + 1) * 64], B.ident_b, [Vtk, B.c_tk], [B.ps_tk[3]])
                e_copy(B, KDT[0:64].rearrange("p a b -> p (a b)"), pk[0:64, 0:1024], [B.ps_tk[2]], [KDT_tk])
                e_act(B, VT[0:64].rearrange("p a b -> p (a b)"), pv[0:64, 0:1024], AF.Copy, [B.ps_tk[3]], [VT_tk])
                for i in range(8):
                    for c in range(2):
                        cs = slice(i * 64 + c * C, i * 64 + (c + 1) * C)
                        e_mm(B, B.ps[c * C:(c + 1) * C, SB_, c * C:(c + 1) * C], KT[:, cs], QT[:, cs], True, True, [KT_tk, QT_tk], [B.ps_tk[SB_]])
                    e_tt(B, sTm[0:64, 0:64], B.ps[0:64, SB_, 0:64], B.blk32_f[0:64, 0:64], ALU.mult, [B.ps_tk[SB_], B.c_tk], [sTm_tk])
                    e_mm(B, B.ps[:, YB_, 0:64], VT[0:64, i, :], sTm[0:64, 0:64], True, False, [VT_tk, sTm_tk], [B.ps_tk[YB_]])
                    for c in range(2):
                        n = i * 2 + c
                        cs = slice(i * 64 + c * C, i * 64 + (c + 1) * C)
                        sub = n % 2
                        e_mm(B, B.ps[:, sub, 0:128], KDT[c * C:(c + 1) * C, i, :], VT[c * C:(c + 1) * C, i, :], True, True, [KDT_tk, VT_tk], [B.ps_tk[sub]])
                        e_mm(B, B.ps[:, YB_, c * C:(c + 1) * C], S, QD[:, cs], False, (c == 1), [S_tk, QD_tk], [B.ps_tk[YB_]])
                        e_stt(B, S, S, GL[:, n:n + 1], B.ps[:, sub, 0:128], ALU.mult, ALU.add, [S_tk, GL_tk, B.ps_tk[sub]], [S_tk])
                    t0 = b * 512 + i * 64
                    if d == 0:
                        e_act(B, yh[:, t0:t0 + 64], B.ps[:, YB_, 0:64], AF.Copy, [B.ps_tk[YB_]], [yh_tk])
                    else:
                        yv = yh[:, T - t0 - 64:T - t0][:, ::-1]
                        e_tt(B, yv, yv, B.ps[:, YB_, 0:64], ALU.add, [B.ps_tk[YB_], yh_tk], [yh_tk])
        if d == 0:
            yf_store(B, g * 4 + hd, yh, yh_tk)
            continue
        for tb in range(NB):
            sl = slice(tb * 512, (tb + 1) * 512)
            bank = 4 + tb % 2
            e_act(B, sq, yh[:, sl], AF.Square, [yh_tk], [sq_tk])
            e_mm(B, B.ps[:, bank, :], B.ones_b, sq, True, True, [sq_tk, B.c_tk], [B.ps_tk[bank]])
            e_act(B, rstd, B.ps[:, bank, :], AF.Sqrt, [B.ps_tk[bank]], [rstd_tk], scale=1.0 / 128, bias=EPS)
            B.P.op('vector', lambda e: e.reciprocal(out=rstd, in_=rstd), reads=[rstd_tk], writes=[rstd_tk])
            e_stt(B, yh[:, sl], yh[:, sl], B.pc(pre + 'hg_norm', 0), rstd, ALU.mult, ALU.mult, [yh_tk, rstd_tk, B.c_tk], [yh_tk])
            e_tt(B, ybf[:, hd, sl], yh[:, sl], sg[:, sl], ALU.mult, [yh_tk, sg_tk], [ybf_tk])
    if d == 1:
        group_out(B, l, g, ybf, ybf_tk)


def make_consts():
    c = np.zeros((128, 768), np.float32)
    c[:, 640:768] = np.eye(128)[::-1]
    bm = np.zeros((128, 128), np.float32)
    for a in range(4):
        bm[a * 32:(a + 1) * 32, a * 32:(a + 1) * 32] = np.triu(np.ones((32, 32)))
    c[:, 512:640] = bm
    c[:, 0:128] = np.eye(128)
    c[:, 128:256] = np.triu(np.ones((128, 128)))
    c[:, 256:384] = np.triu(np.ones((128, 128)), 1)
    c[:, 384:512] = 1.0
    return c


def host_inputs(inp):
    params, _ = pack_params(inp)
    lw = np.stack([np.asarray(inp['lru_wa']), np.asarray(inp['lru_wi'])], axis=1)
    L = lw.shape[0]
    lru_w = np.ascontiguousarray(lw.reshape(L, 16, 128, 128).transpose(0, 2, 1, 3).reshape(L, 128, 16 * 128))
    return {
        'w_in': np.ascontiguousarray(inp['w_in'], dtype=np.float32),
        'w_out': np.ascontiguousarray(inp['w_out'], dtype=np.float32),
        'w_up': np.ascontiguousarray(inp['w_up'], dtype=np.float32),
        'w_down': np.ascontiguousarray(inp['w_down'], dtype=np.float32),
        'lru_w': lru_w,
        'params': params,
        'consts': make_consts(),
    }


_CACHE = {}


def get_nc(T, **kw):
    key = (T, tuple(sorted(kw.items())))
    if key not in _CACHE:
        _CACHE[key] = Builder(T, **kw).build()
    return _CACHE[key]


def make_flags(carry):
    f = np.zeros((128, 4), np.float32)
    f[:, 0] = 1.0 if carry else 0.0
    f[:, 1] = 0.0 if carry else 1.0
    return f


def kernel(**inputs):
    inp = {k: np.asarray(v) for k, v in inputs.items()}
    xp = inp['x_prompt']
    xsm = inp['x_sample']
    T, NSEG = 2048, 4
    shared = host_inputs(inp)
    nc = get_nc(T, nseg=NSEG)
    x_prompt_T = np.ascontiguousarray(xp[0].T)
    x_sample_T = np.ascontiguousarray(xsm.reshape(NSEG * T, D).T)
    in_maps = []
    for c in range(8):
        m = dict(shared)
        if c == 0:
            m['xT'] = x_prompt_T
            m['flags'] = make_flags(True)
        else:
            m['xT'] = x_sample_T
            m['flags'] = make_flags(False)
        in_maps.append(m)
    res = run_bass_kernel_spmd(nc, in_maps, core_ids=list(range(8)))
    y_prompt = np.asarray(res.results[0]['yT'], dtype=np.float32).T[None]
    y_sample = np.asarray(res.results[1]['yT'], dtype=np.float32).T.reshape(NSEG, T, D)
    return (np.ascontiguousarray(y_prompt), np.ascontiguousarray(y_sample))
```

```python
import numpy as np
import ml_dtypes
from contextlib import ExitStack
import concourse.bass as bass
import concourse.mybir as mybir
from concourse.bass_utils import run_bass_kernel_spmd

F32 = mybir.dt.float32
BF16 = mybir.dt.bfloat16
AF = mybir.ActivationFunctionType
ALU = mybir.AluOpType
AX = mybir.AxisListType

ENGS = ('sync', 'scalar', 'vector', 'gpsimd', 'tensor')

D = 2048
KC = 16
DIN = 7200
DFF = 5632
NFC = 44
FFQ = 4
EPS = 1e-6
DEPTH = 2


class Tk:
    __slots__ = ('w', 'r', 'sem', 'cnt', 'name')

    def __init__(self, name=''):
        self.w = None
        self.r = {}
        self.sem = None
        self.cnt = 0
        self.name = name


class Prog:
    def __init__(self, nc, es, same_engine_sync=True):
        self.nc = nc
        self.es = es
        self.q = {e: [] for e in ENGS}
        self.sem = {e: es.enter_context(nc.semaphore('s_' + e)) for e in ENGS}
        self.cnt = {e: 0 for e in ENGS}
        self.waited = {e: {} for e in ENGS}
        self.semobj = {}
        self.same_engine_sync = same_engine_sync
        self.dma_tks = []
        self.pool = []
        self.nsem = len(ENGS)

    def _deps(self, e, reads, writes):
        deps = {}

        def add(s, v):
            k = id(s)
            self.semobj[k] = s
            if deps.get(k, 0) < v:
                deps[k] = v
        for t in reads:
            if t.w is not None:
                add(*t.w)
        for t in writes:
            if t.w is not None:
                add(*t.w)
            for k, (s_, v_) in t.r.items():
                add(s_, v_)
        waits = []
        own = id(self.sem[e])
        for k, v in deps.items():
            if k == own and (e == 'tensor' or not self.same_engine_sync):
                continue
            if self.waited[e].get(k, 0) < v:
                self.waited[e][k] = v
                waits.append((self.semobj[k], v))
        return waits

    def _mark(self, d, reads, writes):
        for t in writes:
            t.w = d
            t.r = {}
        for t in reads:
            k = id(d[0])
            if k not in t.r or t.r[k][1] < d[1]:
                t.r[k] = d

    def op(self, e, fn, reads=(), writes=()):
        waits = self._deps(e, reads, writes)
        self.cnt[e] += 1
        self.q[e].append((waits, fn, (self.sem[e], 1)))
        self._mark((self.sem[e], self.cnt[e]), reads, writes)

    def dma(self, e, out_ap, in_ap, reads=(), writes=(), dtk=None, **kw):
        waits = self._deps(e, reads, writes)
        tk = dtk
        if tk.sem is None:
            if self.pool:
                tk.sem, tk.cnt = self.pool.pop()
            else:
                tk.sem = self.es.enter_context(self.nc.semaphore())
                tk.cnt = 0
                self.nsem += 1
            self.dma_tks.append(tk)
        tk.cnt += 16
        self.q[e].append((waits, lambda eng: eng.dma_start(out=out_ap, in_=in_ap, **kw), (tk.sem, 16)))
        self._mark((tk.sem, tk.cnt), reads, writes)

    def barrier(self, final=False):
        for e in ENGS:
            waits = []
            for e2 in ENGS:
                if e2 != e and self.cnt[e2] > 0 and self.waited[e].get(id(self.sem[e2]), 0) < self.cnt[e2]:
                    self.waited[e][id(self.sem[e2])] = self.cnt[e2]
                    waits.append((self.sem[e2], self.cnt[e2]))
            for tk in self.dma_tks:
                if self.waited[e].get(id(tk.sem), 0) < tk.cnt:
                    self.waited[e][id(tk.sem)] = tk.cnt
                    waits.append((tk.sem, tk.cnt))
            if waits:
                self.q[e].append((waits, None, None))
        for tk in self.dma_tks:
            self.pool.append((tk.sem, tk.cnt))
            tk.sem = None
        self.dma_tks = []

    def replay(self):
        nc = self.nc
        with nc.Block() as block:
            def mk(e):
                def body(eng):
                    for waits, fn, inc in self.q[e]:
                        for s, v in waits:
                            eng.wait_ge(s, v)
                        if fn is not None:
                            fn(eng).then_inc(*inc)
                return body
            block.sync(mk('sync'))
            block.scalar(mk('scalar'))
            block.vector(mk('vector'))
            block.gpsimd(mk('gpsimd'))
            block.tensor(mk('tensor'))


class PL:
    def __init__(self):
        self.off = {}
        self.n = 0

    def add(self, name, ncols):
        self.off[name] = self.n
        self.n += ncols


def param_layout():
    pl = PL()
    for l in range(DEPTH):
        p = 'L%d_' % l
        pl.add(p + 'ln1', 16)
        pl.add(p + 'ln2', 16)
        pl.add(p + 'gdn_conv', 48)
        pl.add(p + 'gdn_alog', 1)
        pl.add(p + 'gdn_dtb', 1)
        pl.add(p + 'gdn_norm', 1)
        pl.add(p + 'lru_conv', 16)
        pl.add(p + 'lru_convb', 4)
        pl.add(p + 'lru_ba', 8)
        pl.add(p + 'lru_bi', 8)
        pl.add(p + 'lru_lam', 8)
        pl.add(p + 'ssd_conv', 32)
        pl.add(p + 'ssd_convb', 8)
        pl.add(p + 'ssd_alog', 1)
        pl.add(p + 'ssd_dtb', 1)
        pl.add(p + 'ssd_d', 4)
        pl.add(p + 'ssd_dtb_bc', 16)
        pl.add(p + 'ssd_alog_bc', 16)
        pl.add(p + 'gdn_dtb_bc', 8)
        pl.add(p + 'gdn_alog_bc', 8)
        pl.add(p + 'ssd_norm', 4)
        pl.add(p + 'hg_lb0', 8)
        pl.add(p + 'hg_lb1', 8)
        pl.add(p + 'hg_norm', 1)
        pl.add(p + 'gn', 16)
        pl.add(p + 'ffn_conv', 264)
        pl.add(p + 'ffn_convb', 88)
    pl.add('final', 16)
    return pl


def _cols(v):
    v = np.asarray(v, np.float32).reshape(-1, 128)
    return np.ascontiguousarray(v.T)


def _rows(v):
    v = np.asarray(v, np.float32).reshape(-1)
    o = np.zeros((128, 1), np.float32)
    o[:v.size, 0] = v
    return o


def pack_params(inp):
    pl = param_layout()
    P = np.zeros((128, pl.n), np.float32)

    def put(name, arr):
        o = pl.off[name]
        P[:, o:o + arr.shape[1]] = arr
    for l in range(DEPTH):
        p = 'L%d_' % l
        put(p + 'ln1', _cols(inp['ln1'][l]))
        put(p + 'ln2', _cols(inp['ln2'][l]))
        put(p + 'gdn_conv', np.concatenate([_cols(inp['gdn_conv_w'][l, t]) for t in range(4)], 1))
        put(p + 'gdn_alog', _rows(inp['gdn_a_log'][l]))
        put(p + 'gdn_dtb', _rows(inp['gdn_dt_bias'][l]))
        put(p + 'gdn_norm', _cols(inp['gdn_norm_w'][l]))
        put(p + 'lru_conv', np.concatenate([_cols(inp['lru_conv_w'][l, t]) for t in range(4)], 1))
        put(p + 'lru_convb', _cols(inp['lru_conv_b'][l]))
        put(p + 'lru_ba', _cols(inp['lru_ba'][l]))
        put(p + 'lru_bi', _cols(inp['lru_bi'][l]))
        put(p + 'lru_lam', _cols(inp['lru_lambda'][l]))
        put(p + 'ssd_conv', np.concatenate([_cols(inp['ssd_conv_w'][l, t]) for t in range(4)], 1))
        put(p + 'ssd_convb', _cols(inp['ssd_conv_b'][l]))
        put(p + 'ssd_alog', _rows(inp['ssd_a_log'][l]))
        put(p + 'ssd_dtb', _rows(inp['ssd_dt_bias'][l]))
        put(p + 'ssd_d', _cols(np.repeat(np.asarray(inp['ssd_d'][l]), 64)))
        put(p + 'ssd_norm', _cols(inp['ssd_norm_w'][l]))
        put(p + 'ssd_dtb_bc', np.broadcast_to(np.asarray(inp['ssd_dt_bias'][l], np.float32).reshape(1, 16), (128, 16)))
        put(p + 'ssd_alog_bc', np.broadcast_to(np.asarray(inp['ssd_a_log'][l], np.float32).reshape(1, 16), (128, 16)))
        put(p + 'gdn_dtb_bc', np.broadcast_to(np.asarray(inp['gdn_dt_bias'][l], np.float32).reshape(1, 8), (128, 8)))
        put(p + 'gdn_alog_bc', np.broadcast_to(np.asarray(inp['gdn_a_log'][l], np.float32).reshape(1, 8), (128, 8)))
        put(p + 'hg_lb0', _cols(inp['hgrn_lb'][:, 0]))
        put(p + 'hg_lb1', _cols(inp['hgrn_lb'][:, 1]))
        put(p + 'hg_norm', _cols(inp['hgrn_norm_w'][l]))
        put(p + 'gn', _cols(inp['group_norm_w'][l]))
        put(p + 'ffn_conv', np.concatenate([_cols(inp['ffn_conv_w'][l, t]) for t in range(3)], 1))
        put(p + 'ffn_convb', _cols(inp['ffn_conv_b'][l]))
    put('final', _cols(inp['final_norm']))
    return P, pl


class Builder:
    def __init__(self, T, n_layers=DEPTH, mixers='abcd', same_engine_sync=True, dbg=None, nseg=1):
        self.NSEG = nseg
        self.TT = T * nseg
        self.seg = 0
        self.T = T
        self.NB = T // 512
        self.NT = T // 128
        self.n_layers = n_layers
        self.mixers = mixers
        self.dbg = dbg
        self.nc = bass.Bass("TRN2", target_bir_lowering=False)
        self.es = ExitStack()
        self.P = Prog(self.nc, self.es, same_engine_sync)
        self.pl = param_layout()

    def sb(self, name, shape, dt=F32):
        return self.es.enter_context(self.nc.sbuf_tensor(name, shape, dt))

    def dram_in(self, name, shape, dt=F32):
        return self.nc.dram_tensor(name, shape, dt, kind="ExternalInput").ap()

    def dram_out(self, name, shape, dt=F32):
        return self.nc.dram_tensor(name, shape, dt, kind="ExternalOutput").ap()

    def dram_scr(self, name, shape, dt=F32):
        return self.nc.dram_tensor(name, shape, dt).ap()

    def pc(self, name, j=0, n=1, rows=128):
        o = self.pl.off[name] + j
        return self.params[0:rows, o:o + n]

    def rx_reset(self):
        self.P.barrier()
        self.rx_off = 0

    def rx(self, shape, dt=F32):
        n = int(np.prod(shape))
        units = n if dt == F32 else (n + 1) // 2
        a = self.arena[:, self.rx_off:self.rx_off + units]
        self.rx_off += units
        assert self.rx_off <= self.RXN, ("arena overflow", self.rx_off, self.RXN)
        if dt != F32:
            a = a.bitcast(dt)
        if len(shape) == 1:
            a = a.rearrange("p (a b) -> p a b", a=1)
        elif len(shape) == 2:
            a = a.rearrange("p (a b) -> p a b", b=shape[1])
        elif len(shape) == 3:
            a = a.rearrange("p (a b c) -> p a b c", b=shape[1], c=shape[2])
        return a

    def setup(self):
        nc, T = self.nc, self.T
        L = DEPTH
        TT = self.TT
        self.xT = self.dram_in("xT", [D, TT])
        self.flags_d = self.dram_in("flags", [128, 4])
        self.w_in = self.dram_in("w_in", [L, D, DIN])
        self.w_out = self.dram_in("w_out", [L, D, D])
        self.w_up = self.dram_in("w_up", [L, D, 2 * DFF])
        self.w_down = self.dram_in("w_down", [L, DFF, D])
        self.lru_w = self.dram_in("lru_w", [L, 128, 16 * 128])
        self.params_d = self.dram_in("params", [128, self.pl.n])
        self.consts_d = self.dram_in("consts", [128, 768])
        self.yT = self.dram_out("yT", [D, TT])
        self.xs = self.dram_scr("xs", [D, TT])
        self.mixD = self.dram_scr("mixD", [D, TT], BF16)
        self.yF = self.dram_scr("yF", [D, TT])
        self.xs_tk = [[Tk() for _ in range(self.NB * self.NSEG)] for _ in range(KC)]
        self.x0_tk = Tk()
        self.mix_tk = [[Tk() for _ in range(self.NSEG)] for _ in range(KC)]
        self.yF_tk = [[Tk() for _ in range(self.NSEG)] for _ in range(KC)]
        self.flags = self.sb("flags_sb", [128, 4])
        self.hh2 = self.sb("hh2", [128, self.NSEG, KC, 2], BF16)
        self.hh2_tk = Tk()
        self.st_gdn = self.sb("st_gdn", [128, 4, 128]); self.st_gdn_tk = [Tk() for _ in range(4)]
        self.st_hg = self.sb("st_hg", [128, 4, 128]); self.st_hg_tk = [Tk() for _ in range(4)]
        self.st_ssd = self.sb("st_ssd", [128, 8, 64]); self.st_ssd_tk = [Tk() for _ in range(8)]
        self.st_lru = self.sb("st_lru", [128, 4]); self.st_lru_tk = Tk()
        self.params = self.sb("params_sb", [128, self.pl.n])
        self.consts = self.sb("consts_sb", [128, 768])
        self.cb = self.sb("consts_bf", [128, 768], BF16)
        self.hT = self.sb("hT", [128, KC, T], BF16)
        self.hT_tk = [Tk() for _ in range(self.NB)]
        self.hh = self.sb("hhalo", [128, KC, 4], BF16)
        self.hh_tk = Tk()
        self.wst = [self.sb("wst%d" % i, [128, KC, 128]) for i in range(2)]
        self.wst_tk = [Tk() for _ in range(2)]
        self.wbf = [self.sb("wbf%d" % i, [128, KC, 128], BF16) for i in range(3)]
        self.wbf_tk = [Tk() for _ in range(3)]
        self.wi = 0
        self.RXN = 24 * 1024
        self.arena = self.sb("arena", [128, self.RXN])
        self.rx_off = 0
        self.ps = self.es.enter_context(nc.psum_tensor("ps", [128, 8, 512], F32))
        self.ps_tk = [Tk() for _ in range(8)]
        P = self.P
        self.c_tk = Tk()
        P.dma('sync', self.params[:], self.params_d[:, :], writes=[self.c_tk], dtk=self.c_tk)
        P.dma('sync', self.consts[:], self.consts_d[:, :], writes=[self.c_tk], dtk=self.c_tk)
        P.dma('sync', self.flags[:], self.flags_d[:, :], writes=[self.c_tk], dtk=self.c_tk)
        P.op('vector', lambda e: e.tensor_copy(out=self.cb[:], in_=self.consts[:]), reads=[self.c_tk], writes=[self.c_tk])
        P.op('vector', lambda e: e.memset(self.hh[:], 0.0), writes=[self.hh_tk])
        self.ident_f = self.consts[:, 0:128]
        self.ident_b = self.cb[:, 0:128]
        self.triu_b = self.cb[:, 128:256]
        self.ones_b = self.cb[:, 384:512]
        self.striu_b = self.cb[:, 256:384]
        self.blk32_f = self.consts[:, 512:640]
        self.J_f = self.consts[:, 640:768]
        self.triu_f = self.consts[:, 128:256]
        self.striu_f = self.consts[:, 256:384]
        self.ones_f = self.consts[:, 384:512]
        self.carry = self.flags[:, 0:1]
        self.ncarry = self.flags[:, 1:2]

    def gb(self, tb):
        return self.seg * self.NB + tb

    def tsl(self, a, b):
        return slice(self.seg * self.T + a, self.seg * self.T + b)

    def ln_halo(self, src, src_tks, wname, toks, dst, dst_tk):
        P = self.P
        nh = len(toks)
        xh = self.rx([KC, nh]); xh_tk = Tk()
        sqh = self.rx([KC, nh], BF16); sqh_tk = Tk()
        rs = self.rx([nh])[:, 0, :]; rs_tk = Tk()
        e_memset(self, xh, 0.0, [xh_tk])
        for j, t in enumerate(toks):
            if t is None:
                continue
            P.dma('gpsimd', xh[:, :, j:j + 1], src[:, t:t + 1].rearrange("(kc p) o -> p kc o", p=128), reads=src_tks, writes=[xh_tk], dtk=xh_tk,
                  allow_slow_non_contiguous=True)
        e_act(self, sqh, xh, AF.Square, [xh_tk], [sqh_tk])
        for kc in range(KC):
            e_mm(self, self.ps[:, 5, 0:nh], self.ones_b, sqh[:, kc, :], kc == 0, kc == KC - 1, [sqh_tk, self.c_tk], [self.ps_tk[5]])
        e_act(self, rs, self.ps[:, 5, 0:nh], AF.Sqrt, [self.ps_tk[5]], [rs_tk], scale=1.0 / D, bias=EPS)
        P.op('vector', lambda e: e.reciprocal(out=rs, in_=rs), reads=[rs_tk], writes=[rs_tk])
        e_ts(self, rs, rs, self.carry, None, ALU.mult, None, [rs_tk, self.c_tk], [rs_tk])
        o = self.pl.off[wname]
        wv = self.params[:, o:o + KC].rearrange("p (k o) -> p k o", o=1).to_broadcast([128, KC, nh])
        e_tt(self, xh, xh, wv, ALU.mult, [xh_tk, self.c_tk], [xh_tk])
        e_tt(self, dst, xh, rs.rearrange("p (o n) -> p o n", o=1).to_broadcast([128, KC, nh]), ALU.mult, [xh_tk, rs_tk], [dst_tk])

    def load_w(self, src_ap, nk, ncol):
        P = self.P
        i = self.wi
        self.wi += 1
        st, st_tk = self.wst[i % 2], self.wst_tk[i % 2]
        wb, wb_tk = self.wbf[i % 3], self.wbf_tk[i % 3]
        src3 = src_ap if len(src_ap.shape) == 3 else src_ap.rearrange("(kc p) n -> p kc n", p=128)
        P.dma('sync', st[:, 0:nk, 0:ncol], src3, writes=[st_tk], dtk=st_tk)
        P.op('gpsimd', lambda e: e.tensor_copy(out=wb[:, 0:nk, 0:ncol], in_=st[:, 0:nk, 0:ncol]), reads=[st_tk], writes=[wb_tk])
        return wb, wb_tk

    def w_begin(self, specs):
        self.wspecs = specs
        self.wpos = 0
        self.wloaded = {}

    def w_get(self):
        i = self.wpos
        if i not in self.wloaded:
            self.wloaded[i] = self.load_w(*self.wspecs[i])
        self.wpos += 1
        return self.wloaded.pop(i)

    def w_prefetch(self):
        i = self.wpos
        if i < len(self.wspecs) and i not in self.wloaded:
            self.wloaded[i] = self.load_w(*self.wspecs[i])

    def proj(self, nk, ncol, rhs_fn, rhs_tks_fn, evac_fn, banks=(0, 1, 2, 3), halo=None):
        P = self.P
        wb, wb_tk = self.w_get()
        if halo is not None:
            hr, hr_tk, hev = halo
            nh = hr.shape[2]
            for kc in range(nk):
                e_mm(self, self.ps[0:ncol, 5, 0:nh], wb[:, kc, 0:ncol], hr[:, kc, :], kc == 0, kc == nk - 1, [wb_tk, hr_tk], [self.ps_tk[5]])
            hev(self.ps[0:ncol, 5, 0:nh], self.ps_tk[5])
        for kc in range(nk):
            for tb in range(self.NB):
                b = banks[tb]
                P.op('tensor', lambda e, kc=kc, tb=tb, b=b: e.matmul(self.ps[0:ncol, b, :], lhsT=wb[:, kc, 0:ncol], rhs=rhs_fn(kc, tb),
                                                                  start=(kc == 0), stop=(kc == nk - 1)),
                     reads=[wb_tk] + rhs_tks_fn(kc, tb), writes=[self.ps_tk[b]])
        self.w_prefetch()
        for tb in range(self.NB):
            b = banks[tb]
            evac_fn(tb, self.ps[0:ncol, b, :], self.ps_tk[b])

    def ln_phase(self, src, src_tks_fn, wname, final=False, reset=True):
        P, T = self.P, self.T
        if reset:
            self.rx_reset()
        xts = [self.rx([KC, 512]) for _ in range(2)]
        xt_tk = [Tk() for _ in range(2)]
        sqs = [self.rx([512], BF16) for _ in range(2)]
        sq_tk = [Tk() for _ in range(2)]
        rstd = self.rx([512])
        rstd_tk = Tk()
        for tb in range(self.NB):
            xt, xtk = xts[tb % 2], xt_tk[tb % 2]
            P.dma('sync', xt, src[:, self.tsl(tb * 512, (tb + 1) * 512)].rearrange("(kc p) t -> p kc t", p=128),
                  reads=src_tks_fn(self.gb(tb)), writes=[xtk], dtk=xtk)
            bank = 4 + tb % 2
            for kc in range(KC):
                sq, stk = sqs[kc % 2], sq_tk[kc % 2]
                P.op('scalar', lambda e, sq=sq, xt=xt, kc=kc: e.activation(out=sq[:, 0, :], in_=xt[:, kc, :], func=AF.Square),
                     reads=[xtk], writes=[stk])
                P.op('tensor', lambda e, sq=sq, kc=kc, bank=bank: e.matmul(self.ps[:, bank, :], lhsT=self.ones_b, rhs=sq[:, 0, :],
                                                                          start=(kc == 0), stop=(kc == KC - 1)),
                     reads=[stk, self.c_tk], writes=[self.ps_tk[bank]])
            P.op('scalar', lambda e, bank=bank: e.activation(out=rstd[:, 0, :], in_=self.ps[:, bank, :], func=AF.Sqrt, scale=1.0 / D, bias=EPS),
                 reads=[self.ps_tk[bank]], writes=[rstd_tk])
            P.op('vector', lambda e: e.reciprocal(out=rstd[:, 0, :], in_=rstd[:, 0, :]), reads=[rstd_tk], writes=[rstd_tk])
            for kc in range(KC):
                eng = 'vector' if kc % 2 == 0 else 'gpsimd'
                if final:
                    dst = xt[:, kc, :]
                    wr = [xtk]
                else:
                    dst = self.hT[:, kc, tb * 512:(tb + 1) * 512]
                    wr = [self.hT_tk[tb]]
                if eng == 'vector':
                    P.op('vector', lambda e, dst=dst, xt=xt, kc=kc: e.scalar_tensor_tensor(out=dst, in0=xt[:, kc, :], scalar=self.pc(wname, kc), in1=rstd[:, 0, :],
                                                                                       op0=ALU.mult, op1=ALU.mult),
                         reads=[xtk, rstd_tk, self.c_tk], writes=wr)
                else:
                    P.op('gpsimd', lambda e, xt=xt, kc=kc: e.tensor_scalar(out=xt[:, kc, :], in0=xt[:, kc, :], scalar1=self.pc(wname, kc), scalar2=None, op0=ALU.mult),
                         reads=[self.c_tk], writes=[xtk])
                    P.op('gpsimd', lambda e, dst=dst, xt=xt, kc=kc: e.tensor_tensor(out=dst, in0=xt[:, kc, :], in1=rstd[:, 0, :], op=ALU.mult),
                         reads=[xtk, rstd_tk], writes=wr)
            if final:
                P.dma('sync', self.yT[:, self.tsl(tb * 512, (tb + 1) * 512)].rearrange("(kc p) t -> p kc t", p=128), xt, reads=[xtk], dtk=xtk)

    def resid_specs(self, w_fn, nk):
        return [(w_fn(dc), nk, 128) for dc in range(KC)]

    def resid_proj(self, nk, act, act_tks_fn, src, src_tks_fn):
        P = self.P
        for dc in range(KC):
            def evac(tb, ps_ap, ps_tk, dc=dc):
                xi, xi_tk = self.xin[self.xin_i % 3], self.xin_tk[self.xin_i % 3]
                self.xin_i += 1
                gsl = self.tsl(tb * 512, (tb + 1) * 512)
                P.dma('gpsimd', xi[:, 0, :], src[dc * 128:(dc + 1) * 128, gsl], reads=src_tks_fn(dc, self.gb(tb)), writes=[xi_tk], dtk=xi_tk)
                P.op('vector', lambda e: e.tensor_tensor(out=xi[:, 0, :], in0=ps_ap, in1=xi[:, 0, :], op=ALU.add), reads=[ps_tk, xi_tk], writes=[xi_tk])
                P.dma('gpsimd', self.xs[dc * 128:(dc + 1) * 128, gsl], xi[:, 0, :], reads=[xi_tk], writes=[self.xs_tk[dc][self.gb(tb)]], dtk=xi_tk)
            self.proj(nk, 128, lambda kc, tb: act[:, kc, tb * 512:(tb + 1) * 512], act_tks_fn, evac)

    def alloc_xin(self):
        self.xin = [self.rx([512]) for _ in range(3)]
        self.xin_tk = [Tk() for _ in range(3)]
        self.xin_i = 0

    def ffn_phase(self, l):
        P, T = self.P, self.T
        self.rx_reset()
        NQ = NFC // FFQ
        act = self.rx([NQ, T], BF16)
        act_tk = [Tk() for _ in range(NQ)]
        ug = self.rx([T + 2]); ug_tk = Tk()
        cg = self.rx([T]); cg_tk = Tk()
        uv = self.rx([T + 2]); uv_tk = Tk()
        cv = self.rx([T]); cv_tk = Tk()
        self.alloc_xin()
        ug, cg, uv, cv = ug[:, 0, :], cg[:, 0, :], uv[:, 0, :], cv[:, 0, :]
        pre = 'L%d_' % l
        hT = self.hT

        def up_chunk(col, fcg, u, u_tk, c, c_tk):
            def evac(tb, ps_ap, ps_tk):
                P.op('scalar', lambda e: e.activation(out=u[:, 1 + tb * 512:1 + (tb + 1) * 512], in_=ps_ap, func=AF.Copy), reads=[ps_tk], writes=[u_tk])
                P.op('scalar', lambda e: e.activation(out=c[:, tb * 512:(tb + 1) * 512], in_=ps_ap, func=AF.Identity,
                                                      scale=self.pc(pre + 'ffn_conv', 88 + fcg), bias=self.pc(pre + 'ffn_convb', fcg)),
                     reads=[ps_tk, self.c_tk], writes=[c_tk])
            def hev(ps_ap, ps_tk):
                e_act(self, u[:, 0:1], ps_ap[:, 0:1], AF.Copy, [ps_tk], [u_tk])
                e_act(self, u[:, T + 1:T + 2], ps_ap[:, 1:2], AF.Copy, [ps_tk], [u_tk])
            self.proj(KC, 128, lambda kc, tb: hT[:, kc, tb * 512:(tb + 1) * 512],
                      lambda kc, tb: [self.hT_tk[tb]], evac, halo=(self.hh2[:, self.seg], self.hh2_tk, hev))
            P.op('vector', lambda e: e.scalar_tensor_tensor(out=c, in0=u[:, 0:T], scalar=self.pc(pre + 'ffn_conv', fcg), in1=c, op0=ALU.mult, op1=ALU.add),
                 reads=[u_tk, c_tk, self.c_tk], writes=[c_tk])
            P.op('vector', lambda e: e.scalar_tensor_tensor(out=c, in0=u[:, 2:T + 2], scalar=self.pc(pre + 'ffn_conv', 176 + fcg), in1=c, op0=ALU.mult, op1=ALU.add),
                 reads=[u_tk, c_tk, self.c_tk], writes=[c_tk])

        specs = []
        for q in range(FFQ):
            for j in range(NQ):
                fc = q * NQ + j
                specs.append((self.w_up[l, :, fc * 128:(fc + 1) * 128], KC, 128))
                specs.append((self.w_up[l, :, DFF + fc * 128:DFF + (fc + 1) * 128], KC, 128))
            specs += self.resid_specs(lambda dc, q=q: self.w_down[l, q * NQ * 128:(q + 1) * NQ * 128, dc * 128:(dc + 1) * 128], NQ)
        self.w_begin(specs)
        for q in range(FFQ):
            for j in range(NQ):
                fc = q * NQ + j
                up_chunk(fc * 128, fc, ug, ug_tk, cg, cg_tk)
                P.op('scalar', lambda e: e.activation(out=cg, in_=cg, func=AF.Silu), reads=[cg_tk], writes=[cg_tk])
                up_chunk(DFF + fc * 128, NFC + fc, uv, uv_tk, cv, cv_tk)
                P.op('gpsimd', lambda e, j=j: e.tensor_tensor(out=act[:, j, :], in0=cg, in1=cv, op=ALU.mult), reads=[cg_tk, cv_tk], writes=[act_tk[j]])
            self.resid_proj(NQ, act, lambda kc, tb: [act_tk[kc]], self.xs, lambda dc, tb: [self.xs_tk[dc][tb]])

    def mixer_phase(self, l):
        from_mixers(self, l)

    def outproj_phase(self, l, src, src_tks_fn):
        P, T = self.P, self.T
        self.rx_reset()
        self.alloc_xin()
        for kc in range(KC):
            P.dma('sync', self.hT[:, kc, :], self.mixD[kc * 128:(kc + 1) * 128, self.tsl(0, T)], reads=[self.mix_tk[kc][self.seg]], writes=self.hT_tk, dtk=self.hT_tk[0])
        self.w_begin(self.resid_specs(lambda dc: self.w_out[l, :, dc * 128:(dc + 1) * 128], KC))
        self.resid_proj(KC, self.hT, lambda kc, tb: [self.hT_tk[tb]], src, src_tks_fn)

    def halo_toks(self, left, right):
        t0 = self.seg * self.T
        toks = []
        for k in range(left, 0, -1):
            toks.append(t0 - k if t0 - k >= 0 else None)
        for k in range(right):
            t = t0 + self.T + k
            toks.append(t if t < self.TT else None)
        return toks

    def build(self):
        self.setup()
        NS = self.NSEG
        all_tk = lambda gbk: [self.xs_tk[dc][gbk] for dc in range(KC)]
        for l in range(self.n_layers):
            pre = 'L%d_' % l
            if l == 0:
                src, stf, stf2 = self.xT, (lambda gbk: [self.x0_tk]), (lambda dc, gbk: [self.x0_tk])
                halo_src_tks = [self.x0_tk]
            else:
                src, stf, stf2 = self.xs, all_tk, (lambda dc, gbk: [self.xs_tk[dc][gbk]])
                halo_src_tks = [t for row in self.xs_tk for t in row]
            for d in (0, 1):
                for seg in (range(NS) if d == 0 else range(NS - 1, -1, -1)):
                    self.seg = seg
                    self.ln_phase(src, stf, pre + 'ln1')
                    self.ln_halo(src, halo_src_tks, pre + 'ln1', self.halo_toks(2, 1), self.hh[:, :, 0:3], self.hh_tk)
                    from_mixers(self, l, d)
            for seg in range(NS):
                self.seg = seg
                self.outproj_phase(l, src, stf2)
            self.rx_reset()
            xs_all = [t for row in self.xs_tk for t in row]
            for seg in range(NS):
                self.seg = seg
                self.ln_halo(self.xs, xs_all, pre + 'ln2', self.halo_toks(1, 1), self.hh2[:, seg], self.hh2_tk)
            for seg in range(NS):
                self.seg = seg
                self.ln_phase(self.xs, all_tk, pre + 'ln2')
                self.ffn_phase(l)
        for seg in range(NS):
            self.seg = seg
            self.ln_phase(self.xs, all_tk, 'final', final=True)
        self.P.barrier(final=True)
        self.P.replay()
        return self.nc


def from_mixers(B, l, d):
    P, T = B.P, B.T
    for g, (name, fn) in enumerate((('a', mixer_gdn), ('b', mixer_lru), ('c', mixer_ssd), ('d', mixer_hgrn))):
        B.rx_reset()
        if name in B.mixers:
            fn(B, l, g, d)
        elif d == 1:
            z = B.rx([T], BF16)
            ztk = Tk()
            P.op('vector', lambda e, z=z: e.memset(z[:, 0, :], 0.0), writes=[ztk])
            for c in range(4):
                kc = g * 4 + c
                P.dma('sync', B.mixD[kc * 128:(kc + 1) * 128, B.tsl(0, T)], z[:, 0, :], reads=[ztk], writes=[B.mix_tk[kc][B.seg]], dtk=ztk)


def lockstep(gens):
    gens = list(gens)
    while gens:
        for g_ in list(gens):
            try:
                next(g_)
            except StopIteration:
                gens.remove(g_)


def is_first(B, d):
    return B.seg == 0 if d == 0 else B.seg == B.NSEG - 1


def state_init(B, d, S, S_tk):
    if is_first(B, d):
        e_memset(B, S, 0.0, [S_tk])
    else:
        e_ts(B, S, S, B.carry, None, ALU.mult, None, [S_tk, B.c_tk], [S_tk])


def yf_store(B, kc, y_ap, y_tk):
    B.P.dma('sync', B.yF[kc * 128:(kc + 1) * 128, B.tsl(0, B.T)], y_ap, reads=[y_tk], writes=[B.yF_tk[kc][B.seg]], dtk=y_tk)


def yf_load(B, kc, y_ap, y_tk):
    B.P.dma('sync', y_ap, B.yF[kc * 128:(kc + 1) * 128, B.tsl(0, B.T)], reads=[B.yF_tk[kc][B.seg]], writes=[y_tk], dtk=y_tk)


def inproj(B, ncol, evac, halo=None):
    B.proj(KC, ncol, lambda kc, tb: B.hT[:, kc, tb * 512:(tb + 1) * 512], lambda kc, tb: [B.hT_tk[tb]], evac, halo=halo)


def win_spec(B, l, col0, ncol):
    return (B.w_in[l, :, col0:col0 + ncol], KC, ncol)


def inproj_raw(B, ncol, u, u_tk, off=2):
    P = B.P

    def evac(tb, ps_ap, ps_tk):
        P.op('scalar', lambda e: e.activation(out=u[0:ncol, off + tb * 512:off + (tb + 1) * 512], in_=ps_ap, func=AF.Copy), reads=[ps_tk], writes=[u_tk])

    def hev(ps_ap, ps_tk):
        T = B.T
        e_act(B, u[0:ncol, 0:2], ps_ap[:, 0:2], AF.Copy, [ps_tk], [u_tk])
        e_act(B, u[0:ncol, T + 2:T + 3], ps_ap[:, 2:3], AF.Copy, [ps_tk], [u_tk])
    inproj(B, ncol, evac, halo=(B.hh[:, :, 0:3], B.hh_tk, hev))


def inproj_act(B, ncol, dst, dst_tk, func, **kw):
    P = B.P

    def evac(tb, ps_ap, ps_tk):
        P.op('scalar', lambda e: e.activation(out=dst[0:ncol, tb * 512:(tb + 1) * 512], in_=ps_ap, func=func, **kw), reads=[ps_tk, B.c_tk], writes=[dst_tk])
    inproj(B, ncol, evac)


def conv4(B, u, u_tk, c, c_tk, wname, nch, ch, bias_ap, dst, dst_tk, func):
    P, T = B.P, B.T
    w = lambda k: B.pc(wname, k * nch + ch)
    P.op('vector', lambda e: e.tensor_scalar(out=c, in0=u[:, 0:T], scalar1=w(0), scalar2=(bias_ap if bias_ap is not None else 0.0), op0=ALU.mult, op1=ALU.add),
         reads=[u_tk, B.c_tk], writes=[c_tk])
    for k in (1, 2, 3):
        P.op('vector', lambda e, k=k: e.scalar_tensor_tensor(out=c, in0=u[:, k:k + T], scalar=w(k), in1=c, op0=ALU.mult, op1=ALU.add),
             reads=[u_tk, c_tk, B.c_tk], writes=[c_tk])
    if dst is not None:
        P.op('scalar', lambda e: e.activation(out=dst, in_=c, func=func), reads=[c_tk], writes=[dst_tk])


def alloc_u(B):
    P, T = B.P, B.T
    u = B.rx([T + 3])[:, 0, :]
    u_tk = Tk()
    return u, u_tk


def group_out(B, l, g, y, y_tk):
    P, T = B.P, B.T
    pre = 'L%d_' % l
    sqs = [B.rx([512], BF16) for _ in range(2)]
    sq_tk = [Tk() for _ in range(2)]
    rstd = B.rx([512])
    rstd_tk = Tk()
    ots = [B.rx([512], BF16) for _ in range(2)]
    ot_tk = [Tk() for _ in range(2)]
    oi = 0
    for tb in range(B.NB):
        bank = 4 + tb % 2
        sl = slice(tb * 512, (tb + 1) * 512)
        for c in range(4):
            sq, stk = sqs[c % 2], sq_tk[c % 2]
            P.op('scalar', lambda e, sq=sq, c=c, sl=sl: e.activation(out=sq[:, 0, :], in_=y[:, c, sl], func=AF.Square), reads=[y_tk], writes=[stk])
            P.op('tensor', lambda e, sq=sq, c=c, bank=bank: e.matmul(B.ps[:, bank, :], lhsT=B.ones_b, rhs=sq[:, 0, :], start=(c == 0), stop=(c == 3)),
                 reads=[stk, B.c_tk], writes=[B.ps_tk[bank]])
        P.op('scalar', lambda e, bank=bank: e.activation(out=rstd[:, 0, :], in_=B.ps[:, bank, :], func=AF.Sqrt, scale=1.0 / 512, bias=EPS),
             reads=[B.ps_tk[bank]], writes=[rstd_tk])
        P.op('vector', lambda e: e.reciprocal(out=rstd[:, 0, :], in_=rstd[:, 0, :]), reads=[rstd_tk], writes=[rstd_tk])
        for c in range(4):
            ot, otk = ots[oi % 2], ot_tk[oi % 2]
            oi += 1
            P.op('vector', lambda e, ot=ot, c=c, sl=sl: e.scalar_tensor_tensor(out=ot[:, 0, :], in0=y[:, c, sl], scalar=B.pc(pre + 'gn', g * 4 + c), in1=rstd[:, 0, :],
                                                                            op0=ALU.mult, op1=ALU.mult),
                 reads=[y_tk, rstd_tk, B.c_tk], writes=[otk])
            kc = g * 4 + c
            P.dma('sync', B.mixD[kc * 128:(kc + 1) * 128, B.tsl(tb * 512, (tb + 1) * 512)], ot[:, 0, :], reads=[otk], writes=[B.mix_tk[kc][B.seg]], dtk=otk)


def e_act(B, out, in_, func, reads, writes, eng='scalar', **kw):
    B.P.op(eng, lambda e: e.activation(out=out, in_=in_, func=func, **kw), reads=reads, writes=writes)


def e_mm(B, out, lhsT, rhs, start, stop, reads, writes):
    B.P.op('tensor', lambda e: e.matmul(out, lhsT=lhsT, rhs=rhs, start=start, stop=stop), reads=reads, writes=writes)


def e_tr(B, out, in_, ident, reads, writes):
    B.P.op('tensor', lambda e: e.transpose(out, in_, ident), reads=reads, writes=writes)


def e_tt(B, out, in0, in1, op, reads, writes, eng='vector'):
    B.P.op(eng, lambda e: e.tensor_tensor(out=out, in0=in0, in1=in1, op=op), reads=reads, writes=writes)


def e_ts(B, out, in0, s1, s2, op0, op1, reads, writes, eng='vector'):
    if s2 is None:
        B.P.op(eng, lambda e: e.tensor_scalar(out=out, in0=in0, scalar1=s1, scalar2=None, op0=op0), reads=reads, writes=writes)
    else:
        B.P.op(eng, lambda e: e.tensor_scalar(out=out, in0=in0, scalar1=s1, scalar2=s2, op0=op0, op1=op1), reads=reads, writes=writes)


def e_stt(B, out, in0, scalar, in1, op0, op1, reads, writes):
    B.P.op('vector', lambda e: e.scalar_tensor_tensor(out=out, in0=in0, scalar=scalar, in1=in1, op0=op0, op1=op1), reads=reads, writes=writes)


def e_scan(B, out, d0, d1, reads, writes, initial=0.0):
    B.P.op('vector', lambda e: e.tensor_tensor_scan(out=out, data0=d0, data1=d1, initial=initial, op0=ALU.mult, op1=ALU.add), reads=reads, writes=writes)


def e_memset(B, ap, val, writes, eng='vector'):
    B.P.op(eng, lambda e: e.memset(ap, val), writes=writes)


def e_copy(B, out, in_, reads, writes, eng='vector'):
    B.P.op(eng, lambda e: e.tensor_copy(out=out, in_=in_), reads=reads, writes=writes)


def mixer_lru(B, l, g, d):
    P, T = B.P, B.T
    pre = 'L%d_' % l
    X0, G0 = 2064, 2576
    y = B.rx([4, T]); y_tk = Tk()
    u, u_tk = alloc_u(B)
    xc = B.rx([T])[:, 0, :]; xc_tk = Tk()
    xcb = B.rx([T], BF16)[:, 0, :]; xcb_tk = Tk()
    gg = B.rx([T])[:, 0, :]; gg_tk = Tk()
    bA = B.rx([T])[:, 0, :]; bA_tk = Tk()
    bB = B.rx([T])[:, 0, :]; bB_tk = Tk()
    bC = B.rx([T])[:, 0, :]; bC_tk = Tk()
    lw = B.rx([16, 128], BF16); lw_tk = Tk()
    sp = B.rx([16])[:, 0, :]; sp_tk = Tk()
    hin = B.rx([4])[:, 0, :]; hin_tk = Tk()
    st, st_tk = B.wst[B.wi % 2], B.wst_tk[B.wi % 2]
    B.wi += 1
    P.dma('sync', st[:, :, :], B.lru_w[l].rearrange("p (k n) -> p k n", n=128), writes=[st_tk], dtk=st_tk)
    e_copy(B, lw, st[:, :, :], [st_tk], [lw_tk], eng='gpsimd')
    e_act(B, sp[:, 0:8], B.pc(pre + 'lru_lam', 0, 8), AF.Exp, [B.c_tk], [sp_tk], scale=-1.0)
    e_act(B, sp[:, 0:8], sp[:, 0:8], AF.Ln, [sp_tk], [sp_tk], bias=1.0)
    e_ts(B, sp[:, 8:16], sp[:, 0:8], -16.0, None, ALU.mult, None, [sp_tk], [sp_tk])
    e_ts(B, sp[:, 0:8], sp[:, 0:8], -8.0, None, ALU.mult, None, [sp_tk], [sp_tk])
    first = is_first(B, d)
    if not first:
        e_ts(B, hin, B.st_lru[:, 0:4], B.carry, None, ALU.mult, None, [B.st_lru_tk, B.c_tk], [hin_tk])
    specs = []
    for n in range(4):
        specs.append(win_spec(B, l, X0 + n * 128, 128))
        if d == 1:
            specs.append(win_spec(B, l, G0 + n * 128, 128))
    B.w_begin(specs)
    for n in range(4):
        inproj_raw(B, 128, u, u_tk)
        conv4(B, u, u_tk, xc, xc_tk, pre + 'lru_conv', 4, n, B.pc(pre + 'lru_convb', n), xcb, xcb_tk, AF.Copy)
        if d == 1:
            inproj_act(B, 128, gg, gg_tk, AF.Gelu)
            yf_load(B, g * 4 + n, y[:, n, :], y_tk)
        for which, dst, dst_tk, bname in ((0, bA, bA_tk, 'lru_ba'), (1, bC, bC_tk, 'lru_bi')):
            for tb in range(B.NB):
                bank = 4 + (tb % 2) + 2 * which
                sl = slice(tb * 512, (tb + 1) * 512)
                e_mm(B, B.ps[:, bank, :], lw[:, which * 8 + d * 4 + n, :], xcb[:, sl], True, True, [lw_tk, xcb_tk], [B.ps_tk[bank]])
                e_act(B, dst[:, sl], B.ps[:, bank, :], AF.Sigmoid, [B.ps_tk[bank], B.c_tk], [dst_tk], bias=B.pc(pre + bname, d * 4 + n))
        k = d * 4 + n
        e_act(B, bB, bA, AF.Exp, [bA_tk, sp_tk], [bB_tk], scale=sp[:, k:k + 1])
        e_act(B, bA, bA, AF.Exp, [bA_tk, sp_tk], [bA_tk], scale=sp[:, 8 + k:9 + k])
        e_act(B, bA, bA, AF.Sqrt, [bA_tk], [bA_tk], scale=-1.0, bias=1.0)
        fp = 0 if d == 0 else T - 1
        if first:
            e_memset(B, bA[:, fp:fp + 1], 1.0, [bA_tk])
        else:
            e_ts(B, bA[:, fp:fp + 1], bA[:, fp:fp + 1], B.carry, B.ncarry, ALU.mult, ALU.add, [bA_tk, B.c_tk], [bA_tk])
        e_tt(B, bC, bC, xc, ALU.mult, [bC_tk, xc_tk], [bC_tk])
        e_tt(B, bC, bC, bA, ALU.mult, [bC_tk, bA_tk], [bC_tk])
        init = 0.0 if first else hin[:, n:n + 1]
        rd = [bB_tk, bC_tk] + ([] if first else [hin_tk])
        if d == 0:
            e_scan(B, bA, bB, bC, rd, [bA_tk], initial=init)
            e_copy(B, B.st_lru[:, n:n + 1], bA[:, T - 1:T], [bA_tk], [B.st_lru_tk])
            yf_store(B, g * 4 + n, bA, bA_tk)
        else:
            e_scan(B, bA[:, ::-1], bB[:, ::-1], bC[:, ::-1], rd, [bA_tk], initial=init)
            e_copy(B, B.st_lru[:, n:n + 1], bA[:, 0:1], [bA_tk], [B.st_lru_tk])
            e_tt(B, y[:, n, :], y[:, n, :], bA, ALU.add, [bA_tk, y_tk], [y_tk])
            e_tt(B, y[:, n, :], y[:, n, :], gg, ALU.mult, [gg_tk, y_tk], [y_tk])
    if d == 1:
        group_out(B, l, g, y, y_tk)


def mixer_gdn(B, l, g, d):
    P, T, NB, NT = B.P, B.T, B.NB, B.NT
    pre = 'L%d_' % l
    ybf = B.rx([4, T], BF16); ybf_tk = Tk()
    specs = [win_spec(B, l, 2048, 16)]
    for h in range(4):
        specs += [win_spec(B, l, h * 128, 128), win_spec(B, l, 512 + h * 128, 128), win_spec(B, l, 1024 + h * 128, 128)]
        if d == 1:
            specs += [win_spec(B, l, 1536 + h * 128, 128)]
    B.w_begin(specs)
    nr, W = 16, NT * 16
    def buf():
        return B.rx([NT, nr])
    RAW, ORD, ACS, ACSL = buf(), buf(), buf(), buf()
    abc = B.rx([8])[:, 0, :]
    stk = Tk()
    flat = lambda a: a.rearrange("p a b -> p (a b)")
    wb, wb_tk = B.w_get()
    for i in range(NT):
        for kc in range(KC):
            e_mm(B, B.ps[:, 4, i * nr:(i + 1) * nr], B.hT[:, kc, i * 128:(i + 1) * 128], wb[:, kc, 0:nr], kc == 0, kc == KC - 1,
                 [B.hT_tk[i // 4], wb_tk], [B.ps_tk[4]])
    B.w_prefetch()
    ps3 = B.ps[:, 4, 0:W].rearrange("p (a b) -> p a b", b=nr)
    e_act(B, RAW[:, :, 0:8], ps3[:, :, 0:8], AF.Sigmoid, [B.ps_tk[4]], [stk])
    e_tt(B, RAW[:, :, 8:16], ps3[:, :, 8:16], B.pc(pre + 'gdn_dtb_bc', 0, 8).rearrange("p (a b) -> p a b", a=1).to_broadcast([128, NT, 8]),
         ALU.add, [B.ps_tk[4], B.c_tk], [stk])
    e_act(B, RAW[:, :, 8:16], RAW[:, :, 8:16], AF.Exp, [stk], [stk])
    e_act(B, RAW[:, :, 8:16], RAW[:, :, 8:16], AF.Ln, [stk], [stk], bias=1.0)
    e_act(B, abc, B.pc(pre + 'gdn_alog_bc', 0, 8), AF.Exp, [B.c_tk], [stk])
    e_ts(B, abc, abc, -1.0, None, ALU.mult, None, [stk], [stk])
    e_tt(B, RAW[:, :, 8:16], RAW[:, :, 8:16], abc.rearrange("p (a b) -> p a b", a=1).to_broadcast([128, NT, 8]), ALU.mult, [stk], [stk])
    e_mm(B, B.ps[:, 5, 0:W], B.J_f, flat(RAW), True, True, [stk, B.c_tk], [B.ps_tk[5]])
    pj3 = B.ps[:, 5, 0:W].rearrange("p (a b) -> p a b", b=nr)
    for c0 in (0, 8):
        e_copy(B, ORD[:, :, c0:c0 + 4], RAW[:, :, c0:c0 + 4], [stk], [stk])
        e_copy(B, ORD[:, :, c0 + 4:c0 + 8], pj3[:, ::-1, c0 + 4:c0 + 8], [B.ps_tk[5], stk], [stk])
    e_mm(B, B.ps[:, 4, 0:W], B.triu_f, flat(ORD), True, True, [stk, B.c_tk], [B.ps_tk[4]])
    e_copy(B, flat(ACS), B.ps[:, 4, 0:W], [B.ps_tk[4]], [stk])
    e_mm(B, B.ps[:, 5, 0:W], B.ones_f, flat(ORD), True, True, [stk, B.c_tk], [B.ps_tk[5]])
    e_copy(B, flat(ACSL), B.ps[:, 5, 0:W], [B.ps_tk[5]], [stk])
    BETA = ORD[:, :, 0:8]
    GC = ACS[:, :, 8:16]
    EG = RAW[:, :, 0:8]; EGL = RAW[:, :, 8:16]; GLt = ACSL[:, :, 0:8]
    e_act(B, EG, GC, AF.Exp, [stk], [stk])
    e_tt(B, EGL, ACSL[:, :, 8:16], GC, ALU.subtract, [stk], [stk])
    e_act(B, EGL, EGL, AF.Exp, [stk], [stk])
    e_act(B, GLt, ACSL[:, :, 8:16], AF.Exp, [stk], [stk])
    qf = B.rx([T], BF16)[:, 0, :]; qf_tk = Tk()
    kf = B.rx([T], BF16)[:, 0, :]; kf_tk = Tk()
    vf = B.rx([T], BF16)[:, 0, :]; vf_tk = Tk()
    zs = B.rx([T], BF16)[:, 0, :]; zs_tk = Tk()
    y = B.rx([T])[:, 0, :]; y_tk = Tk()
    u, u_tk = alloc_u(B)
    cc = B.rx([T])[:, 0, :]; cc_tk = Tk()
    OB = B.rx([3, 512], BF16); OB_tk = Tk()
    def t16(n=1):
        a = B.rx([n, 128], BF16)
        return a, Tk()
    tok3, tok3_tk = t16(3)
    sc = B.rx([8])[:, 0, :]; sc_tk = Tk()
    junk = B.rx([128], BF16)[:, 0, :]; junk_tk = Tk()
    tkvS = [t16(6) for _ in range(2)]
    vbS = [t16(1) for _ in range(2)]
    fm4S = [t16(4) for _ in range(2)]
    NfS = [(B.rx([1, 128]), Tk()) for _ in range(2)]
    qkS = [t16(1) for _ in range(2)]
    RBf = B.rx([128])[:, 0, :]
    Dm = B.rx([128])[:, 0, :]; Dm_tk = Tk()
    EE = B.rx([128])[:, 0, :]; EE_tk = Tk()
    Es = B.rx([128])[:, 0, :]; Es_tk = Tk()
    Et = B.rx([128])[:, 0, :]; Et_tk = Tk()
    Nf = B.rx([2, 128]); Nf_tk = Tk()
    Rr = B.rx([128])[:, 0, :]; Rr_tk = Tk()
    Rb, Rb_tk = t16(1)
    wT, wT_tk = t16(1)
    vn, vn_tk = t16(1)
    Sb, Sb_tk = t16(1)
    sq = B.rx([512], BF16)[:, 0, :]; sq_tk = Tk()
    rstd = B.rx([512])[:, 0, :]; rstd_tk = Tk()
    TRk, RBk, Ak, Bk = 0, 1, 2, 3
    A2, B2, C2, D2 = 4, 5, 6, 7
    ptr = psbf(B, TRk)
    for h in range(4):
        for which, dst, dst_tk in ((0, qf, qf_tk), (1, kf, kf_tk), (2, vf, vf_tk)):
            inproj_raw(B, 128, u, u_tk)
            conv4(B, u, u_tk, cc, cc_tk, pre + 'gdn_conv', 12, which * 4 + h, None, dst, dst_tk, AF.Silu)
        if d == 1:
            inproj_act(B, 128, zs, zs_tk, AF.Silu)
            yf_load(B, g * 4 + h, y, y_tk)
        S = B.st_gdn[:, h, :]
        S_tk = B.st_gdn_tk[h]
        for d in (d,):
            r = d * 4 + h
            state_init(B, d, S, S_tk)
            e_act(B, Sb[:, 0, :], S, AF.Copy, [S_tk], [Sb_tk])
            for b in range(NB):
                if d == 0:
                    sl = slice(b * 512, (b + 1) * 512)
                    srcs = [kf[:, sl], qf[:, sl], vf[:, sl]]
                else:
                    sl = slice(T - (b + 1) * 512, T - b * 512)
                    srcs = [kf[:, sl][:, ::-1], qf[:, sl][:, ::-1], vf[:, sl][:, ::-1]]
                for k, (src, t_) in enumerate(zip(srcs, (kf_tk, qf_tk, vf_tk))):
                    e_copy(B, OB[:, k, :], src, [t_], [OB_tk], eng=('vector' if k != 1 else 'gpsimd'))
                def stage1(i, sl_):
                    ti = b * 4 + i
                    tsl = slice(i * 128, (i + 1) * 128)
                    col = lambda A_: A_[:, ti, r:r + 1]
                    tkv, tkv_tk = tkvS[sl_]
                    vb_, vb_tk = vbS[sl_]
                    fm4, fm4_tk = fm4S[sl_]
                    Nf0, Nf0_tk = NfS[sl_]
                    qkT, qkT_tk = qkS[sl_]
                    for k in range(3):
                        e_tr(B, ptr[:, k * 128:(k + 1) * 128], OB[:, k, tsl], B.ident_b, [OB_tk, B.c_tk], [B.ps_tk[TRk]])
                    yield
                    e_copy(B, tok3.rearrange("p a b -> p (a b)"), ptr[:, 0:384], [B.ps_tk[TRk]], [tok3_tk])
                    yield
                    for k in range(2):
                        e_act(B, junk, tok3[:, k, :], AF.Square, [tok3_tk], [junk_tk, sc_tk], accum_out=sc[:, k:k + 1])
                    yield
                    e_act(B, sc[:, 0:2], sc[:, 0:2], AF.Sqrt, [sc_tk], [sc_tk], bias=EPS)
                    yield
                    B.P.op('vector', lambda e: e.reciprocal(out=sc[:, 0:2], in_=sc[:, 0:2]), reads=[sc_tk], writes=[sc_tk])
                    yield
                    e_tt(B, sc[:, 2:3], sc[:, 0:1], col(BETA), ALU.mult, [sc_tk, stk], [sc_tk])
                    e_tt(B, sc[:, 3:4], sc[:, 2:3], col(EG), ALU.mult, [sc_tk, stk], [sc_tk])
                    yield
                    e_tt(B, sc[:, 4:5], sc[:, 0:1], col(EGL), ALU.mult, [sc_tk, stk], [sc_tk])
                    e_ts(B, sc[:, 5:6], sc[:, 1:2], 128.0 ** -0.5, None, ALU.mult, None, [sc_tk], [sc_tk])
                    e_tt(B, sc[:, 6:7], sc[:, 5:6], col(EG), ALU.mult, [sc_tk, stk], [sc_tk])
                    yield
                    for j, (srcj, scj) in enumerate(((0, 0), (0, 2), (0, 3), (0, 4), (1, 5), (1, 6))):
                        e_act(B, tkv[:, j, :], tok3[:, srcj, :], AF.Copy, [tok3_tk, sc_tk], [tkv_tk], scale=sc[:, scj:scj + 1])
                        if j % 2 == 1:
                            yield
                    e_act(B, vb_[:, 0, :], tok3[:, 2, :], AF.Copy, [tok3_tk, stk], [vb_tk], scale=col(BETA))
                    for j, srcj in enumerate((0, 1, 4, 5)):
                        e_tr(B, ptr[:, j * 128:(j + 1) * 128], tkv[:, srcj, :], B.ident_b, [tkv_tk, B.c_tk], [B.ps_tk[TRk]])
                    yield
                    e_copy(B, fm4.rearrange("p a b -> p (a b)"), ptr[:, 0:512], [B.ps_tk[TRk]], [fm4_tk])
                    knT, kbT, qnT = fm4[:, 0, :], fm4[:, 1, :], fm4[:, 2, :]
                    gcs = col(GC)
                    e_mm(B, B.ps[:, RBk, 0:128], gcs.to_broadcast([128, 128]), B.ident_f, True, True, [stk, B.c_tk], [B.ps_tk[RBk]])
                    yield
                    e_ts(B, Dm, B.ps[:, RBk, 0:128], gcs, 0.0, ALU.subtract, ALU.min, [B.ps_tk[RBk], stk], [Dm_tk])
                    yield
                    e_act(B, EE, Dm, AF.Exp, [Dm_tk], [EE_tk])
                    e_mm(B, B.ps[:, Ak, 0:128], knT, kbT, True, True, [fm4_tk], [B.ps_tk[Ak]])
                    e_mm(B, B.ps[:, Bk, 0:128], knT, qnT, True, True, [fm4_tk], [B.ps_tk[Bk]])
                    yield
                    e_tt(B, Es, EE, B.striu_f, ALU.mult, [EE_tk, B.c_tk], [Es_tk])
                    e_tt(B, Et, EE, B.triu_f, ALU.mult, [EE_tk, B.c_tk], [Et_tk], eng='gpsimd')
                    yield
                    e_tt(B, Nf0[:, 0, :], B.ps[:, Ak, 0:128], Es, ALU.mult, [B.ps_tk[Ak], Es_tk], [Nf0_tk])
                    e_tt(B, qkT[:, 0, :], B.ps[:, Bk, 0:128], Et, ALU.mult, [B.ps_tk[Bk], Et_tk], [qkT_tk])

                def stage2(i, sl_):
                    ti = b * 4 + i
                    col = lambda A_: A_[:, ti, r:r + 1]
                    tkv, tkv_tk = tkvS[sl_]
                    vb_, vb_tk = vbS[sl_]
                    fm4, fm4_tk = fm4S[sl_]
                    Nf0, Nf0_tk = NfS[sl_]
                    qkT, qkT_tk = qkS[sl_]
                    qdT = fm4[:, 3, :]
                    e_tr(B, B.ps[:, C2, 0:128], Nf0[:, 0, :], B.ident_f, [Nf0_tk, B.c_tk], [B.ps_tk[C2]])
                    e_copy(B, Nf[:, 0, :], Nf0[:, 0, :], [Nf0_tk], [Nf_tk], eng='gpsimd')
                    e_tt(B, Rr, B.ident_f, Nf0[:, 0, :], ALU.subtract, [Nf0_tk, B.c_tk], [Rr_tk])
                    yield
                    e_copy(B, Nf[:, 1, :], B.ps[:, C2, 0:128], [B.ps_tk[C2]], [Nf_tk])
                    yield
                    for lev in range(6):
                        e_mm(B, B.ps[:, A2, 0:128], Nf[:, 1, :], Nf[:, 0, :], True, True, [Nf_tk], [B.ps_tk[A2]])
                        e_mm(B, B.ps[:, B2, 0:128], Nf[:, 0, :], Nf[:, 1, :], True, True, [Nf_tk], [B.ps_tk[B2]])
                        yield
                        e_copy(B, Nf[:, 0, :], B.ps[:, A2, 0:128], [B.ps_tk[A2]], [Nf_tk])
                        e_act(B, Nf[:, 1, :], B.ps[:, B2, 0:128], AF.Copy, [B.ps_tk[B2]], [Nf_tk])
                        yield
                        e_mm(B, B.ps[:, C2, 0:128], Nf[:, 1, :], Rr, True, True, [Nf_tk, Rr_tk], [B.ps_tk[C2]])
                        yield
                        e_tt(B, Rr, Rr, B.ps[:, C2, 0:128], ALU.add, [Rr_tk, B.ps_tk[C2]], [Rr_tk])
                        yield
                    e_copy(B, Rb[:, 0, :], Rr, [Rr_tk], [Rb_tk], eng='gpsimd')
                    X = Rb[:, 0, :]
                    yield
                    e_mm(B, B.ps[:, A2, 0:128], tkv[:, 2, :], X, True, True, [tkv_tk, Rb_tk], [B.ps_tk[A2]])
                    yield
                    e_act(B, wT[:, 0, :], B.ps[:, A2, 0:128], AF.Copy, [B.ps_tk[A2]], [wT_tk], scale=-1.0)
                    yield
                    e_mm(B, B.ps[:, B2, 0:128], X, vb_[:, 0, :], True, False, [Rb_tk, vb_tk], [B.ps_tk[B2]])
                    e_mm(B, B.ps[:, B2, 0:128], wT[:, 0, :], Sb[:, 0, :], False, True, [wT_tk, Sb_tk], [B.ps_tk[B2]])
                    yield
                    e_copy(B, vn[:, 0, :], B.ps[:, B2, 0:128], [B.ps_tk[B2]], [vn_tk])
                    yield
                    e_mm(B, B.ps[:, D2, 0:128], Sb[:, 0, :], qdT, True, False, [Sb_tk, fm4_tk], [B.ps_tk[D2]])
                    e_mm(B, B.ps[:, D2, 0:128], vn[:, 0, :], qkT[:, 0, :], False, True, [vn_tk, qkT_tk], [B.ps_tk[D2]])
                    e_mm(B, B.ps[:, C2, 0:128], tkv[:, 3, :], vn[:, 0, :], True, True, [tkv_tk, vn_tk], [B.ps_tk[C2]])
                    yield
                    e_stt(B, S, S, col(GLt), B.ps[:, C2, 0:128], ALU.mult, ALU.add, [S_tk, stk, B.ps_tk[C2]], [S_tk])
                    t0_ = ti * 128
                    if d == 0:
                        e_act(B, y[:, t0_:t0_ + 128], B.ps[:, D2, 0:128], AF.Copy, [B.ps_tk[D2]], [y_tk])
                    else:
                        yv = y[:, T - t0_ - 128:T - t0_][:, ::-1]
                        e_tt(B, yv, yv, B.ps[:, D2, 0:128], ALU.add, [B.ps_tk[D2], y_tk], [y_tk])
                    yield
                    e_act(B, Sb[:, 0, :], S, AF.Copy, [S_tk], [Sb_tk])

                lockstep([stage1(0, 0)])
                for i in range(4):
                    gens = [stage2(i, i % 2)]
                    if i + 1 < 4:
                        gens.append(stage1(i + 1, (i + 1) % 2))
                    lockstep(gens)
        if d == 0:
            yf_store(B, g * 4 + h, y, y_tk)
            continue
        for tb in range(NB):
            sl = slice(tb * 512, (tb + 1) * 512)
            bank = 4 + tb % 2
            e_act(B, sq, y[:, sl], AF.Square, [y_tk], [sq_tk])
            e_mm(B, B.ps[:, bank, :], B.ones_b, sq, True, True, [sq_tk, B.c_tk], [B.ps_tk[bank]])
            e_act(B, rstd, B.ps[:, bank, :], AF.Sqrt, [B.ps_tk[bank]], [rstd_tk], scale=1.0 / 128, bias=EPS)
            B.P.op('vector', lambda e: e.reciprocal(out=rstd, in_=rstd), reads=[rstd_tk], writes=[rstd_tk])
            e_stt(B, y[:, sl], y[:, sl], B.pc(pre + 'gdn_norm', 0), rstd, ALU.mult, ALU.mult, [y_tk, rstd_tk, B.c_tk], [y_tk])
            e_tt(B, ybf[:, h, sl], y[:, sl], zs[:, sl], ALU.mult, [y_tk, zs_tk], [ybf_tk])
    if d == 1:
        group_out(B, l, g, ybf, ybf_tk)


def tok_scalars(B, l, pre, col0, nr, dtb_name, alog_name, pfx):
    P, T, NT = B.P, B.T, B.NT
    nh = nr // 2
    W = NT * nr
    def buf():
        return B.rx([NT, nr])
    RAW, DTO, DA, ACS, ACSL = buf(), buf(), buf(), buf(), buf()
    abc = B.rx([nr])[:, 0, :]
    tk = Tk()
    wb, wb_tk = B.w_get()
    bank = 4
    for i in range(NT):
        for kc in range(KC):
            e_mm(B, B.ps[:, bank, i * nr:(i + 1) * nr], B.hT[:, kc, i * 128:(i + 1) * 128], wb[:, kc, 0:nr], kc == 0, kc == KC - 1,
                 [B.hT_tk[i // 4], wb_tk], [B.ps_tk[bank]])
    B.w_prefetch()
    flat = lambda a: a.rearrange("p a b -> p (a b)")
    e_tt(B, RAW, B.ps[:, bank, 0:W].rearrange("p (a b) -> p a b", b=nr), B.pc(pre + dtb_name, 0, nr).rearrange("p (a b) -> p a b", a=1).to_broadcast([128, NT, nr]),
         ALU.add, [B.ps_tk[bank], B.c_tk], [tk])
    e_act(B, flat(RAW), flat(RAW), AF.Exp, [tk], [tk])
    e_act(B, flat(RAW), flat(RAW), AF.Ln, [tk], [tk], bias=1.0)
    e_mm(B, B.ps[:, bank + 1, 0:W], B.J_f, flat(RAW), True, True, [tk, B.c_tk], [B.ps_tk[bank + 1]])
    e_copy(B, DTO[:, :, 0:nh], RAW[:, :, 0:nh], [tk], [tk])
    e_copy(B, DTO[:, :, nh:nr], B.ps[:, bank + 1, 0:W].rearrange("p (a b) -> p a b", b=nr)[:, ::-1, nh:nr], [B.ps_tk[bank + 1], tk], [tk])
    e_act(B, abc, B.pc(pre + alog_name, 0, nr), AF.Exp, [B.c_tk], [tk])
    e_ts(B, abc, abc, -1.0, None, ALU.mult, None, [tk], [tk])
    e_tt(B, DA, DTO, abc.rearrange("p (a b) -> p a b", a=1).to_broadcast([128, NT, nr]), ALU.mult, [tk], [tk])
    e_mm(B, B.ps[:, bank, 0:W], B.triu_f, flat(DA), True, True, [tk, B.c_tk], [B.ps_tk[bank]])
    e_copy(B, flat(ACS), B.ps[:, bank, 0:W], [B.ps_tk[bank]], [tk])
    e_mm(B, B.ps[:, bank + 1, 0:W], B.ones_f, flat(DA), True, True, [tk, B.c_tk], [B.ps_tk[bank + 1]])
    e_copy(B, flat(ACSL), B.ps[:, bank + 1, 0:W], [B.ps_tk[bank + 1]], [tk])
    return dict(DT=DTO, DA=DA, ACS=ACS, ACSL=ACSL, RAW=RAW, tk=tk)


def mixer_ssd(B, l, g, d):
    P, T, NB, NT = B.P, B.T, B.NB, B.NT
    pre = 'L%d_' % l
    Z0, X0, B0, C0, DT0 = 3088, 3600, 4112, 4368, 4624
    ybf = B.rx([4, T], BF16); ybf_tk = Tk()
    specs = [win_spec(B, l, DT0, 16)]
    for grp in range(2):
        specs += [win_spec(B, l, X0 + (2 * grp) * 128, 128), win_spec(B, l, X0 + (2 * grp + 1) * 128, 128),
                  win_spec(B, l, B0 + grp * 128, 128), win_spec(B, l, C0 + grp * 128, 128)]
        if d == 1:
            specs += [win_spec(B, l, Z0 + (2 * grp) * 128, 128), win_spec(B, l, Z0 + (2 * grp + 1) * 128, 128)]
    B.w_begin(specs)
    ts_ = tok_scalars(B, l, pre, DT0, 16, 'ssd_dtb_bc', 'ssd_alog_bc', 'ssd')
    DT, ACS, ACSL, stk = ts_['DT'], ts_['ACS'], ts_['ACSL'], ts_['tk']
    GLt = ts_['DA']
    Wt = ts_['RAW']
    flat = lambda a: a.rearrange("p a b -> p (a b)")
    e_tt(B, Wt, ACSL, ACS, ALU.subtract, [stk], [stk])
    e_act(B, flat(Wt), flat(Wt), AF.Exp, [stk], [stk])
    e_tt(B, Wt, Wt, DT, ALU.mult, [stk], [stk])
    e_act(B, flat(GLt), flat(ACSL), AF.Exp, [stk], [stk])
    y = B.rx([2, T]); y_tk = Tk()
    xsT = B.rx([2, T], BF16); xs_tk = Tk()
    BT = B.rx([T], BF16)[:, 0, :]; BT_tk = Tk()
    CT = B.rx([T], BF16)[:, 0, :]; CT_tk = Tk()
    u, u_tk = alloc_u(B)
    cc = B.rx([T])[:, 0, :]; cc_tk = Tk()
    OB = B.rx([4, 512], BF16); OB_tk = Tk()
    xtok = B.rx([4, 256], BF16); xtok_tk = Tk()
    btok = B.rx([4, 128], BF16); btok_tk = Tk()
    GmT = B.rx([128])[:, 0, :]; GmT_tk = Tk()
    Dm = B.rx([128])[:, 0, :]; Dm_tk = Tk()
    EE = B.rx([128])[:, 0, :]; EE_tk = Tk()
    MT = B.rx([128], BF16)[:, 0, :]; MT_tk = Tk()
    E2 = B.rx([128])[:, 0, :]; E2_tk = Tk()
    Cd = B.rx([128], BF16)[:, 0, :]; Cd_tk = Tk()
    xdt = B.rx([64], BF16)[:, 0, :]; xdt_tk = Tk()
    xw = B.rx([64], BF16)[:, 0, :]; xw_tk = Tk()
    STb = B.rx([4, 64], BF16); STb_tk = [Tk() for _ in range(4)]
    zs = B.rx([T], BF16)[:, 0, :]; zs_tk = Tk()
    sq = B.rx([512], BF16)[:, 0, :]; sq_tk = Tk()
    rstd = B.rx([512])[:, 0, :]; rstd_tk = Tk()
    RBk, GBk, YBk, SUk, TXk, TBk = 0, 1, 2, 3, 6, 7
    for grp in range(2):
        for cl in range(2):
            ch = 2 * grp + cl
            inproj_raw(B, 128, u, u_tk)
            if d == 0:
                conv4(B, u, u_tk, cc, cc_tk, pre + 'ssd_conv', 8, ch, B.pc(pre + 'ssd_convb', ch), y[:, cl, :], y_tk, AF.Silu)
                e_copy(B, xsT[:, cl, :], y[:, cl, :], [y_tk], [xs_tk], eng='gpsimd')
                e_ts(B, y[:, cl, :], y[:, cl, :], B.pc(pre + 'ssd_d', ch), None, ALU.mult, None, [y_tk, xs_tk, B.c_tk], [y_tk])
            else:
                conv4(B, u, u_tk, cc, cc_tk, pre + 'ssd_conv', 8, ch, B.pc(pre + 'ssd_convb', ch), xsT[:, cl, :], xs_tk, AF.Silu)
                yf_load(B, g * 4 + ch, y[:, cl, :], y_tk)
        inproj_raw(B, 128, u, u_tk)
        conv4(B, u, u_tk, cc, cc_tk, pre + 'ssd_conv', 8, 4 + grp, B.pc(pre + 'ssd_convb', 4 + grp), BT, BT_tk, AF.Silu)
        inproj_raw(B, 128, u, u_tk)
        conv4(B, u, u_tk, cc, cc_tk, pre + 'ssd_conv', 8, 6 + grp, B.pc(pre + 'ssd_convb', 6 + grp), CT, CT_tk, AF.Silu)
        ST = B.st_ssd[:, grp * 4:(grp + 1) * 4, :]
        ST_tk = B.st_ssd_tk[grp * 4:(grp + 1) * 4]
        for d in (d,):
            for hh in range(4):
                state_init(B, d, ST[:, hh, :], ST_tk[hh])
                e_act(B, STb[:, hh, :], ST[:, hh, :], AF.Copy, [ST_tk[hh]], [STb_tk[hh]])
            for b in range(NB):
                if d == 0:
                    sl = slice(b * 512, (b + 1) * 512)
                    srcs = [xsT[:, 0, sl], xsT[:, 1, sl], BT[:, sl], CT[:, sl]]
                else:
                    sl = slice(T - (b + 1) * 512, T - b * 512)
                    srcs = [xsT[:, 0, sl][:, ::-1], xsT[:, 1, sl][:, ::-1], BT[:, sl][:, ::-1], CT[:, sl][:, ::-1]]
                for k, (src, stk_) in enumerate(zip(srcs, (xs_tk, xs_tk, BT_tk, CT_tk))):
                    e_copy(B, OB[:, k, :], src, [stk_], [OB_tk], eng=('vector' if k % 2 == 0 else 'gpsimd'))
                px, pb = psbf(B, TXk), psbf(B, TBk)
                for i in range(4):
                    for cl in range(2):
                        e_tr(B, px[:, (i * 2 + cl) * 128:(i * 2 + cl + 1) * 128], OB[:, cl, i * 128:(i + 1) * 128], B.ident_b, [OB_tk, B.c_tk], [B.ps_tk[TXk]])
                    e_tr(B, pb[:, i * 128:(i + 1) * 128], OB[:, 2, i * 128:(i + 1) * 128], B.ident_b, [OB_tk, B.c_tk], [B.ps_tk[TBk]])
                e_copy(B, xtok.rearrange("p a b -> p (a b)"), px[:, 0:1024], [B.ps_tk[TXk]], [xtok_tk])
                e_act(B, btok.rearrange("p a b -> p (a b)"), pb[:, 0:512], AF.Copy, [B.ps_tk[TBk]], [btok_tk])
                for i in range(4):
                    ti = b * 4 + i
                    tsl = slice(i * 128, (i + 1) * 128)
                    e_mm(B, B.ps[:, GBk, 0:128], OB[:, 2, tsl], OB[:, 3, tsl], True, True, [OB_tk], [B.ps_tk[GBk]])
                    e_tt(B, GmT, B.ps[:, GBk, 0:128], B.triu_f, ALU.mult, [B.ps_tk[GBk], B.c_tk], [GmT_tk])
                    for hh in range(4):
                        r = d * 8 + grp * 4 + hh
                        cl = hh // 2
                        po = (hh % 2) * 64
                        acs = ACS[:, ti, r:r + 1]
                        e_mm(B, B.ps[:, RBk, 0:128], acs.to_broadcast([128, 128]), B.ident_f, True, True, [stk, B.c_tk], [B.ps_tk[RBk]])
                        e_ts(B, Dm, B.ps[:, RBk, 0:128], acs, 0.0, ALU.subtract, ALU.min, [B.ps_tk[RBk], stk], [Dm_tk])
                        e_act(B, EE, Dm, AF.Exp, [Dm_tk], [EE_tk])
                        e_tt(B, MT, EE, GmT, ALU.mult, [EE_tk, GmT_tk], [MT_tk])
                        e_act(B, E2, B.ps[:, RBk, 0:128], AF.Exp, [B.ps_tk[RBk]], [E2_tk])
                        e_tt(B, Cd, OB[:, 3, tsl], E2, ALU.mult, [OB_tk, E2_tk], [Cd_tk], eng='gpsimd')
                        xs_h = xtok[:, i, cl * 128 + po:cl * 128 + po + 64]
                        e_act(B, xdt, xs_h, AF.Copy, [xtok_tk, stk], [xdt_tk], scale=DT[:, ti, r:r + 1])
                        e_act(B, xw, xs_h, AF.Copy, [xtok_tk, stk], [xw_tk], scale=Wt[:, ti, r:r + 1])
                        yo = B.ps[po:po + 64, YBk, cl * 128:(cl + 1) * 128]
                        e_mm(B, yo, xdt, MT, True, False, [xdt_tk, MT_tk], [B.ps_tk[YBk]])
                        e_mm(B, yo, STb[:, hh, :], Cd, False, True, [STb_tk[hh], Cd_tk], [B.ps_tk[YBk]])
                        e_mm(B, B.ps[:, SUk, 0:64], btok[:, i, :], xw, True, True, [btok_tk, xw_tk], [B.ps_tk[SUk]])
                        e_stt(B, ST[:, hh, :], ST[:, hh, :], GLt[:, ti, r:r + 1], B.ps[:, SUk, 0:64], ALU.mult, ALU.add, [ST_tk[hh], stk, B.ps_tk[SUk]], [ST_tk[hh]])
                        e_act(B, STb[:, hh, :], ST[:, hh, :], AF.Copy, [ST_tk[hh]], [STb_tk[hh]])
                    for cl in range(2):
                        t0 = ti * 128
                        if d == 0:
                            yv = y[:, cl, t0:t0 + 128]
                        else:
                            yv = y[:, cl, T - t0 - 128:T - t0][:, ::-1]
                        e_tt(B, yv, yv, B.ps[:, YBk, cl * 128:(cl + 1) * 128], ALU.add, [B.ps_tk[YBk], y_tk], [y_tk])
        if d == 0:
            for cl in range(2):
                yf_store(B, g * 4 + 2 * grp + cl, y[:, cl, :], y_tk)
            continue
        for cl in range(2):
            inproj_act(B, 128, zs, zs_tk, AF.Silu)
            e_tt(B, y[:, cl, :], y[:, cl, :], zs, ALU.mult, [y_tk, zs_tk], [y_tk])
        for tb in range(NB):
            sl = slice(tb * 512, (tb + 1) * 512)
            bank = 4 + tb % 2
            for cl in range(2):
                e_act(B, sq, y[:, cl, sl], AF.Square, [y_tk], [sq_tk])
                e_mm(B, B.ps[:, bank, :], B.ones_b, sq, cl == 0, cl == 1, [sq_tk, B.c_tk], [B.ps_tk[bank]])
            e_act(B, rstd, B.ps[:, bank, :], AF.Sqrt, [B.ps_tk[bank]], [rstd_tk], scale=1.0 / 256, bias=EPS)
            B.P.op('vector', lambda e: e.reciprocal(out=rstd, in_=rstd), reads=[rstd_tk], writes=[rstd_tk])
            for cl in range(2):
                e_stt(B, ybf[:, 2 * grp + cl, sl], y[:, cl, sl], B.pc(pre + 'ssd_norm', 2 * grp + cl), rstd, ALU.mult, ALU.mult, [y_tk, rstd_tk, B.c_tk], [ybf_tk])
    if d == 1:
        group_out(B, l, g, ybf, ybf_tk)


def psbf(B, bank):
    return B.ps[:, bank, :].bitcast(BF16)


def mixer_hgrn(B, l, g, d):
    P, T, NB = B.P, B.T, B.NB
    pre = 'L%d_' % l
    Q0, F0c, I0, G0 = 4640, 5152, 6176, 6688
    C = 32
    NCB = 512 // C
    ybf = B.rx([4, T], BF16); ybf_tk = Tk()
    yh = B.rx([T])[:, 0, :]; yh_tk = Tk()
    qb = B.rx([T], BF16)[:, 0, :]; qb_tk = Tk()
    vb = B.rx([T], BF16)[:, 0, :]; vb_tk = Tk()
    qbr = B.rx([T], BF16)[:, 0, :]; qbr_tk = Tk()
    vbr = B.rx([T], BF16)[:, 0, :]; vbr_tk = Tk()
    sg = B.rx([T], BF16)[:, 0, :]; sg_tk = Tk()
    Fs = [B.rx([T])[:, 0, :] for _ in range(2)]; F_tk = [Tk() for _ in range(2)]
    LF = B.rx([512])[:, 0, :]; LF_tk = Tk()
    BC = B.rx([512])[:, 0, :]; BC_tk = Tk()
    KK = B.rx([512])[:, 0, :]; KK_tk = Tk()
    DF = B.rx([512])[:, 0, :]; DF_tk = Tk()
    EE = B.rx([512])[:, 0, :]; EE_tk = Tk()
    QT = B.rx([512], BF16)[:, 0, :]; QT_tk = Tk()
    KT = B.rx([512], BF16)[:, 0, :]; KT_tk = Tk()
    KD = B.rx([512], BF16)[:, 0, :]; KD_tk = Tk()
    QD = B.rx([512])[:, 0, :]; QD_tk = Tk()
    KDT = B.rx([8, 128], BF16); KDT_tk = Tk()
    VT = B.rx([8, 128], BF16); VT_tk = Tk()
    sTm = B.rx([128], BF16)[:, 0, :]; sTm_tk = Tk()
    BS = B.rx([NCB])[:, 0, :]; BS_tk = Tk()
    GL = B.rx([NCB])[:, 0, :]; GL_tk = Tk()
    ones = B.rx([512])[:, 0, :]; ones_tk = Tk()
    lbv = B.rx([16])[:, 0, :]; lb_tk = Tk()
    zb = B.rx([128], BF16)[:, 0, :]; zb_tk = Tk()
    sq = B.rx([512], BF16)[:, 0, :]; sq_tk = Tk()
    rstd = B.rx([512])[:, 0, :]; rstd_tk = Tk()
    e_memset(B, ones, 1.0, [ones_tk])
    e_memset(B, zb, 0.0, [zb_tk])
    if l == 0:
        e_memset(B, lbv[:, 0:8], 0.0, [lb_tk])
    else:
        e_tt(B, lbv[:, 0:8], B.pc(pre + 'hg_lb1', 0, 8), B.pc(pre + 'hg_lb0', 0, 8), ALU.subtract, [B.c_tk], [lb_tk])
        e_act(B, lbv[:, 0:8], lbv[:, 0:8], AF.Sigmoid, [lb_tk], [lb_tk])
    e_ts(B, lbv[:, 8:16], lbv[:, 0:8], -1.0, 1.0, ALU.mult, ALU.add, [lb_tk], [lb_tk])
    SB_, YB_ = 6, 7
    e_mm(B, B.ps[:, SB_, 0:128], zb, zb, True, True, [zb_tk], [B.ps_tk[SB_]])
    specs = []
    for hd in range(4):
        for c0 in ((Q0, I0, F0c) if d == 0 else (Q0, I0, G0, F0c + 512)):
            specs.append(win_spec(B, l, c0 + hd * 128, 128))
    B.w_begin(specs)

    def rev_sl(sl):
        return slice(T - sl.stop, T - sl.start)

    for hd in range(4):
        inproj_act(B, 128, qb, qb_tk, AF.Silu)
        inproj_act(B, 128, vb, vb_tk, AF.Copy)
        if d == 0:
            inproj_act(B, 128, Fs[0], F_tk[0], AF.Sigmoid)
        else:
            inproj_act(B, 128, sg, sg_tk, AF.Silu)
            def evac_r(tb, ps_ap, ps_tk):
                sl = rev_sl(slice(tb * 512, (tb + 1) * 512))
                e_act(B, Fs[1][:, sl][:, ::-1], ps_ap, AF.Sigmoid, [ps_tk], [F_tk[1]])
            inproj(B, 128, evac_r)
            e_copy(B, qbr, qb[:, ::-1], [qb_tk], [qbr_tk])
            e_copy(B, vbr, vb[:, ::-1], [vb_tk], [vbr_tk])
            yf_load(B, g * 4 + hd, yh, yh_tk)
        S = B.st_hg[:, hd, :]
        S_tk = B.st_hg_tk[hd]
        for d in (d,):
            k8 = d * 4 + hd
            F, Ftk = Fs[d], F_tk[d]
            e_ts(B, F, F, lbv[:, 8 + k8:9 + k8], lbv[:, k8:k8 + 1], ALU.mult, ALU.add, [Ftk, lb_tk], [Ftk])
            Q, Qtk = (qb, qb_tk) if d == 0 else (qbr, qbr_tk)
            V, Vtk = (vb, vb_tk) if d == 0 else (vbr, vbr_tk)
            state_init(B, d, S, S_tk)
            for b in range(NB):
                sl = slice(b * 512, (b + 1) * 512)
                e_act(B, LF, F[:, sl], AF.Ln, [Ftk], [LF_tk])
                e_ts(B, KK, F[:, sl], -1.0, 1.0, ALU.mult, ALU.add, [Ftk], [KK_tk])
                e_scan(B, BC, ones, LF, [ones_tk, LF_tk], [BC_tk])
                BC3 = BC.rearrange("p (n c) -> p n c", c=C)
                DF3 = DF.rearrange("p (n c) -> p n c", c=C)
                sh = [128, NCB, C]
                e_memset(B, BS[:, 0:1], 0.0, [BS_tk])
                e_copy(B, BS[:, 1:NCB], BC3[:, 0:NCB - 1, C - 1], [BC_tk], [BS_tk])
                e_tt(B, GL, BC3[:, :, C - 1], BS, ALU.subtract, [BC_tk, BS_tk], [GL_tk])
                e_act(B, GL, GL, AF.Exp, [GL_tk], [GL_tk])
                e_tt(B, DF3, BC3, BC3[:, :, C // 2 - 1:C // 2].to_broadcast(sh), ALU.subtract, [BC_tk], [DF_tk])
                e_act(B, EE, DF, AF.Exp, [DF_tk], [EE_tk])
                e_tt(B, QT, Q[:, sl], EE, ALU.mult, [Qtk, EE_tk], [QT_tk])
                e_act(B, EE, DF, AF.Exp, [DF_tk], [EE_tk], scale=-1.0)
                e_tt(B, KT, KK, EE, ALU.mult, [KK_tk, EE_tk], [KT_tk])
                e_tt(B, DF3, BC3, BS.rearrange("p (n o) -> p n o", o=1).to_broadcast(sh), ALU.subtract, [BC_tk, BS_tk], [DF_tk])
                e_act(B, EE, DF, AF.Exp, [DF_tk], [EE_tk])
                e_tt(B, QD, Q[:, sl], EE, ALU.mult, [Qtk, EE_tk], [QD_tk])
                e_tt(B, DF3, BC3[:, :, C - 1:C].to_broadcast(sh), BC3, ALU.subtract, [BC_tk], [DF_tk])
                e_act(B, EE, DF, AF.Exp, [DF_tk], [EE_tk])
                e_tt(B, KD, KK, EE, ALU.mult, [KK_tk, EE_tk], [KD_tk])
                pk, pv = psbf(B, 2), psbf(B, 3)
                for i in range(8):
                    e_tr(B, pk[0:64, i * 128:(i + 1) * 128], KD[:, i * 64:(i + 1) * 64], B.ident_b, [KD_tk, B.c_tk], [B.ps_tk[2]])
                    e_tr(B, pv[0:64, i * 128:(i + 1) * 128], V[:, sl][:, i * 64:(i + 1) * 64], B.ident_b, [Vtk, B.c_tk], [B.ps_tk[3]])
                e_copy(B, KDT[0:64].rearrange("p a b -> p (a b)"), pk[0:64, 0:1024], [B.ps_tk[2]], [KDT_tk])
                e_act(B, VT[0:64].rearrange("p a b -> p (a b)"), pv[0:64, 0:1024], AF.Copy, [B.ps_tk[3]], [VT_tk])
                for i in range(8):
                    for c in range(2):
                        cs = slice(i * 64 + c * C, i * 64 + (c + 1) * C)
                        e_mm(B, B.ps[c * C:(c + 1) * C, SB_, c * C:(c + 1) * C], KT[:, cs], QT[:, cs], True, True, [KT_tk, QT_tk], [B.ps_tk[SB_]])
                    e_tt(B, sTm[0:64, 0:64], B.ps[0:64, SB_, 0:64], B.blk32_f[0:64, 0:64], ALU.mult, [B.ps_tk[SB_], B.c_tk], [sTm_tk])
                    e_mm(B, B.ps[:, YB_, 0:64], VT[0:64, i, :], sTm[0:64, 0:64], True, False, [VT_tk, sTm_tk], [B.ps_tk[YB_]])
                    for c in range(2):
                        n = i * 2 + c
                        cs = slice(i * 64 + c * C, i * 64 + (c + 1) * C)
                        sub = n % 2
                        e_mm(B, B.ps[:, sub, 0:128], KDT[c * C:(c + 1) * C, i, :], VT[c * C:(c + 1) * C, i, :], True, True, [KDT_tk, VT_tk], [B.ps_tk[sub]])
                        e_mm(B, B.ps[:, YB_, c * C:(c + 1) * C], S, QD[:, cs], False, (c == 1), [S_tk, QD_tk], [B.ps_tk[YB_]])
                        e_stt(B, S, S, GL[:, n:n + 1], B.ps[:, sub, 0:128], ALU.mult, ALU.add, [S_tk, GL_tk, B.ps_tk[sub]], [S_tk])
                    t0 = b * 512 + i * 64
                    if d == 0:
                        e_act(B, yh[:, t0:t0 + 64], B.ps[:, YB_, 0:64], AF.Copy, [B.ps_tk[YB_]], [yh_tk])
                    else:
                        yv = yh[:, T - t0 - 64:T - t0][:, ::-1]
                        e_tt(B, yv, yv, B.ps[:, YB_, 0:64], ALU.add, [B.ps_tk[YB_], yh_tk], [yh_tk])
        if d == 0:
            yf_store(B, g * 4 + hd, yh, yh_tk)
            continue
        for tb in range(NB):
            sl = slice(tb * 512, (tb + 1) * 512)
            bank = 4 + tb % 2
            e_act(B, sq, yh[:, sl], AF.Square, [yh_tk], [sq_tk])
            e_mm(B, B.ps[:, bank, :], B.ones_b, sq, True, True, [sq_tk, B.c_tk], [B.ps_tk[bank]])
            e_act(B, rstd, B.ps[:, bank, :], AF.Sqrt, [B.ps_tk[bank]], [rstd_tk], scale=1.0 / 128, bias=EPS)
            B.P.op('vector', lambda e: e.reciprocal(out=rstd, in_=rstd), reads=[rstd_tk], writes=[rstd_tk])
            e_stt(B, yh[:, sl], yh[:, sl], B.pc(pre + 'hg_norm', 0), rstd, ALU.mult, ALU.mult, [yh_tk, rstd_tk, B.c_tk], [yh_tk])
            e_tt(B, ybf[:, hd, sl], yh[:, sl], sg[:, sl], ALU.mult, [yh_tk, sg_tk], [ybf_tk])
    if d == 1:
        group_out(B, l, g, ybf, ybf_tk)


def make_consts():
    c = np.zeros((128, 768), np.float32)
    c[:, 640:768] = np.eye(128)[::-1]
    bm = np.zeros((128, 128), np.float32)
    for a in range(4):
        bm[a * 32:(a + 1) * 32, a * 32:(a + 1) * 32] = np.triu(np.ones((32, 32)))
    c[:, 512:640] = bm
    c[:, 0:128] = np.eye(128)
    c[:, 128:256] = np.triu(np.ones((128, 128)))
    c[:, 256:384] = np.triu(np.ones((128, 128)), 1)
    c[:, 384:512] = 1.0
    return c


def host_inputs(inp):
    params, _ = pack_params(inp)
    lw = np.stack([np.asarray(inp['lru_wa']), np.asarray(inp['lru_wi'])], axis=1)
    L = lw.shape[0]
    lru_w = np.ascontiguousarray(lw.reshape(L, 16, 128, 128).transpose(0, 2, 1, 3).reshape(L, 128, 16 * 128))
    return {
        'w_in': np.ascontiguousarray(inp['w_in'], dtype=np.float32),
        'w_out': np.ascontiguousarray(inp['w_out'], dtype=np.float32),
        'w_up': np.ascontiguousarray(inp['w_up'], dtype=np.float32),
        'w_down': np.ascontiguousarray(inp['w_down'], dtype=np.float32),
        'lru_w': lru_w,
        'params': params,
        'consts': make_consts(),
    }


_CACHE = {}


def get_nc(T, **kw):
    key = (T, tuple(sorted(kw.items())))
    if key not in _CACHE:
        _CACHE[key] = Builder(T, **kw).build()
    return _CACHE[key]


def make_flags(carry):
    f = np.zeros((128, 4), np.float32)
    f[:, 0] = 1.0 if carry else 0.0
    f[:, 1] = 0.0 if carry else 1.0
    return f


def kernel(**inputs):
    inp = {k: np.asarray(v) for k, v in inputs.items()}
    xp = inp['x_prompt']
    xsm = inp['x_sample']
    T, NSEG = 2048, 4
    shared = host_inputs(inp)
    nc = get_nc(T, nseg=NSEG)
    x_prompt_T = np.ascontiguousarray(xp[0].T)
    x_sample_T = np.ascontiguousarray(xsm.reshape(NSEG * T, D).T)
    in_maps = []
    for c in range(8):
        m = dict(shared)
        if c == 0:
            m['xT'] = x_prompt_T
            m['flags'] = make_flags(True)
        else:
            m['xT'] = x_sample_T
            m['flags'] = make_flags(False)
        in_maps.append(m)
    res = run_bass_kernel_spmd(nc, in_maps, core_ids=list(range(8)))
    y_prompt = np.asarray(res.results[0]['yT'], dtype=np.float32).T[None]
    y_sample = np.asarray(res.results[1]['yT'], dtype=np.float32).T.reshape(NSEG, T, D)
    return (np.ascontiguousarray(y_prompt), np.ascontiguousarray(y_sample))
```

```python
import numpy as np
import ml_dtypes
from contextlib import ExitStack
import concourse.bass as bass
import concourse.mybir as mybir
from concourse.bass_utils import run_bass_kernel_spmd

F32 = mybir.dt.float32
BF16 = mybir.dt.bfloat16
AF = mybir.ActivationFunctionType
ALU = mybir.AluOpType
AX = mybir.AxisListType

ENGS = ('sync', 'scalar', 'vector', 'gpsimd', 'tensor')

D = 2048
KC = 16
DIN = 7200
DFF = 5632
NFC = 44
FFQ = 4
EPS = 1e-6
DEPTH = 2


class Tk:
    __slots__ = ('w', 'r', 'sem', 'cnt', 'name')

    def __init__(self, name=''):
        self.w = None
        self.r = {}
        self.sem = None
        self.cnt = 0
        self.name = name


class Prog:
    def __init__(self, nc, es, same_engine_sync=True):
        self.nc = nc
        self.es = es
        self.q = {e: [] for e in ENGS}
        self.sem = {e: es.enter_context(nc.semaphore('s_' + e)) for e in ENGS}
        self.cnt = {e: 0 for e in ENGS}
        self.waited = {e: {} for e in ENGS}
        self.semobj = {}
        self.same_engine_sync = same_engine_sync
        self.dma_tks = []
        self.pool = []
        self.nsem = len(ENGS)

    def _deps(self, e, reads, writes):
        deps = {}

        def add(s, v):
            k = id(s)
            self.semobj[k] = s
            if deps.get(k, 0) < v:
                deps[k] = v
        for t in reads:
            if t.w is not None:
                add(*t.w)
        for t in writes:
            if t.w is not None:
                add(*t.w)
            for k, (s_, v_) in t.r.items():
                add(s_, v_)
        waits = []
        own = id(self.sem[e])
        for k, v in deps.items():
            if k == own and (e == 'tensor' or not self.same_engine_sync):
                continue
            if self.waited[e].get(k, 0) < v:
                self.waited[e][k] = v
                waits.append((self.semobj[k], v))
        return waits

    def _mark(self, d, reads, writes):
        for t in writes:
            t.w = d
            t.r = {}
        for t in reads:
            k = id(d[0])
            if k not in t.r or t.r[k][1] < d[1]:
                t.r[k] = d

    def op(self, e, fn, reads=(), writes=()):
        waits = self._deps(e, reads, writes)
        self.cnt[e] += 1
        self.q[e].append((waits, fn, (self.sem[e], 1)))
        self._mark((self.sem[e], self.cnt[e]), reads, writes)

    def dma(self, e, out_ap, in_ap, reads=(), writes=(), dtk=None, **kw):
        waits = self._deps(e, reads, writes)
        tk = dtk
        if tk.sem is None:
            if self.pool:
                tk.sem, tk.cnt = self.pool.pop()
            else:
                tk.sem = self.es.enter_context(self.nc.semaphore())
                tk.cnt = 0
                self.nsem += 1
            self.dma_tks.append(tk)
        tk.cnt += 16
        self.q[e].append((waits, lambda eng: eng.dma_start(out=out_ap, in_=in_ap, **kw), (tk.sem, 16)))
        self._mark((tk.sem, tk.cnt), reads, writes)

    def barrier(self, final=False):
        for e in ENGS:
            waits = []
            for e2 in ENGS:
                if e2 != e and self.cnt[e2] > 0 and self.waited[e].get(id(self.sem[e2]), 0) < self.cnt[e2]:
                    self.waited[e][id(self.sem[e2])] = self.cnt[e2]
                    waits.append((self.sem[e2], self.cnt[e2]))
            for tk in self.dma_tks:
                if self.waited[e].get(id(tk.sem), 0) < tk.cnt:
                    self.waited[e][id(tk.sem)] = tk.cnt
                    waits.append((tk.sem, tk.cnt))
            if waits:
                self.q[e].append((waits, None, None))
        for tk in self.dma_tks:
            self.pool.append((tk.sem, tk.cnt))
            tk.sem = None
        self.dma_tks = []

    def replay(self):
        nc = self.nc
        with nc.Block() as block:
            def mk(e):
                def body(eng):
                    for waits, fn, inc in self.q[e]:
                        for s, v in waits:
                            eng.wait_ge(s, v)
                        if fn is not None:
                            fn(eng).then_inc(*inc)
                return body
            block.sync(mk('sync'))
            block.scalar(mk('scalar'))
            block.vector(mk('vector'))
            block.gpsimd(mk('gpsimd'))
            block.tensor(mk('tensor'))


class PL:
    def __init__(self):
        self.off = {}
        self.n = 0

    def add(self, name, ncols):
        self.off[name] = self.n
        self.n += ncols


def param_layout():
    pl = PL()
    for l in range(DEPTH):
        p = 'L%d_' % l
        pl.add(p + 'ln1', 16)
        pl.add(p + 'ln2', 16)
        pl.add(p + 'gdn_conv', 48)
        pl.add(p + 'gdn_alog', 1)
        pl.add(p + 'gdn_dtb', 1)
        pl.add(p + 'gdn_norm', 1)
        pl.add(p + 'lru_conv', 16)
        pl.add(p + 'lru_convb', 4)
        pl.add(p + 'lru_ba', 8)
        pl.add(p + 'lru_bi', 8)
        pl.add(p + 'lru_lam', 8)
        pl.add(p + 'ssd_conv', 32)
        pl.add(p + 'ssd_convb', 8)
        pl.add(p + 'ssd_alog', 1)
        pl.add(p + 'ssd_dtb', 1)
        pl.add(p + 'ssd_d', 4)
        pl.add(p + 'ssd_dtb_bc', 16)
        pl.add(p + 'ssd_alog_bc', 16)
        pl.add(p + 'gdn_dtb_bc', 8)
        pl.add(p + 'gdn_alog_bc', 8)
        pl.add(p + 'ssd_norm', 4)
        pl.add(p + 'hg_lb0', 8)
        pl.add(p + 'hg_lb1', 8)
        pl.add(p + 'hg_norm', 1)
        pl.add(p + 'gn', 16)
        pl.add(p + 'ffn_conv', 264)
        pl.add(p + 'ffn_convb', 88)
    pl.add('final', 16)
    return pl


def _cols(v):
    v = np.asarray(v, np.float32).reshape(-1, 128)
    return np.ascontiguousarray(v.T)


def _rows(v):
    v = np.asarray(v, np.float32).reshape(-1)
    o = np.zeros((128, 1), np.float32)
    o[:v.size, 0] = v
    return o


def pack_params(inp):
    pl = param_layout()
    P = np.zeros((128, pl.n), np.float32)

    def put(name, arr):
        o = pl.off[name]
        P[:, o:o + arr.shape[1]] = arr
    for l in range(DEPTH):
        p = 'L%d_' % l
        put(p + 'ln1', _cols(inp['ln1'][l]))
        put(p + 'ln2', _cols(inp['ln2'][l]))
        put(p + 'gdn_conv', np.concatenate([_cols(inp['gdn_conv_w'][l, t]) for t in range(4)], 1))
        put(p + 'gdn_alog', _rows(inp['gdn_a_log'][l]))
        put(p + 'gdn_dtb', _rows(inp['gdn_dt_bias'][l]))
        put(p + 'gdn_norm', _cols(inp['gdn_norm_w'][l]))
        put(p + 'lru_conv', np.concatenate([_cols(inp['lru_conv_w'][l, t]) for t in range(4)], 1))
        put(p + 'lru_convb', _cols(inp['lru_conv_b'][l]))
        put(p + 'lru_ba', _cols(inp['lru_ba'][l]))
        put(p + 'lru_bi', _cols(inp['lru_bi'][l]))
        put(p + 'lru_lam', _cols(inp['lru_lambda'][l]))
        put(p + 'ssd_conv', np.concatenate([_cols(inp['ssd_conv_w'][l, t]) for t in range(4)], 1))
        put(p + 'ssd_convb', _cols(inp['ssd_conv_b'][l]))
        put(p + 'ssd_alog', _rows(inp['ssd_a_log'][l]))
        put(p + 'ssd_dtb', _rows(inp['ssd_dt_bias'][l]))
        put(p + 'ssd_d', _cols(np.repeat(np.asarray(inp['ssd_d'][l]), 64)))
        put(p + 'ssd_norm', _cols(inp['ssd_norm_w'][l]))
        put(p + 'ssd_dtb_bc', np.broadcast_to(np.asarray(inp['ssd_dt_bias'][l], np.float32).reshape(1, 16), (128, 16)))
        put(p + 'ssd_alog_bc', np.broadcast_to(np.asarray(inp['ssd_a_log'][l], np.float32).reshape(1, 16), (128, 16)))
        put(p + 'gdn_dtb_bc', np.broadcast_to(np.asarray(inp['gdn_dt_bias'][l], np.float32).reshape(1, 8), (128, 8)))
        put(p + 'gdn_alog_bc', np.broadcast_to(np.asarray(inp['gdn_a_log'][l], np.float32).reshape(1, 8), (128, 8)))
        put(p + 'hg_lb0', _cols(inp['hgrn_lb'][:, 0]))
        put(p + 'hg_lb1', _cols(inp['hgrn_lb'][:, 1]))
        put(p + 'hg_norm', _cols(inp['hgrn_norm_w'][l]))
        put(p + 'gn', _cols(inp['group_norm_w'][l]))
        put(p + 'ffn_conv', np.concatenate([_cols(inp['ffn_conv_w'][l, t]) for t in range(3)], 1))
        put(p + 'ffn_convb', _cols(inp['ffn_conv_b'][l]))
    put('final', _cols(inp['final_norm']))
    return P, pl


class Builder:
    def __init__(self, T, n_layers=DEPTH, mixers='abcd', same_engine_sync=True, dbg=None, nseg=1):
        self.NSEG = nseg
        self.TT = T * nseg
        self.seg = 0
        self.T = T
        self.NB = T // 512
        self.NT = T // 128
        self.n_layers = n_layers
        self.mixers = mixers
        self.dbg = dbg
        self.nc = bass.Bass("TRN2", target_bir_lowering=False)
        self.es = ExitStack()
        self.P = Prog(self.nc, self.es, same_engine_sync)
        self.pl = param_layout()

    def sb(self, name, shape, dt=F32):
        return self.es.enter_context(self.nc.sbuf_tensor(name, shape, dt))

    def dram_in(self, name, shape, dt=F32):
        return self.nc.dram_tensor(name, shape, dt, kind="ExternalInput").ap()

    def dram_out(self, name, shape, dt=F32):
        return self.nc.dram_tensor(name, shape, dt, kind="ExternalOutput").ap()

    def dram_scr(self, name, shape, dt=F32):
        return self.nc.dram_tensor(name, shape, dt).ap()

    def pc(self, name, j=0, n=1, rows=128):
        o = self.pl.off[name] + j
        return self.params[0:rows, o:o + n]

    def rx_reset(self):
        self.P.barrier()
        self.rx_off = 0

    def rx(self, shape, dt=F32):
        n = int(np.prod(shape))
        units = n if dt == F32 else (n + 1) // 2
        a = self.arena[:, self.rx_off:self.rx_off + units]
        self.rx_off += units
        assert self.rx_off <= self.RXN, ("arena overflow", self.rx_off, self.RXN)
        if dt != F32:
            a = a.bitcast(dt)
        if len(shape) == 1:
            a = a.rearrange("p (a b) -> p a b", a=1)
        elif len(shape) == 2:
            a = a.rearrange("p (a b) -> p a b", b=shape[1])
        elif len(shape) == 3:
            a = a.rearrange("p (a b c) -> p a b c", b=shape[1], c=shape[2])
        return a

    def setup(self):
        nc, T = self.nc, self.T
        L = DEPTH
        TT = self.TT
        self.xT = self.dram_in("xT", [D, TT])
        self.flags_d = self.dram_in("flags", [128, 4])
        self.w_in = self.dram_in("w_in", [L, D, DIN])
        self.w_out = self.dram_in("w_out", [L, D, D])
        self.w_up = self.dram_in("w_up", [L, D, 2 * DFF])
        self.w_down = self.dram_in("w_down", [L, DFF, D])
        self.lru_w = self.dram_in("lru_w", [L, 128, 16 * 128])
        self.params_d = self.dram_in("params", [128, self.pl.n])
        self.consts_d = self.dram_in("consts", [128, 768])
        self.yT = self.dram_out("yT", [D, TT])
        self.xs = self.dram_scr("xs", [D, TT])
        self.mixD = self.dram_scr("mixD", [D, TT], BF16)
        self.yF = self.dram_scr("yF", [D, TT])
        self.xs_tk = [[Tk() for _ in range(self.NB * self.NSEG)] for _ in range(KC)]
        self.x0_tk = Tk()
        self.mix_tk = [[Tk() for _ in range(self.NSEG)] for _ in range(KC)]
        self.yF_tk = [[Tk() for _ in range(self.NSEG)] for _ in range(KC)]
        self.flags = self.sb("flags_sb", [128, 4])
        self.hh2 = self.sb("hh2", [128, self.NSEG, KC, 2], BF16)
        self.hh2_tk = Tk()
        self.st_gdn = self.sb("st_gdn", [128, 4, 128]); self.st_gdn_tk = [Tk() for _ in range(4)]
        self.st_hg = self.sb("st_hg", [128, 4, 128]); self.st_hg_tk = [Tk() for _ in range(4)]
        self.st_ssd = self.sb("st_ssd", [128, 8, 64]); self.st_ssd_tk = [Tk() for _ in range(8)]
        self.st_lru = self.sb("st_lru", [128, 4]); self.st_lru_tk = Tk()
        self.params = self.sb("params_sb", [128, self.pl.n])
        self.consts = self.sb("consts_sb", [128, 768])
        self.cb = self.sb("consts_bf", [128, 768], BF16)
        self.hT = self.sb("hT", [128, KC, T], BF16)
        self.hT_tk = [Tk() for _ in range(self.NB)]
        self.hh = self.sb("hhalo", [128, KC, 4], BF16)
        self.hh_tk = Tk()
        self.wst = [self.sb("wst%d" % i, [128, KC, 128]) for i in range(2)]
        self.wst_tk = [Tk() for _ in range(2)]
        self.wbf = [self.sb("wbf%d" % i, [128, KC, 128], BF16) for i in range(3)]
        self.wbf_tk = [Tk() for _ in range(3)]
        self.wi = 0
        self.RXN = 24 * 1024
        self.arena = self.sb("arena", [128, self.RXN])
        self.rx_off = 0
        self.ps = self.es.enter_context(nc.psum_tensor("ps", [128, 8, 512], F32))
        self.ps_tk = [Tk() for _ in range(8)]
        P = self.P
        self.c_tk = Tk()
        P.dma('sync', self.params[:], self.params_d[:, :], writes=[self.c_tk], dtk=self.c_tk)
        P.dma('sync', self.consts[:], self.consts_d[:, :], writes=[self.c_tk], dtk=self.c_tk)
        P.dma('sync', self.flags[:], self.flags_d[:, :], writes=[self.c_tk], dtk=self.c_tk)
        P.op('vector', lambda e: e.tensor_copy(out=self.cb[:], in_=self.consts[:]), reads=[self.c_tk], writes=[self.c_tk])
        P.op('vector', lambda e: e.memset(self.hh[:], 0.0), writes=[self.hh_tk])
        self.ident_f = self.consts[:, 0:128]
        self.ident_b = self.cb[:, 0:128]
        self.triu_b = self.cb[:, 128:256]
        self.ones_b = self.cb[:, 384:512]
        self.striu_b = self.cb[:, 256:384]
        self.blk32_f = self.consts[:, 512:640]
        self.J_f = self.consts[:, 640:768]
        self.triu_f = self.consts[:, 128:256]
        self.striu_f = self.consts[:, 256:384]
        self.ones_f = self.consts[:, 384:512]
        self.carry = self.flags[:, 0:1]
        self.ncarry = self.flags[:, 1:2]

    def gb(self, tb):
        return self.seg * self.NB + tb

    def tsl(self, a, b):
        return slice(self.seg * self.T + a, self.seg * self.T + b)

    def ln_halo(self, src, src_tks, wname, toks, dst, dst_tk):
        P = self.P
        nh = len(toks)
        xh = self.rx([KC, nh]); xh_tk = Tk()
        sqh = self.rx([KC, nh], BF16); sqh_tk = Tk()
        rs = self.rx([nh])[:, 0, :]; rs_tk = Tk()
        e_memset(self, xh, 0.0, [xh_tk])
        for j, t in enumerate(toks):
            if t is None:
                continue
            P.dma('gpsimd', xh[:, :, j:j + 1], src[:, t:t + 1].rearrange("(kc p) o -> p kc o", p=128), reads=src_tks, writes=[xh_tk], dtk=xh_tk,
                  allow_slow_non_contiguous=True)
        e_act(self, sqh, xh, AF.Square, [xh_tk], [sqh_tk])
        for kc in range(KC):
            e_mm(self, self.ps[:, 5, 0:nh], self.ones_b, sqh[:, kc, :], kc == 0, kc == KC - 1, [sqh_tk, self.c_tk], [self.ps_tk[5]])
        e_act(self, rs, self.ps[:, 5, 0:nh], AF.Sqrt, [self.ps_tk[5]], [rs_tk], scale=1.0 / D, bias=EPS)
        P.op('vector', lambda e: e.reciprocal(out=rs, in_=rs), reads=[rs_tk], writes=[rs_tk])
        e_ts(self, rs, rs, self.carry, None, ALU.mult, None, [rs_tk, self.c_tk], [rs_tk])
        o = self.pl.off[wname]
        wv = self.params[:, o:o + KC].rearrange("p (k o) -> p k o", o=1).to_broadcast([128, KC, nh])
        e_tt(self, xh, xh, wv, ALU.mult, [xh_tk, self.c_tk], [xh_tk])
        e_tt(self, dst, xh, rs.rearrange("p (o n) -> p o n", o=1).to_broadcast([128, KC, nh]), ALU.mult, [xh_tk, rs_tk], [dst_tk])

    def load_w(self, src_ap, nk, ncol):
        P = self.P
        i = self.wi
        self.wi += 1
        st, st_tk = self.wst[i % 2], self.wst_tk[i % 2]
        wb, wb_tk = self.wbf[i % 3], self.wbf_tk[i % 3]
        src3 = src_ap if len(src_ap.shape) == 3 else src_ap.rearrange("(kc p) n -> p kc n", p=128)
        P.dma('sync', st[:, 0:nk, 0:ncol], src3, writes=[st_tk], dtk=st_tk)
        P.op('gpsimd', lambda e: e.tensor_copy(out=wb[:, 0:nk, 0:ncol], in_=st[:, 0:nk, 0:ncol]), reads=[st_tk], writes=[wb_tk])
        return wb, wb_tk

    def w_begin(self, specs):
        self.wspecs = specs
        self.wpos = 0
        self.wloaded = {}

    def w_get(self):
        i = self.wpos
        if i not in self.wloaded:
            self.wloaded[i] = self.load_w(*self.wspecs[i])
        self.wpos += 1
        return self.wloaded.pop(i)

    def w_prefetch(self):
        i = self.wpos
        if i < len(self.wspecs) and i not in self.wloaded:
            self.wloaded[i] = self.load_w(*self.wspecs[i])

    def proj(self, nk, ncol, rhs_fn, rhs_tks_fn, evac_fn, banks=(0, 1, 2, 3), halo=None):
        P = self.P
        wb, wb_tk = self.w_get()
        if halo is not None:
            hr, hr_tk, hev = halo
            nh = hr.shape[2]
            for kc in range(nk):
                e_mm(self, self.ps[0:ncol, 5, 0:nh], wb[:, kc, 0:ncol], hr[:, kc, :], kc == 0, kc == nk - 1, [wb_tk, hr_tk], [self.ps_tk[5]])
            hev(self.ps[0:ncol, 5, 0:nh], self.ps_tk[5])
        for kc in range(nk):
            for tb in range(self.NB):
                b = banks[tb]
                P.op('tensor', lambda e, kc=kc, tb=tb, b=b: e.matmul(self.ps[0:ncol, b, :], lhsT=wb[:, kc, 0:ncol], rhs=rhs_fn(kc, tb),
                                                                  start=(kc == 0), stop=(kc == nk - 1)),
                     reads=[wb_tk] + rhs_tks_fn(kc, tb), writes=[self.ps_tk[b]])
        self.w_prefetch()
        for tb in range(self.NB):
            b = banks[tb]
            evac_fn(tb, self.ps[0:ncol, b, :], self.ps_tk[b])

    def ln_phase(self, src, src_tks_fn, wname, final=False, reset=True):
        P, T = self.P, self.T
        if reset:
            self.rx_reset()
        xts = [self.rx([KC, 512]) for _ in range(2)]
        xt_tk = [Tk() for _ in range(2)]
        sqs = [self.rx([512], BF16) for _ in range(2)]
        sq_tk = [Tk() for _ in range(2)]
        rstd = self.rx([512])
        rstd_tk = Tk()
        for tb in range(self.NB):
            xt, xtk = xts[tb % 2], xt_tk[tb % 2]
            P.dma('sync', xt, src[:, self.tsl(tb * 512, (tb + 1) * 512)].rearrange("(kc p) t -> p kc t", p=128),
                  reads=src_tks_fn(self.gb(tb)), writes=[xtk], dtk=xtk)
            bank = 4 + tb % 2
            for kc in range(KC):
                sq, stk = sqs[kc % 2], sq_tk[kc % 2]
                P.op('scalar', lambda e, sq=sq, xt=xt, kc=kc: e.activation(out=sq[:, 0, :], in_=xt[:, kc, :], func=AF.Square),
                     reads=[xtk], writes=[stk])
                P.op('tensor', lambda e, sq=sq, kc=kc, bank=bank: e.matmul(self.ps[:, bank, :], lhsT=self.ones_b, rhs=sq[:, 0, :],
                                                                          start=(kc == 0), stop=(kc == KC - 1)),
                     reads=[stk, self.c_tk], writes=[self.ps_tk[bank]])
            P.op('scalar', lambda e, bank=bank: e.activation(out=rstd[:, 0, :], in_=self.ps[:, bank, :], func=AF.Sqrt, scale=1.0 / D, bias=EPS),
                 reads=[self.ps_tk[bank]], writes=[rstd_tk])
            P.op('vector', lambda e: e.reciprocal(out=rstd[:, 0, :], in_=rstd[:, 0, :]), reads=[rstd_tk], writes=[rstd_tk])
            for kc in range(KC):
                eng = 'vector' if kc % 2 == 0 else 'gpsimd'
                if final:
                    dst = xt[:, kc, :]
                    wr = [xtk]
                else:
                    dst = self.hT[:, kc, tb * 512:(tb + 1) * 512]
                    wr = [self.hT_tk[tb]]
                if eng == 'vector':
                    P.op('vector', lambda e, dst=dst, xt=xt, kc=kc: e.scalar_tensor_tensor(out=dst, in0=xt[:, kc, :], scalar=self.pc(wname, kc), in1=rstd[:, 0, :],
                                                                                       op0=ALU.mult, op1=ALU.mult),
                         reads=[xtk, rstd_tk, self.c_tk], writes=wr)
                else:
                    P.op('gpsimd', lambda e, xt=xt, kc=kc: e.tensor_scalar(out=xt[:, kc, :], in0=xt[:, kc, :], scalar1=self.pc(wname, kc), scalar2=None, op0=ALU.mult),
                         reads=[self.c_tk], writes=[xtk])
                    P.op('gpsimd', lambda e, dst=dst, xt=xt, kc=kc: e.tensor_tensor(out=dst, in0=xt[:, kc, :], in1=rstd[:, 0, :], op=ALU.mult),
                         reads=[xtk, rstd_tk], writes=wr)
            if final:
                P.dma('sync', self.yT[:, self.tsl(tb * 512, (tb + 1) * 512)].rearrange("(kc p) t -> p kc t", p=128), xt, reads=[xtk], dtk=xtk)

    def resid_specs(self, w_fn, nk):
        return [(w_fn(dc), nk, 128) for dc in range(KC)]

    def resid_proj(self, nk, act, act_tks_fn, src, src_tks_fn):
        P = self.P
        for dc in range(KC):
            def evac(tb, ps_ap, ps_tk, dc=dc):
                xi, xi_tk = self.xin[self.xin_i % 3], self.xin_tk[self.xin_i % 3]
                self.xin_i += 1
                gsl = self.tsl(tb * 512, (tb + 1) * 512)
                P.dma('gpsimd', xi[:, 0, :], src[dc * 128:(dc + 1) * 128, gsl], reads=src_tks_fn(dc, self.gb(tb)), writes=[xi_tk], dtk=xi_tk)
                P.op('vector', lambda e: e.tensor_tensor(out=xi[:, 0, :], in0=ps_ap, in1=xi[:, 0, :], op=ALU.add), reads=[ps_tk, xi_tk], writes=[xi_tk])
                P.dma('gpsimd', self.xs[dc * 128:(dc + 1) * 128, gsl], xi[:, 0, :], reads=[xi_tk], writes=[self.xs_tk[dc][self.gb(tb)]], dtk=xi_tk)
            self.proj(nk, 128, lambda kc, tb: act[:, kc, tb * 512:(tb + 1) * 512], act_tks_fn, evac)

    def alloc_xin(self):
        self.xin = [self.rx([512]) for _ in range(3)]
        self.xin_tk = [Tk() for _ in range(3)]
        self.xin_i = 0

    def ffn_phase(self, l):
        P, T = self.P, self.T
        self.rx_reset()
        NQ = NFC // FFQ
        act = self.rx([NQ, T], BF16)
        act_tk = [Tk() for _ in range(NQ)]
        ug = self.rx([T + 2]); ug_tk = Tk()
        cg = self.rx([T]); cg_tk = Tk()
        uv = self.rx([T + 2]); uv_tk = Tk()
        cv = self.rx([T]); cv_tk = Tk()
        self.alloc_xin()
        ug, cg, uv, cv = ug[:, 0, :], cg[:, 0, :], uv[:, 0, :], cv[:, 0, :]
        pre = 'L%d_' % l
        hT = self.hT

        def up_chunk(col, fcg, u, u_tk, c, c_tk):
            def evac(tb, ps_ap, ps_tk):
                P.op('scalar', lambda e: e.activation(out=u[:, 1 + tb * 512:1 + (tb + 1) * 512], in_=ps_ap, func=AF.Copy), reads=[ps_tk], writes=[u_tk])
                P.op('scalar', lambda e: e.activation(out=c[:, tb * 512:(tb + 1) * 512], in_=ps_ap, func=AF.Identity,
                                                      scale=self.pc(pre + 'ffn_conv', 88 + fcg), bias=self.pc(pre + 'ffn_convb', fcg)),
                     reads=[ps_tk, self.c_tk], writes=[c_tk])
            def hev(ps_ap, ps_tk):
                e_act(self, u[:, 0:1], ps_ap[:, 0:1], AF.Copy, [ps_tk], [u_tk])
                e_act(self, u[:, T + 1:T + 2], ps_ap[:, 1:2], AF.Copy, [ps_tk], [u_tk])
            self.proj(KC, 128, lambda kc, tb: hT[:, kc, tb * 512:(tb + 1) * 512],
                      lambda kc, tb: [self.hT_tk[tb]], evac, halo=(self.hh2[:, self.seg], self.hh2_tk, hev))
            P.op('vector', lambda e: e.scalar_tensor_tensor(out=c, in0=u[:, 0:T], scalar=self.pc(pre + 'ffn_conv', fcg), in1=c, op0=ALU.mult, op1=ALU.add),
                 reads=[u_tk, c_tk, self.c_tk], writes=[c_tk])
            P.op('vector', lambda e: e.scalar_tensor_tensor(out=c, in0=u[:, 2:T + 2], scalar=self.pc(pre + 'ffn_conv', 176 + fcg), in1=c, op0=ALU.mult, op1=ALU.add),
                 reads=[u_tk, c_tk, self.c_tk], writes=[c_tk])

        specs = []
        for q in range(FFQ):
            for j in range(NQ):
                fc = q * NQ + j
                specs.append((self.w_up[l, :, fc * 128:(fc + 1) * 128], KC, 128))
                specs.append((self.w_up[l, :, DFF + fc * 128:DFF + (fc + 1) * 128], KC, 128))
            specs += self.resid_specs(lambda dc, q=q: self.w_down[l, q * NQ * 128:(q + 1) * NQ * 128, dc * 128:(dc + 1) * 128], NQ)
        self.w_begin(specs)
        for q in range(FFQ):
            for j in range(NQ):
                fc = q * NQ + j
                up_chunk(fc * 128, fc, ug, ug_tk, cg, cg_tk)
                P.op('scalar', lambda e: e.activation(out=cg, in_=cg, func=AF.Silu), reads=[cg_tk], writes=[cg_tk])
                up_chunk(DFF + fc * 128, NFC + fc, uv, uv_tk, cv, cv_tk)
                P.op('gpsimd', lambda e, j=j: e.tensor_tensor(out=act[:, j, :], in0=cg, in1=cv, op=ALU.mult), reads=[cg_tk, cv_tk], writes=[act_tk[j]])
            self.resid_proj(NQ, act, lambda kc, tb: [act_tk[kc]], self.xs, lambda dc, tb: [self.xs_tk[dc][tb]])

    def mixer_phase(self, l):
        from_mixers(self, l)

    def outproj_phase(self, l, src, src_tks_fn):
        P, T = self.P, self.T
        self.rx_reset()
        self.alloc_xin()
        for kc in range(KC):
            P.dma('sync', self.hT[:, kc, :], self.mixD[kc * 128:(kc + 1) * 128, self.tsl(0, T)], reads=[self.mix_tk[kc][self.seg]], writes=self.hT_tk, dtk=self.hT_tk[0])
        self.w_begin(self.resid_specs(lambda dc: self.w_out[l, :, dc * 128:(dc + 1) * 128], KC))
        self.resid_proj(KC, self.hT, lambda kc, tb: [self.hT_tk[tb]], src, src_tks_fn)

    def halo_toks(self, left, right):
        t0 = self.seg * self.T
        toks = []
        for k in range(left, 0, -1):
            toks.append(t0 - k if t0 - k >= 0 else None)
        for k in range(right):
            t = t0 + self.T + k
            toks.append(t if t < self.TT else None)
        return toks

    def build(self):
        self.setup()
        NS = self.NSEG
        all_tk = lambda gbk: [self.xs_tk[dc][gbk] for dc in range(KC)]
        for l in range(self.n_layers):
            pre = 'L%d_' % l
            if l == 0:
                src, stf, stf2 = self.xT, (lambda gbk: [self.x0_tk]), (lambda dc, gbk: [self.x0_tk])
                halo_src_tks = [self.x0_tk]
            else:
                src, stf, stf2 = self.xs, all_tk, (lambda dc, gbk: [self.xs_tk[dc][gbk]])
                halo_src_tks = [t for row in self.xs_tk for t in row]
            for d in (0, 1):
                for seg in (range(NS) if d == 0 else range(NS - 1, -1, -1)):
                    self.seg = seg
                    self.ln_phase(src, stf, pre + 'ln1')
                    self.ln_halo(src, halo_src_tks, pre + 'ln1', self.halo_toks(2, 1), self.hh[:, :, 0:3], self.hh_tk)
                    from_mixers(self, l, d)
            for seg in range(NS):
                self.seg = seg
                self.outproj_phase(l, src, stf2)
            self.rx_reset()
            xs_all = [t for row in self.xs_tk for t in row]
            for seg in range(NS):
                self.seg = seg
                self.ln_halo(self.xs, xs_all, pre + 'ln2', self.halo_toks(1, 1), self.hh2[:, seg], self.hh2_tk)
            for seg in range(NS):
                self.seg = seg
                self.ln_phase(self.xs, all_tk, pre + 'ln2')
                self.ffn_phase(l)
        for seg in range(NS):
            self.seg = seg
            self.ln_phase(self.xs, all_tk, 'final', final=True)
        self.P.barrier(final=True)
        self.P.replay()
        return self.nc


def from_mixers(B, l, d):
    P, T = B.P, B.T
    for g, (name, fn) in enumerate((('a', mixer_gdn), ('b', mixer_lru), ('c', mixer_ssd), ('d', mixer_hgrn))):
        B.rx_reset()
        if name in B.mixers:
            fn(B, l, g, d)
        elif d == 1:
            z = B.rx([T], BF16)
            ztk = Tk()
            P.op('vector', lambda e, z=z: e.memset(z[:, 0, :], 0.0), writes=[ztk])
            for c in range(4):
                kc = g * 4 + c
                P.dma('sync', B.mixD[kc * 128:(kc + 1) * 128, B.tsl(0, T)], z[:, 0, :], reads=[ztk], writes=[B.mix_tk[kc][B.seg]], dtk=ztk)


def lockstep(gens):
    gens = list(gens)
    while gens:
        for g_ in list(gens):
            try:
                next(g_)
            except StopIteration:
                gens.remove(g_)


def is_first(B, d):
    return B.seg == 0 if d == 0 else B.seg == B.NSEG - 1


def state_init(B, d, S, S_tk):
    if is_first(B, d):
        e_memset(B, S, 0.0, [S_tk])
    else:
        e_ts(B, S, S, B.carry, None, ALU.mult, None, [S_tk, B.c_tk], [S_tk])


def yf_store(B, kc, y_ap, y_tk):
    B.P.dma('sync', B.yF[kc * 128:(kc + 1) * 128, B.tsl(0, B.T)], y_ap, reads=[y_tk], writes=[B.yF_tk[kc][B.seg]], dtk=y_tk)


def yf_load(B, kc, y_ap, y_tk):
    B.P.dma('sync', y_ap, B.yF[kc * 128:(kc + 1) * 128, B.tsl(0, B.T)], reads=[B.yF_tk[kc][B.seg]], writes=[y_tk], dtk=y_tk)


def inproj(B, ncol, evac, halo=None):
    B.proj(KC, ncol, lambda kc, tb: B.hT[:, kc, tb * 512:(tb + 1) * 512], lambda kc, tb: [B.hT_tk[tb]], evac, halo=halo)


def win_spec(B, l, col0, ncol):
    return (B.w_in[l, :, col0:col0 + ncol], KC, ncol)


def inproj_raw(B, ncol, u, u_tk, off=2):
    P = B.P

    def evac(tb, ps_ap, ps_tk):
        P.op('scalar', lambda e: e.activation(out=u[0:ncol, off + tb * 512:off + (tb + 1) * 512], in_=ps_ap, func=AF.Copy), reads=[ps_tk], writes=[u_tk])

    def hev(ps_ap, ps_tk):
        T = B.T
        e_act(B, u[0:ncol, 0:2], ps_ap[:, 0:2], AF.Copy, [ps_tk], [u_tk])
        e_act(B, u[0:ncol, T + 2:T + 3], ps_ap[:, 2:3], AF.Copy, [ps_tk], [u_tk])
    inproj(B, ncol, evac, halo=(B.hh[:, :, 0:3], B.hh_tk, hev))


def inproj_act(B, ncol, dst, dst_tk, func, **kw):
    P = B.P

    def evac(tb, ps_ap, ps_tk):
        P.op('scalar', lambda e: e.activation(out=dst[0:ncol, tb * 512:(tb + 1) * 512], in_=ps_ap, func=func, **kw), reads=[ps_tk, B.c_tk], writes=[dst_tk])
    inproj(B, ncol, evac)


def conv4(B, u, u_tk, c, c_tk, wname, nch, ch, bias_ap, dst, dst_tk, func):
    P, T = B.P, B.T
    w = lambda k: B.pc(wname, k * nch + ch)
    P.op('vector', lambda e: e.tensor_scalar(out=c, in0=u[:, 0:T], scalar1=w(0), scalar2=(bias_ap if bias_ap is not None else 0.0), op0=ALU.mult, op1=ALU.add),
         reads=[u_tk, B.c_tk], writes=[c_tk])
    for k in (1, 2, 3):
        P.op('vector', lambda e, k=k: e.scalar_tensor_tensor(out=c, in0=u[:, k:k + T], scalar=w(k), in1=c, op0=ALU.mult, op1=ALU.add),
             reads=[u_tk, c_tk, B.c_tk], writes=[c_tk])
    if dst is not None:
        P.op('scalar', lambda e: e.activation(out=dst, in_=c, func=func), reads=[c_tk], writes=[dst_tk])


def alloc_u(B):
    P, T = B.P, B.T
    u = B.rx([T + 3])[:, 0, :]
    u_tk = Tk()
    return u, u_tk


def group_out(B, l, g, y, y_tk):
    P, T = B.P, B.T
    pre = 'L%d_' % l
    sqs = [B.rx([512], BF16) for _ in range(2)]
    sq_tk = [Tk() for _ in range(2)]
    rstd = B.rx([512])
    rstd_tk = Tk()
    ots = [B.rx([512], BF16) for _ in range(2)]
    ot_tk = [Tk() for _ in range(2)]
    oi = 0
    for tb in range(B.NB):
        bank = 4 + tb % 2
        sl = slice(tb * 512, (tb + 1) * 512)
        for c in range(4):
            sq, stk = sqs[c % 2], sq_tk[c % 2]
            P.op('scalar', lambda e, sq=sq, c=c, sl=sl: e.activation(out=sq[:, 0, :], in_=y[:, c, sl], func=AF.Square), reads=[y_tk], writes=[stk])
            P.op('tensor', lambda e, sq=sq, c=c, bank=bank: e.matmul(B.ps[:, bank, :], lhsT=B.ones_b, rhs=sq[:, 0, :], start=(c == 0), stop=(c == 3)),
                 reads=[stk, B.c_tk], writes=[B.ps_tk[bank]])
        P.op('scalar', lambda e, bank=bank: e.activation(out=rstd[:, 0, :], in_=B.ps[:, bank, :], func=AF.Sqrt, scale=1.0 / 512, bias=EPS),
             reads=[B.ps_tk[bank]], writes=[rstd_tk])
        P.op('vector', lambda e: e.reciprocal(out=rstd[:, 0, :], in_=rstd[:, 0, :]), reads=[rstd_tk], writes=[rstd_tk])
        for c in range(4):
            ot, otk = ots[oi % 2], ot_tk[oi % 2]
            oi += 1
            P.op('vector', lambda e, ot=ot, c=c, sl=sl: e.scalar_tensor_tensor(out=ot[:, 0, :], in0=y[:, c, sl], scalar=B.pc(pre + 'gn', g * 4 + c), in1=rstd[:, 0, :],
                                                                            op0=ALU.mult, op1=ALU.mult),
                 reads=[y_tk, rstd_tk, B.c_tk], writes=[otk])
            kc = g * 4 + c
            P.dma('sync', B.mixD[kc * 128:(kc + 1) * 128, B.tsl(tb * 512, (tb + 1) * 512)], ot[:, 0, :], reads=[otk], writes=[B.mix_tk[kc][B.seg]], dtk=otk)


def e_act(B, out, in_, func, reads, writes, eng='scalar', **kw):
    B.P.op(eng, lambda e: e.activation(out=out, in_=in_, func=func, **kw), reads=reads, writes=writes)


def e_mm(B, out, lhsT, rhs, start, stop, reads, writes):
    B.P.op('tensor', lambda e: e.matmul(out, lhsT=lhsT, rhs=rhs, start=start, stop=stop), reads=reads, writes=writes)


def e_tr(B, out, in_, ident, reads, writes):
    B.P.op('tensor', lambda e: e.transpose(out, in_, ident), reads=reads, writes=writes)


def e_tt(B, out, in0, in1, op, reads, writes, eng='vector'):
    B.P.op(eng, lambda e: e.tensor_tensor(out=out, in0=in0, in1=in1, op=op), reads=reads, writes=writes)


def e_ts(B, out, in0, s1, s2, op0, op1, reads, writes, eng='vector'):
    if s2 is None:
        B.P.op(eng, lambda e: e.tensor_scalar(out=out, in0=in0, scalar1=s1, scalar2=None, op0=op0), reads=reads, writes=writes)
    else:
        B.P.op(eng, lambda e: e.tensor_scalar(out=out, in0=in0, scalar1=s1, scalar2=s2, op0=op0, op1=op1), reads=reads, writes=writes)


def e_stt(B, out, in0, scalar, in1, op0, op1, reads, writes):
    B.P.op('vector', lambda e: e.scalar_tensor_tensor(out=out, in0=in0, scalar=scalar, in1=in1, op0=op0, op1=op1), reads=reads, writes=writes)


def e_scan(B, out, d0, d1, reads, writes, initial=0.0):
    B.P.op('vector', lambda e: e.tensor_tensor_scan(out=out, data0=d0, data1=d1, initial=initial, op0=ALU.mult, op1=ALU.add), reads=reads, writes=writes)


def e_memset(B, ap, val, writes, eng='vector'):
    B.P.op(eng, lambda e: e.memset(ap, val), writes=writes)


def e_copy(B, out, in_, reads, writes, eng='vector'):
    B.P.op(eng, lambda e: e.tensor_copy(out=out, in_=in_), reads=reads, writes=writes)


def mixer_lru(B, l, g, d):
    P, T = B.P, B.T
    pre = 'L%d_' % l
    X0, G0 = 2064, 2576
    y = B.rx([4, T]); y_tk = Tk()
    u, u_tk = alloc_u(B)
    xc = B.rx([T])[:, 0, :]; xc_tk = Tk()
    xcb = B.rx([T], BF16)[:, 0, :]; xcb_tk = Tk()
    gg = B.rx([T])[:, 0, :]; gg_tk = Tk()
    bA = B.rx([T])[:, 0, :]; bA_tk = Tk()
    bB = B.rx([T])[:, 0, :]; bB_tk = Tk()
    bC = B.rx([T])[:, 0, :]; bC_tk = Tk()
    lw = B.rx([16, 128], BF16); lw_tk = Tk()
    sp = B.rx([16])[:, 0, :]; sp_tk = Tk()
    hin = B.rx([4])[:, 0, :]; hin_tk = Tk()
    st, st_tk = B.wst[B.wi % 2], B.wst_tk[B.wi % 2]
    B.wi += 1
    P.dma('sync', st[:, :, :], B.lru_w[l].rearrange("p (k n) -> p k n", n=128), writes=[st_tk], dtk=st_tk)
    e_copy(B, lw, st[:, :, :], [st_tk], [lw_tk], eng='gpsimd')
    e_act(B, sp[:, 0:8], B.pc(pre + 'lru_lam', 0, 8), AF.Exp, [B.c_tk], [sp_tk], scale=-1.0)
    e_act(B, sp[:, 0:8], sp[:, 0:8], AF.Ln, [sp_tk], [sp_tk], bias=1.0)
    e_ts(B, sp[:, 8:16], sp[:, 0:8], -16.0, None, ALU.mult, None, [sp_tk], [sp_tk])
    e_ts(B, sp[:, 0:8], sp[:, 0:8], -8.0, None, ALU.mult, None, [sp_tk], [sp_tk])
    first = is_first(B, d)
    if not first:
        e_ts(B, hin, B.st_lru[:, 0:4], B.carry, None, ALU.mult, None, [B.st_lru_tk, B.c_tk], [hin_tk])
    specs = []
    for n in range(4):
        specs.append(win_spec(B, l, X0 + n * 128, 128))
        if d == 1:
            specs.append(win_spec(B, l, G0 + n * 128, 128))
    B.w_begin(specs)
    for n in range(4):
        inproj_raw(B, 128, u, u_tk)
        conv4(B, u, u_tk, xc, xc_tk, pre + 'lru_conv', 4, n, B.pc(pre + 'lru_convb', n), xcb, xcb_tk, AF.Copy)
        if d == 1:
            inproj_act(B, 128, gg, gg_tk, AF.Copy)
            e_act(B, bB, gg, AF.Square, [gg_tk], [bB_tk])
            e_ts(B, bB, bB, 0.044715, 1.0, ALU.mult, ALU.add, [bB_tk], [bB_tk])
            e_tt(B, bB, bB, gg, ALU.mult, [bB_tk, gg_tk], [bB_tk])
            e_act(B, bB, bB, AF.Tanh, [bB_tk], [bB_tk], scale=0.7978845608028654)
            e_ts(B, bB, bB, 0.5, 0.5, ALU.mult, ALU.add, [bB_tk], [bB_tk])
            e_tt(B, gg, gg, bB, ALU.mult, [gg_tk, bB_tk], [gg_tk])
            yf_load(B, g * 4 + n, y[:, n, :], y_tk)
        for which, dst, dst_tk, bname in ((0, bA, bA_tk, 'lru_ba'), (1, bC, bC_tk, 'lru_bi')):
            for tb in range(B.NB):
                bank = 4 + (tb % 2) + 2 * which
                sl = slice(tb * 512, (tb + 1) * 512)
                e_mm(B, B.ps[:, bank, :], lw[:, which * 8 + d * 4 + n, :], xcb[:, sl], True, True, [lw_tk, xcb_tk], [B.ps_tk[bank]])
                e_act(B, dst[:, sl], B.ps[:, bank, :], AF.Sigmoid, [B.ps_tk[bank], B.c_tk], [dst_tk], bias=B.pc(pre + bname, d * 4 + n))
        k = d * 4 + n
        e_act(B, bB, bA, AF.Exp, [bA_tk, sp_tk], [bB_tk], scale=sp[:, k:k + 1])
        e_act(B, bA, bA, AF.Exp, [bA_tk, sp_tk], [bA_tk], scale=sp[:, 8 + k:9 + k])
        e_act(B, bA, bA, AF.Sqrt, [bA_tk], [bA_tk], scale=-1.0, bias=1.0)
        fp = 0 if d == 0 else T - 1
        if first:
            e_memset(B, bA[:, fp:fp + 1], 1.0, [bA_tk])
        else:
            e_ts(B, bA[:, fp:fp + 1], bA[:, fp:fp + 1], B.carry, B.ncarry, ALU.mult, ALU.add, [bA_tk, B.c_tk], [bA_tk])
        e_tt(B, bC, bC, xc, ALU.mult, [bC_tk, xc_tk], [bC_tk])
        e_tt(B, bC, bC, bA, ALU.mult, [bC_tk, bA_tk], [bC_tk])
        init = 0.0 if first else hin[:, n:n + 1]
        rd = [bB_tk, bC_tk] + ([] if first else [hin_tk])
        if d == 0:
            e_scan(B, bA, bB, bC, rd, [bA_tk], initial=init)
            e_copy(B, B.st_lru[:, n:n + 1], bA[:, T - 1:T], [bA_tk], [B.st_lru_tk])
            yf_store(B, g * 4 + n, bA, bA_tk)
        else:
            e_scan(B, bA[:, ::-1], bB[:, ::-1], bC[:, ::-1], rd, [bA_tk], initial=init)
            e_copy(B, B.st_lru[:, n:n + 1], bA[:, 0:1], [bA_tk], [B.st_lru_tk])
            e_tt(B, y[:, n, :], y[:, n, :], bA, ALU.add, [bA_tk, y_tk], [y_tk])
            e_tt(B, y[:, n, :], y[:, n, :], gg, ALU.mult, [gg_tk, y_tk], [y_tk])
    if d == 1:
        group_out(B, l, g, y, y_tk)


def mixer_gdn(B, l, g, d):
    P, T, NB, NT = B.P, B.T, B.NB, B.NT
    pre = 'L%d_' % l
    ybf = B.rx([4, T], BF16); ybf_tk = Tk()
    specs = [win_spec(B, l, 2048, 16)]
    for h in range(4):
        specs += [win_spec(B, l, h * 128, 128), win_spec(B, l, 512 + h * 128, 128), win_spec(B, l, 1024 + h * 128, 128)]
        if d == 1:
            specs += [win_spec(B, l, 1536 + h * 128, 128)]
    B.w_begin(specs)
    nr, W = 16, NT * 16
    def buf():
        return B.rx([NT, nr])
    RAW, ORD, ACS, ACSL = buf(), buf(), buf(), buf()
    abc = B.rx([8])[:, 0, :]
    stk = Tk()
    flat = lambda a: a.rearrange("p a b -> p (a b)")
    wb, wb_tk = B.w_get()
    for i in range(NT):
        for kc in range(KC):
            e_mm(B, B.ps[:, 4, i * nr:(i + 1) * nr], B.hT[:, kc, i * 128:(i + 1) * 128], wb[:, kc, 0:nr], kc == 0, kc == KC - 1,
                 [B.hT_tk[i // 4], wb_tk], [B.ps_tk[4]])
    B.w_prefetch()
    ps3 = B.ps[:, 4, 0:W].rearrange("p (a b) -> p a b", b=nr)
    e_act(B, RAW[:, :, 0:8], ps3[:, :, 0:8], AF.Sigmoid, [B.ps_tk[4]], [stk])
    e_tt(B, RAW[:, :, 8:16], ps3[:, :, 8:16], B.pc(pre + 'gdn_dtb_bc', 0, 8).rearrange("p (a b) -> p a b", a=1).to_broadcast([128, NT, 8]),
         ALU.add, [B.ps_tk[4], B.c_tk], [stk])
    e_act(B, RAW[:, :, 8:16], RAW[:, :, 8:16], AF.Exp, [stk], [stk])
    e_act(B, RAW[:, :, 8:16], RAW[:, :, 8:16], AF.Ln, [stk], [stk], bias=1.0)
    e_act(B, abc, B.pc(pre + 'gdn_alog_bc', 0, 8), AF.Exp, [B.c_tk], [stk])
    e_ts(B, abc, abc, -1.0, None, ALU.mult, None, [stk], [stk])
    e_tt(B, RAW[:, :, 8:16], RAW[:, :, 8:16], abc.rearrange("p (a b) -> p a b", a=1).to_broadcast([128, NT, 8]), ALU.mult, [stk], [stk])
    e_mm(B, B.ps[:, 5, 0:W], B.J_f, flat(RAW), True, True, [stk, B.c_tk], [B.ps_tk[5]])
    pj3 = B.ps[:, 5, 0:W].rearrange("p (a b) -> p a b", b=nr)
    for c0 in (0, 8):
        e_copy(B, ORD[:, :, c0:c0 + 4], RAW[:, :, c0:c0 + 4], [stk], [stk])
        e_copy(B, ORD[:, :, c0 + 4:c0 + 8], pj3[:, ::-1, c0 + 4:c0 + 8], [B.ps_tk[5], stk], [stk])
    e_mm(B, B.ps[:, 4, 0:W], B.triu_f, flat(ORD), True, True, [stk, B.c_tk], [B.ps_tk[4]])
    e_copy(B, flat(ACS), B.ps[:, 4, 0:W], [B.ps_tk[4]], [stk])
    e_mm(B, B.ps[:, 5, 0:W], B.ones_f, flat(ORD), True, True, [stk, B.c_tk], [B.ps_tk[5]])
    e_copy(B, flat(ACSL), B.ps[:, 5, 0:W], [B.ps_tk[5]], [stk])
    BETA = ORD[:, :, 0:8]
    GC = ACS[:, :, 8:16]
    EG = RAW[:, :, 0:8]; EGL = RAW[:, :, 8:16]; GLt = ACSL[:, :, 0:8]
    e_act(B, EG, GC, AF.Exp, [stk], [stk])
    e_tt(B, EGL, ACSL[:, :, 8:16], GC, ALU.subtract, [stk], [stk])
    e_act(B, EGL, EGL, AF.Exp, [stk], [stk])
    e_act(B, GLt, ACSL[:, :, 8:16], AF.Exp, [stk], [stk])
    qf = B.rx([T], BF16)[:, 0, :]; qf_tk = Tk()
    kf = B.rx([T], BF16)[:, 0, :]; kf_tk = Tk()
    vf = B.rx([T], BF16)[:, 0, :]; vf_tk = Tk()
    zs = B.rx([T], BF16)[:, 0, :]; zs_tk = Tk()
    y = B.rx([T])[:, 0, :]; y_tk = Tk()
    u, u_tk = alloc_u(B)
    cc = B.rx([T])[:, 0, :]; cc_tk = Tk()
    OB = B.rx([3, 512], BF16); OB_tk = Tk()
    def t16(n=1):
        a = B.rx([n, 128], BF16)
        return a, Tk()
    tok3, tok3_tk = t16(3)
    sc = B.rx([8])[:, 0, :]; sc_tk = Tk()
    junk = B.rx([128], BF16)[:, 0, :]; junk_tk = Tk()
    tkvS = [t16(6) for _ in range(2)]
    vbS = [t16(1) for _ in range(2)]
    fm4S = [t16(4) for _ in range(2)]
    PSs = [(B.rx([2, 128]), Tk()) for _ in range(2)]
    RSs = [(B.rx([128])[:, 0, :], Tk()) for _ in range(2)]
    qkS = [t16(1) for _ in range(2)]
    RBf = B.rx([128])[:, 0, :]
    Dm = B.rx([128])[:, 0, :]; Dm_tk = Tk()
    EE = B.rx([128])[:, 0, :]; EE_tk = Tk()
    Es = B.rx([128])[:, 0, :]; Es_tk = Tk()
    Et = B.rx([128])[:, 0, :]; Et_tk = Tk()
    Rb, Rb_tk = t16(1)
    wT, wT_tk = t16(1)
    vn, vn_tk = t16(1)
    Sb, Sb_tk = t16(1)
    sq = B.rx([512], BF16)[:, 0, :]; sq_tk = Tk()
    rstd = B.rx([512])[:, 0, :]; rstd_tk = Tk()
    TRk, RBk, Ak, Bk = 0, 1, 2, 3
    A2, B2, C2, D2 = 4, 5, 6, 7
    ptr = psbf(B, TRk)
    for h in range(4):
        for which, dst, dst_tk in ((0, qf, qf_tk), (1, kf, kf_tk), (2, vf, vf_tk)):
            inproj_raw(B, 128, u, u_tk)
            conv4(B, u, u_tk, cc, cc_tk, pre + 'gdn_conv', 12, which * 4 + h, None, dst, dst_tk, AF.Silu)
        if d == 1:
            inproj_act(B, 128, zs, zs_tk, AF.Silu)
            yf_load(B, g * 4 + h, y, y_tk)
        S = B.st_gdn[:, h, :]
        S_tk = B.st_gdn_tk[h]
        for d in (d,):
            r = d * 4 + h
            state_init(B, d, S, S_tk)
            e_act(B, Sb[:, 0, :], S, AF.Copy, [S_tk], [Sb_tk])
            for b in range(NB):
                if d == 0:
                    sl = slice(b * 512, (b + 1) * 512)
                    srcs = [kf[:, sl], qf[:, sl], vf[:, sl]]
                else:
                    sl = slice(T - (b + 1) * 512, T - b * 512)
                    srcs = [kf[:, sl][:, ::-1], qf[:, sl][:, ::-1], vf[:, sl][:, ::-1]]
                for k, (src, t_) in enumerate(zip(srcs, (kf_tk, qf_tk, vf_tk))):
                    e_copy(B, OB[:, k, :], src, [t_], [OB_tk], eng=('vector' if k != 1 else 'gpsimd'))
                def neumann_levels(Nf, Nf_tk, Rr, Rr_tk, ka, kb, kc_, nlev):
                    for lev in range(nlev):
                        e_mm(B, B.ps[:, ka, 0:128], Nf[:, 1, :], Nf[:, 0, :], True, True, [Nf_tk], [B.ps_tk[ka]])
                        e_mm(B, B.ps[:, kb, 0:128], Nf[:, 0, :], Nf[:, 1, :], True, True, [Nf_tk], [B.ps_tk[kb]])
                        yield
                        e_copy(B, Nf[:, 0, :], B.ps[:, ka, 0:128], [B.ps_tk[ka]], [Nf_tk])
                        e_act(B, Nf[:, 1, :], B.ps[:, kb, 0:128], AF.Copy, [B.ps_tk[kb]], [Nf_tk])
                        yield
                        e_mm(B, B.ps[:, kc_, 0:128], Nf[:, 1, :], Rr, True, True, [Nf_tk, Rr_tk], [B.ps_tk[kc_]])
                        yield
                        e_tt(B, Rr, Rr, B.ps[:, kc_, 0:128], ALU.add, [Rr_tk, B.ps_tk[kc_]], [Rr_tk])
                        yield

                def stage1(i, sl_):
                    ti = b * 4 + i
                    tsl = slice(i * 128, (i + 1) * 128)
                    col = lambda A_: A_[:, ti, r:r + 1]
                    tkv, tkv_tk = tkvS[sl_]
                    vb_, vb_tk = vbS[sl_]
                    fm4, fm4_tk = fm4S[sl_]
                    Nf, Nf_tk = PSs[sl_]
                    Rr, Rr_tk = RSs[sl_]
                    qkT, qkT_tk = qkS[sl_]
                    for k in range(3):
                        e_tr(B, ptr[:, k * 128:(k + 1) * 128], OB[:, k, tsl], B.ident_b, [OB_tk, B.c_tk], [B.ps_tk[TRk]])
                    yield
                    e_copy(B, tok3.rearrange("p a b -> p (a b)"), ptr[:, 0:384], [B.ps_tk[TRk]], [tok3_tk])
                    yield
                    for k in range(2):
                        e_act(B, junk, tok3[:, k, :], AF.Square, [tok3_tk], [junk_tk, sc_tk], accum_out=sc[:, k:k + 1])
                    yield
                    e_act(B, sc[:, 0:2], sc[:, 0:2], AF.Sqrt, [sc_tk], [sc_tk], bias=EPS)
                    yield
                    B.P.op('vector', lambda e: e.reciprocal(out=sc[:, 0:2], in_=sc[:, 0:2]), reads=[sc_tk], writes=[sc_tk])
                    yield
                    e_tt(B, sc[:, 2:3], sc[:, 0:1], col(BETA), ALU.mult, [sc_tk, stk], [sc_tk])
                    e_tt(B, sc[:, 3:4], sc[:, 2:3], col(EG), ALU.mult, [sc_tk, stk], [sc_tk])
                    yield
                    e_tt(B, sc[:, 4:5], sc[:, 0:1], col(EGL), ALU.mult, [sc_tk, stk], [sc_tk])
                    e_ts(B, sc[:, 5:6], sc[:, 1:2], 128.0 ** -0.5, None, ALU.mult, None, [sc_tk], [sc_tk])
                    e_tt(B, sc[:, 6:7], sc[:, 5:6], col(EG), ALU.mult, [sc_tk, stk], [sc_tk])
                    yield
                    for j, (srcj, scj) in enumerate(((0, 0), (0, 2), (0, 3), (0, 4), (1, 5), (1, 6))):
                        e_act(B, tkv[:, j, :], tok3[:, srcj, :], AF.Copy, [tok3_tk, sc_tk], [tkv_tk], scale=sc[:, scj:scj + 1])
                        if j % 2 == 1:
                            yield
                    e_act(B, vb_[:, 0, :], tok3[:, 2, :], AF.Copy, [tok3_tk, stk], [vb_tk], scale=col(BETA))
                    for j, srcj in enumerate((0, 1, 4, 5)):
                        e_tr(B, ptr[:, j * 128:(j + 1) * 128], tkv[:, srcj, :], B.ident_b, [tkv_tk, B.c_tk], [B.ps_tk[TRk]])
                    yield
                    e_copy(B, fm4.rearrange("p a b -> p (a b)"), ptr[:, 0:512], [B.ps_tk[TRk]], [fm4_tk])
                    knT, kbT, qnT = fm4[:, 0, :], fm4[:, 1, :], fm4[:, 2, :]
                    gcs = col(GC)
                    e_mm(B, B.ps[:, RBk, 0:128], gcs.to_broadcast([128, 128]), B.ident_f, True, True, [stk, B.c_tk], [B.ps_tk[RBk]])
                    yield
                    e_ts(B, Dm, B.ps[:, RBk, 0:128], gcs, 0.0, ALU.subtract, ALU.min, [B.ps_tk[RBk], stk], [Dm_tk])
                    yield
                    e_act(B, EE, Dm, AF.Exp, [Dm_tk], [EE_tk])
                    e_mm(B, B.ps[:, Ak, 0:128], knT, kbT, True, True, [fm4_tk], [B.ps_tk[Ak]])
                    e_mm(B, B.ps[:, Bk, 0:128], knT, qnT, True, True, [fm4_tk], [B.ps_tk[Bk]])
                    yield
                    e_tt(B, Es, EE, B.striu_f, ALU.mult, [EE_tk, B.c_tk], [Es_tk])
                    e_tt(B, Et, EE, B.triu_f, ALU.mult, [EE_tk, B.c_tk], [Et_tk], eng='gpsimd')
                    yield
                    e_tt(B, Nf[:, 0, :], B.ps[:, Ak, 0:128], Es, ALU.mult, [B.ps_tk[Ak], Es_tk], [Nf_tk])
                    e_tt(B, qkT[:, 0, :], B.ps[:, Bk, 0:128], Et, ALU.mult, [B.ps_tk[Bk], Et_tk], [qkT_tk])
                    yield
                    e_tr(B, B.ps[:, RBk, 0:128], Nf[:, 0, :], B.ident_f, [Nf_tk, B.c_tk], [B.ps_tk[RBk]])
                    e_tt(B, Rr, B.ident_f, Nf[:, 0, :], ALU.subtract, [Nf_tk, B.c_tk], [Rr_tk])
                    yield
                    e_copy(B, Nf[:, 1, :], B.ps[:, RBk, 0:128], [B.ps_tk[RBk]], [Nf_tk])
                    yield
                    yield from neumann_levels(Nf, Nf_tk, Rr, Rr_tk, Ak, Bk, RBk, 3)

                def stage2(i, sl_):
                    ti = b * 4 + i
                    col = lambda A_: A_[:, ti, r:r + 1]
                    tkv, tkv_tk = tkvS[sl_]
                    vb_, vb_tk = vbS[sl_]
                    fm4, fm4_tk = fm4S[sl_]
                    Nf, Nf_tk = PSs[sl_]
                    Rr, Rr_tk = RSs[sl_]
                    qkT, qkT_tk = qkS[sl_]
                    qdT = fm4[:, 3, :]
                    yield from neumann_levels(Nf, Nf_tk, Rr, Rr_tk, A2, B2, C2, 3)
                    e_copy(B, Rb[:, 0, :], Rr, [Rr_tk], [Rb_tk], eng='gpsimd')
                    X = Rb[:, 0, :]
                    yield
                    e_mm(B, B.ps[:, A2, 0:128], tkv[:, 2, :], X, True, True, [tkv_tk, Rb_tk], [B.ps_tk[A2]])
                    yield
                    e_act(B, wT[:, 0, :], B.ps[:, A2, 0:128], AF.Copy, [B.ps_tk[A2]], [wT_tk], scale=-1.0)
                    yield
                    e_mm(B, B.ps[:, B2, 0:128], X, vb_[:, 0, :], True, False, [Rb_tk, vb_tk], [B.ps_tk[B2]])
                    e_mm(B, B.ps[:, B2, 0:128], wT[:, 0, :], Sb[:, 0, :], False, True, [wT_tk, Sb_tk], [B.ps_tk[B2]])
                    yield
                    e_copy(B, vn[:, 0, :], B.ps[:, B2, 0:128], [B.ps_tk[B2]], [vn_tk])
                    yield
                    e_mm(B, B.ps[:, D2, 0:128], Sb[:, 0, :], qdT, True, False, [Sb_tk, fm4_tk], [B.ps_tk[D2]])
                    e_mm(B, B.ps[:, D2, 0:128], vn[:, 0, :], qkT[:, 0, :], False, True, [vn_tk, qkT_tk], [B.ps_tk[D2]])
                    e_mm(B, B.ps[:, C2, 0:128], tkv[:, 3, :], vn[:, 0, :], True, True, [tkv_tk, vn_tk], [B.ps_tk[C2]])
                    yield
                    e_stt(B, S, S, col(GLt), B.ps[:, C2, 0:128], ALU.mult, ALU.add, [S_tk, stk, B.ps_tk[C2]], [S_tk])
                    t0_ = ti * 128
                    if d == 0:
                        e_act(B, y[:, t0_:t0_ + 128], B.ps[:, D2, 0:128], AF.Copy, [B.ps_tk[D2]], [y_tk])
                    else:
                        yv = y[:, T - t0_ - 128:T - t0_][:, ::-1]
                        e_tt(B, yv, yv, B.ps[:, D2, 0:128], ALU.add, [B.ps_tk[D2], y_tk], [y_tk])
                    yield
                    e_act(B, Sb[:, 0, :], S, AF.Copy, [S_tk], [Sb_tk])

                lockstep([stage1(0, 0)])
                for i in range(4):
                    gens = [stage2(i, i % 2)]
                    if i + 1 < 4:
                        gens.append(stage1(i + 1, (i + 1) % 2))
                    lockstep(gens)
        if d == 0:
            yf_store(B, g * 4 + h, y, y_tk)
            continue
        for tb in range(NB):
            sl = slice(tb * 512, (tb + 1) * 512)
            bank = 4 + tb % 2
            e_act(B, sq, y[:, sl], AF.Square, [y_tk], [sq_tk])
            e_mm(B, B.ps[:, bank, :], B.ones_b, sq, True, True, [sq_tk, B.c_tk], [B.ps_tk[bank]])
            e_act(B, rstd, B.ps[:, bank, :], AF.Sqrt, [B.ps_tk[bank]], [rstd_tk], scale=1.0 / 128, bias=EPS)
            B.P.op('vector', lambda e: e.reciprocal(out=rstd, in_=rstd), reads=[rstd_tk], writes=[rstd_tk])
            e_stt(B, y[:, sl], y[:, sl], B.pc(pre + 'gdn_norm', 0), rstd, ALU.mult, ALU.mult, [y_tk, rstd_tk, B.c_tk], [y_tk])
            e_tt(B, ybf[:, h, sl], y[:, sl], zs[:, sl], ALU.mult, [y_tk, zs_tk], [ybf_tk])
    if d == 1:
        group_out(B, l, g, ybf, ybf_tk)


def tok_scalars(B, l, pre, col0, nr, dtb_name, alog_name, pfx):
    P, T, NT = B.P, B.T, B.NT
    nh = nr // 2
    W = NT * nr
    def buf():
        return B.rx([NT, nr])
    RAW, DTO, DA, ACS, ACSL = buf(), buf(), buf(), buf(), buf()
    abc = B.rx([nr])[:, 0, :]
    tk = Tk()
    wb, wb_tk = B.w_get()
    bank = 4
    for i in range(NT):
        for kc in range(KC):
            e_mm(B, B.ps[:, bank, i * nr:(i + 1) * nr], B.hT[:, kc, i * 128:(i + 1) * 128], wb[:, kc, 0:nr], kc == 0, kc == KC - 1,
                 [B.hT_tk[i // 4], wb_tk], [B.ps_tk[bank]])
    B.w_prefetch()
    flat = lambda a: a.rearrange("p a b -> p (a b)")
    e_tt(B, RAW, B.ps[:, bank, 0:W].rearrange("p (a b) -> p a b", b=nr), B.pc(pre + dtb_name, 0, nr).rearrange("p (a b) -> p a b", a=1).to_broadcast([128, NT, nr]),
         ALU.add, [B.ps_tk[bank], B.c_tk], [tk])
    e_act(B, flat(RAW), flat(RAW), AF.Exp, [tk], [tk])
    e_act(B, flat(RAW), flat(RAW), AF.Ln, [tk], [tk], bias=1.0)
    e_mm(B, B.ps[:, bank + 1, 0:W], B.J_f, flat(RAW), True, True, [tk, B.c_tk], [B.ps_tk[bank + 1]])
    e_copy(B, DTO[:, :, 0:nh], RAW[:, :, 0:nh], [tk], [tk])
    e_copy(B, DTO[:, :, nh:nr], B.ps[:, bank + 1, 0:W].rearrange("p (a b) -> p a b", b=nr)[:, ::-1, nh:nr], [B.ps_tk[bank + 1], tk], [tk])
    e_act(B, abc, B.pc(pre + alog_name, 0, nr), AF.Exp, [B.c_tk], [tk])
    e_ts(B, abc, abc, -1.0, None, ALU.mult, None, [tk], [tk])
    e_tt(B, DA, DTO, abc.rearrange("p (a b) -> p a b", a=1).to_broadcast([128, NT, nr]), ALU.mult, [tk], [tk])
    e_mm(B, B.ps[:, bank, 0:W], B.triu_f, flat(DA), True, True, [tk, B.c_tk], [B.ps_tk[bank]])
    e_copy(B, flat(ACS), B.ps[:, bank, 0:W], [B.ps_tk[bank]], [tk])
    e_mm(B, B.ps[:, bank + 1, 0:W], B.ones_f, flat(DA), True, True, [tk, B.c_tk], [B.ps_tk[bank + 1]])
    e_copy(B, flat(ACSL), B.ps[:, bank + 1, 0:W], [B.ps_tk[bank + 1]], [tk])
    return dict(DT=DTO, DA=DA, ACS=ACS, ACSL=ACSL, RAW=RAW, tk=tk)


def mixer_ssd(B, l, g, d):
    P, T, NB, NT = B.P, B.T, B.NB, B.NT
    pre = 'L%d_' % l
    Z0, X0, B0, C0, DT0 = 3088, 3600, 4112, 4368, 4624
    ybf = B.rx([4, T], BF16); ybf_tk = Tk()
    specs = [win_spec(B, l, DT0, 16)]
    for grp in range(2):
        specs += [win_spec(B, l, X0 + (2 * grp) * 128, 128), win_spec(B, l, X0 + (2 * grp + 1) * 128, 128),
                  win_spec(B, l, B0 + grp * 128, 128), win_spec(B, l, C0 + grp * 128, 128)]
        if d == 1:
            specs += [win_spec(B, l, Z0 + (2 * grp) * 128, 128), win_spec(B, l, Z0 + (2 * grp + 1) * 128, 128)]
    B.w_begin(specs)
    ts_ = tok_scalars(B, l, pre, DT0, 16, 'ssd_dtb_bc', 'ssd_alog_bc', 'ssd')
    DT, ACS, ACSL, stk = ts_['DT'], ts_['ACS'], ts_['ACSL'], ts_['tk']
    GLt = ts_['DA']
    Wt = ts_['RAW']
    flat = lambda a: a.rearrange("p a b -> p (a b)")
    e_tt(B, Wt, ACSL, ACS, ALU.subtract, [stk], [stk])
    e_act(B, flat(Wt), flat(Wt), AF.Exp, [stk], [stk])
    e_tt(B, Wt, Wt, DT, ALU.mult, [stk], [stk])
    e_act(B, flat(GLt), flat(ACSL), AF.Exp, [stk], [stk])
    y = B.rx([2, T]); y_tk = Tk()
    xsT = B.rx([2, T], BF16); xs_tk = Tk()
    BT = B.rx([T], BF16)[:, 0, :]; BT_tk = Tk()
    CT = B.rx([T], BF16)[:, 0, :]; CT_tk = Tk()
    u, u_tk = alloc_u(B)
    cc = B.rx([T])[:, 0, :]; cc_tk = Tk()
    OB = B.rx([4, 512], BF16); OB_tk = Tk()
    xtok = B.rx([4, 256], BF16); xtok_tk = Tk()
    btok = B.rx([4, 128], BF16); btok_tk = Tk()
    GmT = B.rx([128])[:, 0, :]; GmT_tk = Tk()
    Dm = B.rx([128])[:, 0, :]; Dm_tk = Tk()
    EE = B.rx([128])[:, 0, :]; EE_tk = Tk()
    MT = B.rx([128], BF16)[:, 0, :]; MT_tk = Tk()
    E2 = B.rx([128])[:, 0, :]; E2_tk = Tk()
    Cd = B.rx([128], BF16)[:, 0, :]; Cd_tk = Tk()
    xdt = B.rx([64], BF16)[:, 0, :]; xdt_tk = Tk()
    xw = B.rx([64], BF16)[:, 0, :]; xw_tk = Tk()
    STb = B.rx([4, 64], BF16); STb_tk = [Tk() for _ in range(4)]
    zs = B.rx([T], BF16)[:, 0, :]; zs_tk = Tk()
    sq = B.rx([512], BF16)[:, 0, :]; sq_tk = Tk()
    rstd = B.rx([512])[:, 0, :]; rstd_tk = Tk()
    RBk, GBk, YBk, SUk, TXk, TBk = 0, 1, 2, 3, 6, 7
    for grp in range(2):
        for cl in range(2):
            ch = 2 * grp + cl
            inproj_raw(B, 128, u, u_tk)
            if d == 0:
                conv4(B, u, u_tk, cc, cc_tk, pre + 'ssd_conv', 8, ch, B.pc(pre + 'ssd_convb', ch), y[:, cl, :], y_tk, AF.Silu)
                e_copy(B, xsT[:, cl, :], y[:, cl, :], [y_tk], [xs_tk], eng='gpsimd')
                e_ts(B, y[:, cl, :], y[:, cl, :], B.pc(pre + 'ssd_d', ch), None, ALU.mult, None, [y_tk, xs_tk, B.c_tk], [y_tk])
            else:
                conv4(B, u, u_tk, cc, cc_tk, pre + 'ssd_conv', 8, ch, B.pc(pre + 'ssd_convb', ch), xsT[:, cl, :], xs_tk, AF.Silu)
                yf_load(B, g * 4 + ch, y[:, cl, :], y_tk)
        inproj_raw(B, 128, u, u_tk)
        conv4(B, u, u_tk, cc, cc_tk, pre + 'ssd_conv', 8, 4 + grp, B.pc(pre + 'ssd_convb', 4 + grp), BT, BT_tk, AF.Silu)
        inproj_raw(B, 128, u, u_tk)
        conv4(B, u, u_tk, cc, cc_tk, pre + 'ssd_conv', 8, 6 + grp, B.pc(pre + 'ssd_convb', 6 + grp), CT, CT_tk, AF.Silu)
        ST = B.st_ssd[:, grp * 4:(grp + 1) * 4, :]
        ST_tk = B.st_ssd_tk[grp * 4:(grp + 1) * 4]
        for d in (d,):
            for hh in range(4):
                state_init(B, d, ST[:, hh, :], ST_tk[hh])
                e_act(B, STb[:, hh, :], ST[:, hh, :], AF.Copy, [ST_tk[hh]], [STb_tk[hh]])
            for b in range(NB):
                if d == 0:
                    sl = slice(b * 512, (b + 1) * 512)
                    srcs = [xsT[:, 0, sl], xsT[:, 1, sl], BT[:, sl], CT[:, sl]]
                else:
                    sl = slice(T - (b + 1) * 512, T - b * 512)
                    srcs = [xsT[:, 0, sl][:, ::-1], xsT[:, 1, sl][:, ::-1], BT[:, sl][:, ::-1], CT[:, sl][:, ::-1]]
                for k, (src, stk_) in enumerate(zip(srcs, (xs_tk, xs_tk, BT_tk, CT_tk))):
                    e_copy(B, OB[:, k, :], src, [stk_], [OB_tk], eng=('vector' if k % 2 == 0 else 'gpsimd'))
                px, pb = psbf(B, TXk), psbf(B, TBk)
                for i in range(4):
                    for cl in range(2):
                        e_tr(B, px[:, (i * 2 + cl) * 128:(i * 2 + cl + 1) * 128], OB[:, cl, i * 128:(i + 1) * 128], B.ident_b, [OB_tk, B.c_tk], [B.ps_tk[TXk]])
                    e_tr(B, pb[:, i * 128:(i + 1) * 128], OB[:, 2, i * 128:(i + 1) * 128], B.ident_b, [OB_tk, B.c_tk], [B.ps_tk[TBk]])
                e_copy(B, xtok.rearrange("p a b -> p (a b)"), px[:, 0:1024], [B.ps_tk[TXk]], [xtok_tk])
                e_act(B, btok.rearrange("p a b -> p (a b)"), pb[:, 0:512], AF.Copy, [B.ps_tk[TBk]], [btok_tk])
                for i in range(4):
                    ti = b * 4 + i
                    tsl = slice(i * 128, (i + 1) * 128)
                    e_mm(B, B.ps[:, GBk, 0:128], OB[:, 2, tsl], OB[:, 3, tsl], True, True, [OB_tk], [B.ps_tk[GBk]])
                    e_tt(B, GmT, B.ps[:, GBk, 0:128], B.triu_f, ALU.mult, [B.ps_tk[GBk], B.c_tk], [GmT_tk])
                    for hh in range(4):
                        r = d * 8 + grp * 4 + hh
                        cl = hh // 2
                        po = (hh % 2) * 64
                        acs = ACS[:, ti, r:r + 1]
                        e_mm(B, B.ps[:, RBk, 0:128], acs.to_broadcast([128, 128]), B.ident_f, True, True, [stk, B.c_tk], [B.ps_tk[RBk]])
                        e_ts(B, Dm, B.ps[:, RBk, 0:128], acs, 0.0, ALU.subtract, ALU.min, [B.ps_tk[RBk], stk], [Dm_tk])
                        e_act(B, EE, Dm, AF.Exp, [Dm_tk], [EE_tk])
                        e_tt(B, MT, EE, GmT, ALU.mult, [EE_tk, GmT_tk], [MT_tk])
                        e_act(B, E2, B.ps[:, RBk, 0:128], AF.Exp, [B.ps_tk[RBk]], [E2_tk])
                        e_tt(B, Cd, OB[:, 3, tsl], E2, ALU.mult, [OB_tk, E2_tk], [Cd_tk], eng='gpsimd')
                        xs_h = xtok[:, i, cl * 128 + po:cl * 128 + po + 64]
                        e_act(B, xdt, xs_h, AF.Copy, [xtok_tk, stk], [xdt_tk], scale=DT[:, ti, r:r + 1])
                        e_act(B, xw, xs_h, AF.Copy, [xtok_tk, stk], [xw_tk], scale=Wt[:, ti, r:r + 1])
                        yo = B.ps[po:po + 64, YBk, cl * 128:(cl + 1) * 128]
                        e_mm(B, yo, xdt, MT, True, False, [xdt_tk, MT_tk], [B.ps_tk[YBk]])
                        e_mm(B, yo, STb[:, hh, :], Cd, False, True, [STb_tk[hh], Cd_tk], [B.ps_tk[YBk]])
                        e_mm(B, B.ps[:, SUk, 0:64], btok[:, i, :], xw, True, True, [btok_tk, xw_tk], [B.ps_tk[SUk]])
                        e_stt(B, ST[:, hh, :], ST[:, hh, :], GLt[:, ti, r:r + 1], B.ps[:, SUk, 0:64], ALU.mult, ALU.add, [ST_tk[hh], stk, B.ps_tk[SUk]], [ST_tk[hh]])
                        e_act(B, STb[:, hh, :], ST[:, hh, :], AF.Copy, [ST_tk[hh]], [STb_tk[hh]])
                    for cl in range(2):
                        t0 = ti * 128
                        if d == 0:
                            yv = y[:, cl, t0:t0 + 128]
                        else:
                            yv = y[:, cl, T - t0 - 128:T - t0][:, ::-1]
                        e_tt(B, yv, yv, B.ps[:, YBk, cl * 128:(cl + 1) * 128], ALU.add, [B.ps_tk[YBk], y_tk], [y_tk])
        if d == 0:
            for cl in range(2):
                yf_store(B, g * 4 + 2 * grp + cl, y[:, cl, :], y_tk)
            continue
        for cl in range(2):
            inproj_act(B, 128, zs, zs_tk, AF.Silu)
            e_tt(B, y[:, cl, :], y[:, cl, :], zs, ALU.mult, [y_tk, zs_tk], [y_tk])
        for tb in range(NB):
            sl = slice(tb * 512, (tb + 1) * 512)
            bank = 4 + tb % 2
            for cl in range(2):
                e_act(B, sq, y[:, cl, sl], AF.Square, [y_tk], [sq_tk])
                e_mm(B, B.ps[:, bank, :], B.ones_b, sq, cl == 0, cl == 1, [sq_tk, B.c_tk], [B.ps_tk[bank]])
            e_act(B, rstd, B.ps[:, bank, :], AF.Sqrt, [B.ps_tk[bank]], [rstd_tk], scale=1.0 / 256, bias=EPS)
            B.P.op('vector', lambda e: e.reciprocal(out=rstd, in_=rstd), reads=[rstd_tk], writes=[rstd_tk])
            for cl in range(2):
                e_stt(B, ybf[:, 2 * grp + cl, sl], y[:, cl, sl], B.pc(pre + 'ssd_norm', 2 * grp + cl), rstd, ALU.mult, ALU.mult, [y_tk, rstd_tk, B.c_tk], [ybf_tk])
    if d == 1:
        group_out(B, l, g, ybf, ybf_tk)


def psbf(B, bank):
    return B.ps[:, bank, :].bitcast(BF16)


def mixer_hgrn(B, l, g, d):
    P, T, NB = B.P, B.T, B.NB
    pre = 'L%d_' % l
    Q0, F0c, I0, G0 = 4640, 5152, 6176, 6688
    C = 32
    NCB = 512 // C
    ybf = B.rx([4, T], BF16); ybf_tk = Tk()
    yh = B.rx([T])[:, 0, :]; yh_tk = Tk()
    qb = B.rx([T], BF16)[:, 0, :]; qb_tk = Tk()
    vb = B.rx([T], BF16)[:, 0, :]; vb_tk = Tk()
    qbr = B.rx([T], BF16)[:, 0, :]; qbr_tk = Tk()
    vbr = B.rx([T], BF16)[:, 0, :]; vbr_tk = Tk()
    sg = B.rx([T], BF16)[:, 0, :]; sg_tk = Tk()
    Fs = [B.rx([T])[:, 0, :] for _ in range(2)]; F_tk = [Tk() for _ in range(2)]
    LF = B.rx([512])[:, 0, :]; LF_tk = Tk()
    BC = B.rx([512])[:, 0, :]; BC_tk = Tk()
    KK = B.rx([512])[:, 0, :]; KK_tk = Tk()
    DF = B.rx([512])[:, 0, :]; DF_tk = Tk()
    EE = B.rx([512])[:, 0, :]; EE_tk = Tk()
    QT = B.rx([512], BF16)[:, 0, :]; QT_tk = Tk()
    KT = B.rx([512], BF16)[:, 0, :]; KT_tk = Tk()
    KD = B.rx([512], BF16)[:, 0, :]; KD_tk = Tk()
    QD = B.rx([512])[:, 0, :]; QD_tk = Tk()
    KDT = B.rx([8, 128], BF16); KDT_tk = Tk()
    VT = B.rx([8, 128], BF16); VT_tk = Tk()
    sTmS = [(B.rx([128], BF16)[:, 0, :], Tk()) for _ in range(2)]
    BS = B.rx([NCB])[:, 0, :]; BS_tk = Tk()
    GL = B.rx([NCB])[:, 0, :]; GL_tk = Tk()
    ones = B.rx([512])[:, 0, :]; ones_tk = Tk()
    lbv = B.rx([16])[:, 0, :]; lb_tk = Tk()
    zb = B.rx([128], BF16)[:, 0, :]; zb_tk = Tk()
    sq = B.rx([512], BF16)[:, 0, :]; sq_tk = Tk()
    rstd = B.rx([512])[:, 0, :]; rstd_tk = Tk()
    e_memset(B, ones, 1.0, [ones_tk])
    e_memset(B, zb, 0.0, [zb_tk])
    if l == 0:
        e_memset(B, lbv[:, 0:8], 0.0, [lb_tk])
    else:
        e_tt(B, lbv[:, 0:8], B.pc(pre + 'hg_lb1', 0, 8), B.pc(pre + 'hg_lb0', 0, 8), ALU.subtract, [B.c_tk], [lb_tk])
        e_act(B, lbv[:, 0:8], lbv[:, 0:8], AF.Sigmoid, [lb_tk], [lb_tk])
    e_ts(B, lbv[:, 8:16], lbv[:, 0:8], -1.0, 1.0, ALU.mult, ALU.add, [lb_tk], [lb_tk])
    SB_, YB_ = 6, 7
    e_mm(B, B.ps[:, SB_, 0:128], zb, zb, True, True, [zb_tk], [B.ps_tk[SB_]])
    specs = []
    for hd in range(4):
        for c0 in ((Q0, I0, F0c) if d == 0 else (Q0, I0, G0, F0c + 512)):
            specs.append(win_spec(B, l, c0 + hd * 128, 128))
    B.w_begin(specs)

    def rev_sl(sl):
        return slice(T - sl.stop, T - sl.start)

    for hd in range(4):
        inproj_act(B, 128, qb, qb_tk, AF.Silu)
        inproj_act(B, 128, vb, vb_tk, AF.Copy)
        if d == 0:
            inproj_act(B, 128, Fs[0], F_tk[0], AF.Sigmoid)
        else:
            inproj_act(B, 128, sg, sg_tk, AF.Silu)
            def evac_r(tb, ps_ap, ps_tk):
                sl = rev_sl(slice(tb * 512, (tb + 1) * 512))
                e_act(B, Fs[1][:, sl][:, ::-1], ps_ap, AF.Sigmoid, [ps_tk], [F_tk[1]])
            inproj(B, 128, evac_r)
            e_copy(B, qbr, qb[:, ::-1], [qb_tk], [qbr_tk])
            e_copy(B, vbr, vb[:, ::-1], [vb_tk], [vbr_tk])
            yf_load(B, g * 4 + hd, yh, yh_tk)
        S = B.st_hg[:, hd, :]
        S_tk = B.st_hg_tk[hd]
        for d in (d,):
            k8 = d * 4 + hd
            F, Ftk = Fs[d], F_tk[d]
            e_ts(B, F, F, lbv[:, 8 + k8:9 + k8], lbv[:, k8:k8 + 1], ALU.mult, ALU.add, [Ftk, lb_tk], [Ftk])
            Q, Qtk = (qb, qb_tk) if d == 0 else (qbr, qbr_tk)
            V, Vtk = (vb, vb_tk) if d == 0 else (vbr, vbr_tk)
            state_init(B, d, S, S_tk)
            for b in range(NB):
                sl = slice(b * 512, (b + 1) * 512)
                e_act(B, LF, F[:, sl], AF.Ln, [Ftk], [LF_tk])
                e_ts(B, KK, F[:, sl], -1.0, 1.0, ALU.mult, ALU.add, [Ftk], [KK_tk])
                e_scan(B, BC, ones, LF, [ones_tk, LF_tk], [BC_tk])
                BC3 = BC.rearrange("p (n c) -> p n c", c=C)
                DF3 = DF.rearrange("p (n c) -> p n c", c=C)
                sh = [128, NCB, C]
                e_memset(B, BS[:, 0:1], 0.0, [BS_tk])
                e_copy(B, BS[:, 1:NCB], BC3[:, 0:NCB - 1, C - 1], [BC_tk], [BS_tk])
                e_tt(B, GL, BC3[:, :, C - 1], BS, ALU.subtract, [BC_tk, BS_tk], [GL_tk])
                e_act(B, GL, GL, AF.Exp, [GL_tk], [GL_tk])
                e_tt(B, DF3, BC3, BC3[:, :, C // 2 - 1:C // 2].to_broadcast(sh), ALU.subtract, [BC_tk], [DF_tk])
                e_act(B, EE, DF, AF.Exp, [DF_tk], [EE_tk])
                e_tt(B, QT, Q[:, sl], EE, ALU.mult, [Qtk, EE_tk], [QT_tk])
                e_act(B, EE, DF, AF.Exp, [DF_tk], [EE_tk], scale=-1.0)
                e_tt(B, KT, KK, EE, ALU.mult, [KK_tk, EE_tk], [KT_tk])
                e_tt(B, DF3, BC3, BS.rearrange("p (n o) -> p n o", o=1).to_broadcast(sh), ALU.subtract, [BC_tk, BS_tk], [DF_tk])
                e_act(B, EE, DF, AF.Exp, [DF_tk], [EE_tk])
                e_tt(B, QD, Q[:, sl], EE, ALU.mult, [Qtk, EE_tk], [QD_tk])
                e_tt(B, DF3, BC3[:, :, C - 1:C].to_broadcast(sh), BC3, ALU.subtract, [BC_tk], [DF_tk])
                e_act(B, EE, DF, AF.Exp, [DF_tk], [EE_tk])
                e_tt(B, KD, KK, EE, ALU.mult, [KK_tk, EE_tk], [KD_tk])
                pk, pv = psbf(B, 2), psbf(B, 3)
                for i in range(8):
                    e_tr(B, pk[0:64, i * 128:(i + 1) * 128], KD[:, i * 64:(i + 1) * 64], B.ident_b, [KD_tk, B.c_tk], [B.ps_tk[2]])
                    e_tr(B, pv[0:64, i * 128:(i + 1) * 128], V[:, sl][:, i * 64:(i + 1) * 64], B.ident_b, [Vtk, B.c_tk], [B.ps_tk[3]])
                e_copy(B, KDT[0:64].rearrange("p a b -> p (a b)"), pk[0:64, 0:1024], [B.ps_tk[2]], [KDT_tk])
                e_act(B, VT[0:64].rearrange("p a b -> p (a b)"), pv[0:64, 0:1024], AF.Copy, [B.ps_tk[3]], [VT_tk])
                def emit_scores(i):
                    sT_, sT_tk = sTmS[i % 2]
                    for c in range(2):
                        cs = slice(i * 64 + c * C, i * 64 + (c + 1) * C)
                        e_mm(B, B.ps[c * C:(c + 1) * C, SB_, c * C:(c + 1) * C], KT[:, cs], QT[:, cs], True, True, [KT_tk, QT_tk], [B.ps_tk[SB_]])
                    e_tt(B, sT_[0:64, 0:64], B.ps[0:64, SB_, 0:64], B.blk32_f[0:64, 0:64], ALU.mult, [B.ps_tk[SB_], B.c_tk], [sT_tk])
                emit_scores(0)
                for i in range(8):
                    if i + 1 < 8:
                        emit_scores(i + 1)
                    sTm, sTm_tk = sTmS[i % 2]
                    e_mm(B, B.ps[:, YB_, 0:64], VT[0:64, i, :], sTm[0:64, 0:64], True, False, [VT_tk, sTm_tk], [B.ps_tk[YB_]])
                    for c in range(2):
                        n = i * 2 + c
                        cs = slice(i * 64 + c * C, i * 64 + (c + 1) * C)
                        sub = n % 2
                        e_mm(B, B.ps[:, sub, 0:128], KDT[c * C:(c + 1) * C, i, :], VT[c * C:(c + 1) * C, i, :], True, True, [KDT_tk, VT_tk], [B.ps_tk[sub]])
                        e_mm(B, B.ps[:, YB_, c * C:(c + 1) * C], S, QD[:, cs], False, (c == 1), [S_tk, QD_tk], [B.ps_tk[YB_]])
                        e_stt(B, S, S, GL[:, n:n + 1], B.ps[:, sub, 0:128], ALU.mult, ALU.add, [S_tk, GL_tk, B.ps_tk[sub]], [S_tk])
                    t0 = b * 512 + i * 64
                    if d == 0:
                        e_act(B, yh[:, t0:t0 + 64], B.ps[:, YB_, 0:64], AF.Copy, [B.ps_tk[YB_]], [yh_tk])
                    else:
                        yv = yh[:, T - t0 - 64:T - t0][:, ::-1]
                        e_tt(B, yv, yv, B.ps[:, YB_, 0:64], ALU.add, [B.ps_tk[YB_], yh_tk], [yh_tk])
        if d == 0:
            yf_store(B, g * 4 + hd, yh, yh_tk)
            continue
        for tb in range(NB):
            sl = slice(tb * 512, (tb + 1) * 512)
            bank = 4 + tb % 2
            e_act(B, sq, yh[:, sl], AF.Square, [yh_tk], [sq_tk])
            e_mm(B, B.ps[:, bank, :], B.ones_b, sq, True, True, [sq_tk, B.c_tk], [B.ps_tk[bank]])
            e_act(B, rstd, B.ps[:, bank, :], AF.Sqrt, [B.ps_tk[bank]], [rstd_tk], scale=1.0 / 128, bias=EPS)
            B.P.op('vector', lambda e: e.reciprocal(out=rstd, in_=rstd), reads=[rstd_tk], writes=[rstd_tk])
            e_stt(B, yh[:, sl], yh[:, sl], B.pc(pre + 'hg_norm', 0), rstd, ALU.mult, ALU.mult, [yh_tk, rstd_tk, B.c_tk], [yh_tk])
            e_tt(B, ybf[:, hd, sl], yh[:, sl], sg[:, sl], ALU.mult, [yh_tk, sg_tk], [ybf_tk])
    if d == 1:
        group_out(B, l, g, ybf, ybf_tk)


def make_consts():
    c = np.zeros((128, 768), np.float32)
    c[:, 640:768] = np.eye(128)[::-1]
    bm = np.zeros((128, 128), np.float32)
    for a in range(4):
        bm[a * 32:(a + 1) * 32, a * 32:(a + 1) * 32] = np.triu(np.ones((32, 32)))
    c[:, 512:640] = bm
    c[:, 0:128] = np.eye(128)
    c[:, 128:256] = np.triu(np.ones((128, 128)))
    c[:, 256:384] = np.triu(np.ones((128, 128)), 1)
    c[:, 384:512] = 1.0
    return c


def host_inputs(inp):
    params, _ = pack_params(inp)
    lw = np.stack([np.asarray(inp['lru_wa']), np.asarray(inp['lru_wi'])], axis=1)
    L = lw.shape[0]
    lru_w = np.ascontiguousarray(lw.reshape(L, 16, 128, 128).transpose(0, 2, 1, 3).reshape(L, 128, 16 * 128))
    return {
        'w_in': np.ascontiguousarray(inp['w_in'], dtype=np.float32),
        'w_out': np.ascontiguousarray(inp['w_out'], dtype=np.float32),
        'w_up': np.ascontiguousarray(inp['w_up'], dtype=np.float32),
        'w_down': np.ascontiguousarray(inp['w_down'], dtype=np.float32),
        'lru_w': lru_w,
        'params': params,
        'consts': make_consts(),
    }


_CACHE = {}


def get_nc(T, **kw):
    key = (T, tuple(sorted(kw.items())))
    if key not in _CACHE:
        _CACHE[key] = Builder(T, **kw).build()
    return _CACHE[key]


def make_flags(carry):
    f = np.zeros((128, 4), np.float32)
    f[:, 0] = 1.0 if carry else 0.0
    f[:, 1] = 0.0 if carry else 1.0
    return f


def kernel(**inputs):
    inp = {k: np.asarray(v) for k, v in inputs.items()}
    xp = inp['x_prompt']
    xsm = inp['x_sample']
    T, NSEG = 2048, 4
    shared = host_inputs(inp)
    nc = get_nc(T, nseg=NSEG)
    x_prompt_T = np.ascontiguousarray(xp[0].T)
    x_sample_T = np.ascontiguousarray(xsm.reshape(NSEG * T, D).T)
    in_maps = []
    for c in range(8):
        m = dict(shared)
        if c == 0:
            m['xT'] = x_prompt_T
            m['flags'] = make_flags(True)
        else:
            m['xT'] = x_sample_T
            m['flags'] = make_flags(False)
        in_maps.append(m)
    res = run_bass_kernel_spmd(nc, in_maps, core_ids=list(range(8)))
    y_prompt = np.asarray(res.results[0]['yT'], dtype=np.float32).T[None]
    y_sample = np.asarray(res.results[1]['yT'], dtype=np.float32).T.reshape(NSEG, T, D)
    return (np.ascontiguousarray(y_prompt), np.ascontiguousarray(y_sample))
```
